# Optimizing a Trainium2 kernel written in Bass

```python
import jax
import jax.numpy as jnp
from jax import lax
import numpy as np

D_MODEL = 2048
BATCH = 8
SEQ = 4096
DEPTH = 1
DEC_BATCH = 16
DEC_SEQ = 16
PAST_LEN = 1024

CHUNK = 64
D_CONV = 1024
CONV_WIDTH = 31
N_HEADS = 16
N_KV_HEADS = 2
HEAD_DIM = 64
GROUP = N_HEADS // N_KV_HEADS
ATTN_W = N_HEADS * HEAD_DIM
KV_W = N_KV_HEADS * HEAD_DIM
WINDOW = 128
WINDOW_CHUNKS = WINDOW // CHUNK
N_BRANCH = 2
EPS = 1e-6
NEG = -1e30
SCALE = HEAD_DIM ** -0.5
IN_COLS = 3 * D_CONV + 2 * ATTN_W + 2 * KV_W + N_BRANCH * D_MODEL
SPLIT_POINTS = (D_CONV, 2 * D_CONV, 3 * D_CONV, 3 * D_CONV + ATTN_W, 3 * D_CONV + ATTN_W + KV_W, 3 * D_CONV + ATTN_W + 2 * KV_W, 3 * D_CONV + 2 * ATTN_W + 2 * KV_W, 3 * D_CONV + 2 * ATTN_W + 2 * KV_W + D_MODEL)

kernel_name = 'hybrid_stream_conv_swa_step'


def _rmsnorm(x, g):
    xf = x.astype(jnp.float32)
    y = xf * lax.rsqrt(jnp.mean(xf * xf, axis=-1, keepdims=True) + EPS) * g.astype(jnp.float32)
    return y.astype(x.dtype)


def _layernorm(x, g, b):
    xf = x.astype(jnp.float32)
    mu = jnp.mean(xf, axis=-1, keepdims=True)
    xc = xf - mu
    y = xc * lax.rsqrt(jnp.mean(xc * xc, axis=-1, keepdims=True) + EPS) * g.astype(jnp.float32) + b.astype(jnp.float32)
    return y.astype(x.dtype)


def _alibi_slopes():
    h = jnp.arange(1, N_HEADS + 1, dtype=jnp.float32)
    return jnp.exp2(-8.0 * h / N_HEADS).reshape(N_KV_HEADS, GROUP)


def _sink_softmax(scores, sink):
    s = jnp.broadcast_to(sink.astype(jnp.float32).reshape(N_KV_HEADS, GROUP, 1, 1), scores.shape[:-1] + (1,))
    p = jax.nn.softmax(jnp.concatenate([scores, s], axis=-1), axis=-1)
    return p[..., :-1]


def _in_project(x, norm_g, w_in):
    h = _rmsnorm(x, norm_g)
    return jnp.split(h @ w_in, SPLIT_POINTS, axis=-1)


def _conv_branch(a, b, gate, left, conv_w, conv_b, ln_g, ln_b, w_pw):
    glu = a * jax.nn.sigmoid(b)
    full = jnp.concatenate([left.astype(glu.dtype), glu], axis=1)
    y = lax.conv_general_dilated(full, conv_w[:, None, :].astype(full.dtype), window_strides=(1,), padding='VALID', dimension_numbers=('NWC', 'WIO', 'NWC'), feature_group_count=D_CONV) + conv_b
    z = jax.nn.silu(_layernorm(y, ln_g, ln_b)) * jax.nn.silu(gate)
    return z @ w_pw, full[:, -(CONV_WIDTH - 1):]


def _attn_prompt(q, k, v, sink):
    B, S, _ = q.shape
    NC = S // CHUNK
    NK = (WINDOW_CHUNKS + 1) * CHUNK
    qb = q.reshape(B, NC, CHUNK, N_KV_HEADS, GROUP, HEAD_DIM)
    kc = k.reshape(B, NC, CHUNK, N_KV_HEADS, HEAD_DIM)
    vc = v.reshape(B, NC, CHUNK, N_KV_HEADS, HEAD_DIM)
    pad = ((0, 0), (WINDOW_CHUNKS, 0), (0, 0), (0, 0), (0, 0))
    kp = jnp.pad(kc, pad)
    vp = jnp.pad(vc, pad)
    kb = jnp.concatenate([kp[:, j:j + NC] for j in range(WINDOW_CHUNKS + 1)], axis=2)
    vb = jnp.concatenate([vp[:, j:j + NC] for j in range(WINDOW_CHUNKS + 1)], axis=2)
    scores = jnp.einsum('bcqkgd,bcskd->bckgqs', qb, kb, preferred_element_type=jnp.float32) * SCALE
    qi = jnp.arange(CHUNK)
    sj = jnp.arange(NK)
    dist = jnp.abs(qi[:, None] + WINDOW - sj[None, :]).astype(jnp.float32)
    bias = -_alibi_slopes()[:, :, None, None] * dist
    valid = (jnp.arange(NC)[:, None] * CHUNK - WINDOW + sj[None, :]) >= 0
    scores = jnp.where(valid[None, :, None, None, None, :], scores + bias, NEG)
    probs = _sink_softmax(scores, sink)
    out = jnp.einsum('bckgqs,bcskd->bcqkgd', probs.astype(vb.dtype), vb)
    return out.reshape(B, S, ATTN_W)


def _attn_sample(q, k, v, k_cache, v_cache, sink):
    N, T, _ = q.shape
    L = k_cache.shape[1]
    k_all = jnp.concatenate([k_cache.astype(k.dtype), k.reshape(N, T, N_KV_HEADS, HEAD_DIM)], axis=1)
    v_all = jnp.concatenate([v_cache.astype(v.dtype), v.reshape(N, T, N_KV_HEADS, HEAD_DIM)], axis=1)
    qh = q.reshape(N, T, N_KV_HEADS, GROUP, HEAD_DIM)
    scores = jnp.einsum('btkgd,bskd->bkgts', qh, k_all, preferred_element_type=jnp.float32) * SCALE
    tpos = PAST_LEN + jnp.arange(T)
    spos = PAST_LEN - L + jnp.arange(L + T)
    tc = tpos[:, None] // CHUNK
    sc = spos[None, :] // CHUNK
    valid = (sc >= tc - WINDOW_CHUNKS) & (sc <= tc) & (spos[None, :] >= 0)
    dist = jnp.abs(tpos[:, None] - spos[None, :]).astype(jnp.float32)
    bias = -_alibi_slopes()[:, :, None, None] * dist
    scores = jnp.where(valid[None, None, None], scores + bias, NEG)
    probs = _sink_softmax(scores, sink)
    out = jnp.einsum('bkgts,bskd->btkgd', probs.astype(v_all.dtype), v_all)
    return out.reshape(N, T, ATTN_W), k_all[:, -L:], v_all[:, -L:]


def _merge(conv_o, attn_o, m_conv, m_attn, w_out):
    return (jax.nn.sigmoid(m_conv) * conv_o + jax.nn.sigmoid(m_attn) * attn_o) @ w_out


def setup_inputs(seed: int = 0) -> dict:
    key = jax.random.key(seed)
    ks = jax.random.split(key, 17)
    f32 = jnp.float32
    nrm = lambda k, shape: jax.random.normal(k, shape, dtype=f32)
    return {
        'x_prompt': nrm(ks[0], (BATCH, SEQ, D_MODEL)),
        'x_sample': nrm(ks[1], (DEC_BATCH, DEC_SEQ, D_MODEL)),
        'cache_k': nrm(ks[2], (DEPTH, DEC_BATCH, WINDOW, N_KV_HEADS, HEAD_DIM)),
        'cache_v': nrm(ks[3], (DEPTH, DEC_BATCH, WINDOW, N_KV_HEADS, HEAD_DIM)),
        'state_conv': 0.5 * nrm(ks[4], (DEPTH, DEC_BATCH, CONV_WIDTH - 1, D_CONV)),
        'norm_g': 1.0 + 0.02 * nrm(ks[5], (DEPTH, D_MODEL)),
        'w_in': nrm(ks[6], (DEPTH, D_MODEL, IN_COLS)) * D_MODEL ** -0.5,
        'conv_w': nrm(ks[7], (DEPTH, CONV_WIDTH, D_CONV)) * CONV_WIDTH ** -0.5,
        'conv_b': 0.02 * nrm(ks[8], (DEPTH, D_CONV)),
        'ln_g': 1.0 + 0.02 * nrm(ks[9], (DEPTH, D_CONV)),
        'ln_b': 0.02 * nrm(ks[10], (DEPTH, D_CONV)),
        'w_conv_pw': nrm(ks[11], (DEPTH, D_CONV, D_MODEL)) * D_CONV ** -0.5,
        'attn_sink': 0.5 * nrm(ks[12], (DEPTH, N_HEADS)),
        'w_o_attn': nrm(ks[13], (DEPTH, ATTN_W, D_MODEL)) * ATTN_W ** -0.5,
        'w_out': nrm(ks[14], (DEPTH, D_MODEL, D_MODEL)) * D_MODEL ** -0.5,
        'final_g': 1.0 + 0.02 * nrm(ks[15], (D_MODEL,)),
    }


def reference(x_prompt, x_sample, cache_k, cache_v, state_conv, norm_g, w_in, conv_w, conv_b, ln_g, ln_b, w_conv_pw, attn_sink, w_o_attn, w_out, final_g):
    xp = x_prompt
    xs = x_sample
    Bp, S, _ = xp.shape
    kp_l, vp_l, cp_l, ks_l, vs_l, cs_l = [], [], [], [], [], []
    for l in range(DEPTH):
        a, b, gc, q, k, v, ga, mc, ma = _in_project(xp, norm_g[l], w_in[l])
        left = jnp.zeros((Bp, CONV_WIDTH - 1, D_CONV), a.dtype)
        conv_o, c_new = _conv_branch(a, b, gc, left, conv_w[l], conv_b[l], ln_g[l], ln_b[l], w_conv_pw[l])
        attn = _attn_prompt(q, k, v, attn_sink[l])
        attn_o = (attn * jax.nn.silu(ga)) @ w_o_attn[l]
        xp = xp + _merge(conv_o, attn_o, mc, ma, w_out[l])
        kp_l.append(k.reshape(Bp, S, N_KV_HEADS, HEAD_DIM)[:, -WINDOW:])
        vp_l.append(v.reshape(Bp, S, N_KV_HEADS, HEAD_DIM)[:, -WINDOW:])
        cp_l.append(c_new)
        a, b, gc, q, k, v, ga, mc, ma = _in_project(xs, norm_g[l], w_in[l])
        conv_o, c_new = _conv_branch(a, b, gc, state_conv[l], conv_w[l], conv_b[l], ln_g[l], ln_b[l], w_conv_pw[l])
        attn, k_new, v_new = _attn_sample(q, k, v, cache_k[l], cache_v[l], attn_sink[l])
        attn_o = (attn * jax.nn.silu(ga)) @ w_o_attn[l]
        xs = xs + _merge(conv_o, attn_o, mc, ma, w_out[l])
        ks_l.append(k_new)
        vs_l.append(v_new)
        cs_l.append(c_new)
    y_prompt = _rmsnorm(xp, final_g)
    y_sample = _rmsnorm(xs, final_g)
    return (y_prompt, y_sample, jnp.stack(kp_l), jnp.stack(vp_l), jnp.stack(cp_l), jnp.stack(ks_l), jnp.stack(vs_l), jnp.stack(cs_l))
```

```python
from contextlib import ExitStack

import numpy as np
import concourse.bass as bass
import concourse.mybir as mybir
from concourse.bass_utils import run_bass_kernel_spmd

F32 = mybir.dt.float32
BF16 = mybir.dt.bfloat16
AF = mybir.ActivationFunctionType
ALU = mybir.AluOpType

D_MODEL = 2048
SEQ = 4096
TT = 512
EPS = 1e-6
SCALE = 0.125
O_A, O_B, O_GC, O_Q, O_K, O_V, O_GA, O_MC, O_MA = 0, 1024, 2048, 3072, 4096, 4224, 4352, 5376, 7424
IN_COLS = 9472
NW = 4
CW = 544


class Sched:
    ENGS = ("pe", "act", "dve", "pool", "sp")

    def __init__(self):
        self.ops = []
        self.lastw = {}
        self.readers = {}
        self.dma_count = {}
        self.last_dma = {}

    def add(self, eng, fn, r=(), w=(), dma=None):
        i = len(self.ops)
        raw = set()
        deps = set()
        for k in r:
            j = self.lastw.get(k)
            if j is not None:
                raw.add(j)
                deps.add(j)
        for k in w:
            j = self.lastw.get(k)
            if j is not None:
                deps.add(j)
            rd = self.readers.get(k)
            if rd:
                deps.update(rd.values())
        for k in w:
            self.lastw[k] = i
            self.readers[k] = {}
        for k in r:
            rd = self.readers.setdefault(k, {})
            rd[(eng if dma is None else ("dma", i))] = i
        dmaval = None
        if dma is not None:
            j = self.last_dma.get(dma)
            if j is not None:
                deps.add(j)
            self.last_dma[dma] = i
            self.dma_count[dma] = self.dma_count.get(dma, 0) + 1
            dmaval = 16 * self.dma_count[dma]
        self.ops.append([eng, fn, deps, raw, dma, dmaval, False, None])
        return i

    def plan(self):
        ops = self.ops
        for i, op in enumerate(ops):
            eng, fn, deps, raw, dma = op[0], op[1], op[2], op[3], op[4]
            keep = []
            for d in deps:
                o = ops[d]
                if o[4] is None and dma is None and o[0] == eng:
                    if eng == "pe":
                        continue
                keep.append(d)
                o[6] = True
            op[2] = sorted(keep)
        cnt = {e: 0 for e in self.ENGS}
        for op in ops:
            if op[4] is not None:
                op[7] = ("dma_" + op[4], op[5])
            elif op[6]:
                cnt[op[0]] += 1
                op[7] = ("eng_" + op[0], cnt[op[0]])
        self.final = {}
        for op in ops:
            if op[7] is not None:
                s, v = op[7]
                self.final[s] = max(self.final.get(s, 0), v)

    def run_engine(self, eng, e, sems, final_wait=False):
        known = {}
        for op in self.ops:
            if op[0] != eng:
                continue
            for d in op[2]:
                s, v = self.ops[d][7]
                if known.get(s, 0) >= v:
                    continue
                known[s] = v
                e.wait_ge(sems[s], v)
            ins = op[1](e)
            if op[7] is not None:
                s, v = op[7]
                ins.then_inc(sems[s], 16 if op[4] is not None else 1)
        if final_wait:
            for s, v in self.final.items():
                if known.get(s, 0) < v:
                    e.wait_ge(sems[s], v)

    def sem_names(self):
        return ["eng_" + e for e in self.ENGS] + ["dma_" + k for k in self.dma_count]


def _slope(h):
    return float(2.0 ** (-(h + 1) / 2.0))


def make_consts():
    c = np.zeros((128, CW), np.float32)
    c[:, 0:128] = np.eye(128, dtype=np.float32)
    c[:, 128:256] = 1.0 / 1024.0
    c[:, 256:320] = 1.0
    c[:, 384 + 64:512] = 1.0
    s = np.arange(128)[:, None].astype(np.float32)
    q = np.arange(128)[None, :].astype(np.float32)
    c[:, 512:528] = -(128.0 + q[:, :16] - s)
    c[:16, 528:544] = -np.abs(q[:, :16] - s[:16])
    return c


def _nd_tables():
    s = np.arange(128)[:, None].astype(np.float32)
    q = np.arange(128)[None, :].astype(np.float32)
    ndA = -(128.0 + q - s)
    ndA[:64, 64:] = -1e5
    ndB = -np.abs(q - s)
    ndB[64:, :64] = -1e5
    return ndA, ndB


def make_cbf():
    import ml_dtypes
    ndA_, ndB_ = _nd_tables()
    out = np.zeros((128, 256 + 4096), np.float32)
    out[:, 0:128] = ndA_.T
    out[:, 128:256] = ndB_.T
    eye = np.eye(128, dtype=np.float32)
    for g in range(2):
        for par in range(2):
            for j in range(4):
                v = np.float32(_slope(8 * g + 2 * j + par) / SCALE)
                hi = np.float32(v.astype(ml_dtypes.bfloat16))
                lo = np.float32(np.float32(v - hi).astype(ml_dtypes.bfloat16))
                base = 256 + ((g * 2 + par) * 2) * 512 + j * 128
                out[:, base:base + 128] = hi * eye
                out[:, base + 512:base + 640] = lo * eye
    return out


def build_nc(NT=8, SAMPLE=True, STOP=None, DBG=False):
    nc = bass.Bass("TRN2", target_bir_lowering=False)
    din = lambda n, sh: nc.dram_tensor(n, sh, F32, kind="ExternalInput").ap()
    dout = lambda n, sh: nc.dram_tensor(n, sh, F32, kind="ExternalOutput").ap()
    x = din("x", [SEQ, D_MODEL])
    xs = din("xs", [32, D_MODEL])
    ck = din("ck", [2, 128, 128])
    cv = din("cv", [2, 128, 128])
    sc = din("sc", [2, 30, 1024])
    norm_g = din("norm_g", [D_MODEL])
    w_in = din("w_in", [D_MODEL, IN_COLS])
    conv_w = din("conv_w", [31, 1024])
    conv_b = din("conv_b", [1024])
    ln_g = din("ln_g", [1024])
    ln_b = din("ln_b", [1024])
    w_pw = din("w_pw", [1024, D_MODEL])
    sink = din("sink", [16])
    w_o = din("w_o", [1024, D_MODEL])
    w_out = din("w_out", [D_MODEL, D_MODEL])
    final_g = din("final_g", [D_MODEL])
    consts = din("consts", [128, CW])
    cbfd = din("cbf", [128, 4352])
    y = dout("y", [SEQ, D_MODEL])
    ys = dout("ys", [32, D_MODEL])
    kwin = dout("kwin", [128, 128])
    vwin = dout("vwin", [128, 128])
    cwin = dout("cwin", [30, 1024])
    ksw = dout("ksw", [2, 128, 128])
    vsw = dout("vsw", [2, 128, 128])
    csw = dout("csw", [2, 30, 1024])
    if DBG:
        dbg = {n: dout("dbg_" + n, [128, 4096]) for n in ("ypre", "stat", "yb", "z", "attn_n", "sga", "qT", "mg0", "mg1", "kT0", "vpad")}

    units = []
    U = {}
    def addu(name, spec):
        U[name] = len(units)
        units.append(spec)
    for c in range(8):
        addu(("a", c), ("in", O_A + c * 128))
        addu(("b", c), ("in", O_B + c * 128))
    for c in range(8):
        addu(("q", c), ("in", O_Q + c * 128))
    for g in range(2):
        addu(("k", g), ("kdup", O_K + g * 64))
    addu(("v", 0), ("in", O_V))
    for c in range(8):
        addu(("ga", c), ("in", O_GA + c * 128))
    for c in range(8):
        addu(("gc", c), ("in", O_GC + c * 128))
    for j in range(16):
        addu(("pw", j), ("pw", j))
        addu(("mc", j), ("in", O_MC + j * 128))
        addu(("wo", j), ("wo", j))
        addu(("ma", j), ("in", O_MA + j * 128))
    for cb in range(4):
        for kq in range(4):
            addu(("out", cb, kq), ("out", cb, kq))
    NU = len(units)
    wsc = nc.dram_tensor("wsc", [NU, 128, 2048], BF16, kind="Internal").ap()

    S = Sched()
    KM = {"ckl": [("xa", 1)], "cvl": [("xa", 1)], ("ctk", 0): [("xa", 1)], ("ctk", 1): [("xa", 1)],
          "v34a": [("xa", 0)], "v34b": [("xa", 0)], "v34c": [("xa", 0)], "v34d": [("xa", 0)],
          "ysq": [("T", 2)], "mean": [("T", 3)], "rstdl": [("T", 4)], "tmpl": [("T", 5)], "sg": [("T", 6)],
          "dn": [("T", 2)], "o1": [("T", 3)], "junk": [("attn_n", g_, s_) for g_ in range(2) for s_ in (0, 16, 128, 256, 384)], "scl": [("xa", 0)]}
    for i_ in range(2):
        KM[("sbt", i_)] = [("T", i_)]
        KM[("Sb", i_)] = [("T", i_)]
        KM[("smc", i_)] = [("T", 4 + i_)]
        KM[("t1", i_)] = [("T", 6)]
        KM[("t2", i_)] = [("T", 7)]
        KM[("cout", i_)] = [("z", c_) for c_ in range(8)]
    for i_ in range(8):
        KM[("sga", i_)] = [("glu", i_)]
        KM[("PT", i_)] = [("T", 4 + i_ // 2)]
        KM[("yb", i_)] = [("Y", 2 * i_), ("Y", 2 * i_ + 1)]
    for i_ in range(8):
        KM[("s_yb", i_)] = [("sY", 2 * i_), ("sY", 2 * i_ + 1)]
    for i_ in range(16):
        KM[("s_mg", i_)] = [("sY", i_)]
        KM[("mg", i_)] = [("Y", i_)]

    def km(keys):
        out = []
        for k in keys:
            for kk in KM.get(k, [k]):
                if kk not in out:
                    out.append(kk)
        return out

    def A(eng, fn, r=(), w=(), dma=None):
        return S.add(eng, fn, km(r), km(w), dma)
    es = ExitStack()
    with es:
        sb = lambda name, shape, dt: es.enter_context(nc.sbuf_tensor(name, shape, dt))
        cst = sb("cst", [128, CW], F32)
        ident = cst[:, 0:128]
        onesm = cst[:, 128:256]
        ndSc = cst[:, 512:528]
        ndSn = cst[:, 528:544]
        identbf = sb("identbf", [128, 128], BF16)
        cbf = sb("cbf_sb", [128, 4352], BF16)
        onespad = sb("onespad", [128, 2, 128], BF16)
        onesbf = sb("onesbf", [128, 128], BF16)
        ng16 = sb("ng16", [16, 128], F32)
        cw = sb("cw", [128, 8, 34], F32)
        normg = sb("normg", [128, 16], F32)
        gfin = sb("gfin", [128, D_MODEL], F32)
        sk = sb("sk", [128, 16], F32)
        esink = sb("esink", [128, 8], F32)
        Wt = [sb(f"W{i}", [128, 2048], BF16) for i in range(NW)]
        hTs = [sb(f"hT{i}", [128, 16, TT], BF16) for i in range(2)]
        cur = {"hs": 0}
        xa = [sb(f"xa{i}", [128, D_MODEL], F32) for i in range(2)]
        v34 = xa[0][0:34, 0:1024]
        xr = [sb(f"xr{i}", [128, D_MODEL], F32) for i in range(2)]
        st4 = [sb(f"st4_{i}", [128, 4], F32) for i in range(2)]
        st5 = [sb(f"st5_{i}", [128, 4], F32) for i in range(4)]
        glu = sb("glu", [128, 8, 30 + TT], BF16)
        gluS = sb("gluS", [128, 8, 2, 46], BF16)
        glu32 = sb("glu32", [128, 8, 32], F32)
        TP = [sb(f"tp{i}", [128, TT], F32) for i in range(8)]
        sbt = [TP[0], TP[1]]
        diag = [sb(f"diag{i}", [128, 31, 128], BF16) for i in range(2)]
        yraw = sb("yraw", [128, 8 * TT], F32)
        yb = yraw[:].rearrange("p (c n) -> p c n", c=8)
        merged = yraw[:].bitcast(BF16).rearrange("p (j n) -> p j n", j=16)
        ysq, mean, rstdl, tmpl, sg = TP[2], TP[3], TP[4], TP[5], TP[6]
        z = sb("z", [128, 8, TT], BF16)
        qT = sb("qT", [128, 8, TT], BF16)
        kT = [sb(f"kT{g}", [128, 128 + TT], BF16) for g in range(2)]
        kTs = [sb(f"kTs{g}", [128, 2, 144], BF16) for g in range(2)]
        vT = sb("vT", [128, TT], BF16)
        k32 = [sb(f"k32_{g}", [128, 128], F32) for g in range(2)]
        v32 = sb("v32", [128, 128], F32)
        vpad = sb("vpad", [128, 5, 2, 2, 128], BF16)
        vpadSc = sb("vpadSc", [128, 2, 2, 2, 128], BF16)
        vpadSn = sb("vpadSn", [16, 2, 2, 2, 128], BF16)
        sga = glu[:, :, 0:TT]
        ctx = sb("ctx", [128, 8, 30], BF16)
        Sb = [TP[0], TP[1]]
        PT = [TP[4 + i // 2][:].bitcast(BF16)[:, (i % 2) * TT:(i % 2 + 1) * TT] for i in range(8)]
        dn, o1 = TP[2], TP[3]
        attn_n = sb("attn_n", [128, 8, TT], BF16)
        junk = attn_n[:].rearrange("p a b -> p (a b)")[:, 0:D_MODEL]
        smc = [TP[4], TP[5]]
        T1P = [TP[6], TP[6]]
        T2P = [TP[7], TP[7]]
        kvo = sb("kvo", [128, 384], F32)
        cout = z[:].rearrange("p a b -> p (a b)").bitcast(F32)[0:32, 0:1024]
        ctk = xa[1][:, 0:256].rearrange("p (a b c) -> p a b c", a=2, b=2)
        ckl = xa[1][:, 256:384]
        cvl = xa[1][:, 384:512]
        scl = xa[0][0:30, 0:1024]
        hT_s = sb("hT_s", [128, 16, 32], BF16)
        t1s = sb("t1s", [128, 32], F32)
        lns = [sb(f"lns{i}", [128, 32], F32) for i in range(3)]
        t2s = sb("t2s", [128, 32], F32)
        yraw_s = sb("yraw_s", [128, 8 * 32], F32)
        SB = {"yb": yraw_s[:].rearrange("p (c n) -> p c n", c=8),
              "merged": yraw_s[:].bitcast(BF16).rearrange("p (j n) -> p j n", j=16),
              "z": sb("z_s", [128, 8, 32], BF16), "qT": sb("qT_s", [128, 8, 32], BF16), "sga": sb("sga_s", [128, 8, 32], BF16),
              "attn_n": sb("attn_n_s", [128, 8, 32], BF16), "glu32": sb("glu32_s", [128, 8, 32], F32),
              "k32": [sb(f"k32s_{g}", [128, 32], F32) for g in range(2)], "v32": sb("v32_s", [128, 32], F32)}
        PB = {"yb": yb, "merged": merged, "z": z, "qT": qT, "sga": sga, "attn_n": attn_n, "glu32": glu32, "k32": k32, "v32": v32}
        banks = [es.enter_context(nc.psum_tensor(f"pb{i}", [128, 512], F32)) for i in range(8)]

        st = {"nbank": 6, "diag": 0, "bank": 0, "w": 0, "xa": 0, "xr": 0, "sbt": 0, "smc": 0, "Sb": 0, "PT": 0}

        def nxt(name, n):
            v = st[name]
            st[name] = (v + 1) % n
            return v

        held = set()

        def bank():
            nb = st["nbank"]
            b = nxt("bank", nb)
            while b in held:
                b = nxt("bank", nb)
            return b, banks[b]

        A("sp", lambda e: e.dma_start(out=cst[:], in_=consts), w=["cst"], dma="c0")
        A("sp", lambda e: e.dma_start(out=v34[0:31, :], in_=conv_w), w=["v34a"], dma="c1")
        A("sp", lambda e: e.dma_start(out=v34[31:32, :], in_=conv_b.rearrange("(o n) -> o n", o=1)), w=["v34b"], dma="c2")
        A("sp", lambda e: e.dma_start(out=v34[32:33, :], in_=ln_g.rearrange("(o n) -> o n", o=1)), w=["v34c"], dma="c3")
        A("sp", lambda e: e.dma_start(out=v34[33:34, :], in_=ln_b.rearrange("(o n) -> o n", o=1)), w=["v34d"], dma="c4")
        A("sp", lambda e: e.dma_start(out=ng16[:], in_=norm_g.rearrange("(k p) -> k p", p=128)), w=["ng16"], dma="c5")
        A("sp", lambda e: e.dma_start(out=gfin[:], in_=final_g.partition_broadcast(128)), w=["gfin"], dma="c6")
        A("sp", lambda e: e.dma_start(out=sk[:], in_=sink.partition_broadcast(128)), w=["sk"], dma="c7")
        A("dve", lambda e: e.tensor_copy(out=identbf[:], in_=ident), r=["cst"], w=["identbf"])
        A("dve", lambda e: e.tensor_copy(out=onesbf[:], in_=onesm), r=["cst"], w=["onesbf"])
        A("dve", lambda e: e.tensor_copy(out=onespad[:].rearrange("p a b -> p (a b)"), in_=cst[:, 256:512]), r=["cst"], w=["onespad"])
        SETUP_LVL = {"S1": 1, "S2": 2, "S3": 3, "S4": 4}.get(STOP, 9)
        if SETUP_LVL >= 2:
          A("pool", lambda e: e.memset(vpad[:].rearrange("p a b c d -> p (a b c d)"), 0.0), w=["vpadz"])
          A("pool", lambda e: e.memset(vpadSc[:].rearrange("p a b c d -> p (a b c d)"), 0.0), w=["vpadScz"])
          A("pool", lambda e: e.memset(vpadSn[:].rearrange("p a b c d -> p (a b c d)"), 0.0), w=["vpadSnz"])
          A("pool", lambda e: e.memset(glu[:, :, 0:30], 0.0), w=[("glu", c) for c in range(8)])
          A("act", lambda e: e.activation(out=sk[:], in_=sk[:], func=AF.Exp), r=["sk"], w=["sk"])
          A("dve", lambda e: e.tensor_copy(out=esink[0:64, :], in_=sk[0:64, 0:16:2]), r=["sk"], w=["esinka"])
          A("dve", lambda e: e.tensor_copy(out=esink[64:128, :], in_=sk[64:128, 1:16:2]), r=["sk"], w=["esinkb"])
        b0, bk0 = bank()
        if SETUP_LVL >= 3:
          pass
        def _tp(e):
            ins = None
            for c in range(8):
                ins = e.transpose(out=bk0[:, c * 34:(c + 1) * 34], in_=v34[0:34, c * 128:(c + 1) * 128], identity=cst[0:34, 0:34])
            return ins
        A("pe", _tp, r=["cst", "v34a", "v34b", "v34c", "v34d"], w=[("B", b0)])
        A("dve", lambda e: e.tensor_copy(out=cw[:].rearrange("p a b -> p (a b)"), in_=bk0[:, 0:272]), r=[("B", b0)], w=["cw"])
        b1, bk1 = bank()
        A("pe", lambda e: e.transpose(out=bk1[:, 0:16], in_=ng16[:], identity=cst[0:16, 0:16]), r=["cst", "ng16"], w=[("B", b1)])
        A("dve", lambda e: e.tensor_copy(out=normg[:], in_=bk1[:, 0:16]), r=[("B", b1)], w=["normg"])

        if SETUP_LVL < 4:
            A("pool", lambda e: e.dma_start(out=cbf[:], in_=cbfd), w=["cbf"], dma="c8")
        for u, spec in enumerate(units if SETUP_LVL >= 4 else []):
            if u == 48:
                A("pool", lambda e: e.dma_start(out=cbf[:], in_=cbfd), w=["cbf"], dma="c8")
            dk = f"cv{u % 8}"
            if spec[0] == "in":
                c0 = spec[1]
                A("pool", lambda e, u=u, c0=c0: e.dma_start(
                    out=wsc[u].rearrange("p (kc j) -> p kc j", kc=16),
                    in_=w_in[:, c0:c0 + 128].rearrange("(kc p) j -> p kc j", p=128)), w=[("wsc", u)], dma=dk)
            elif spec[0] == "kdup":
                c0 = spec[1]
                A("pool", lambda e, u=u, c0=c0: e.dma_start(
                    out=wsc[u].rearrange("p (kc j) -> p kc j", kc=16)[:, :, 0:64],
                    in_=w_in[:, c0:c0 + 64].rearrange("(kc p) j -> p kc j", p=128)), w=[("wsc", u, 0)], dma=dk)
                A("pool", lambda e, u=u, c0=c0: e.dma_start(
                    out=wsc[u].rearrange("p (kc j) -> p kc j", kc=16)[:, :, 64:128],
                    in_=w_in[:, c0:c0 + 64].rearrange("(kc p) j -> p kc j", p=128)), w=[("wsc", u)], r=[("wsc", u, 0)], dma=dk)
            elif spec[0] in ("pw", "wo"):
                src = w_pw if spec[0] == "pw" else w_o
                j = spec[1]
                A("pool", lambda e, u=u, j=j, src=src: e.dma_start(
                    out=wsc[u][:, 0:1024].rearrange("p (kc j) -> p kc j", kc=8),
                    in_=src[:, j * 128:(j + 1) * 128].rearrange("(kc p) j -> p kc j", p=128)), w=[("wsc", u)], dma=dk)
            else:
                cb, kq = spec[1], spec[2]
                A("pool", lambda e, u=u, cb=cb, kq=kq: e.dma_start(
                    out=wsc[u].rearrange("p (kc j) -> p kc j", kc=4),
                    in_=w_out[kq * 512:(kq + 1) * 512, cb * 512:(cb + 1) * 512].rearrange("(kc p) j -> p kc j", p=128)),
                    w=[("wsc", u)], dma=dk)


        def load_w(name):
            if name[0] == "outS":
                name = ("out",) + tuple(name[1:])
            u = U[name]
            s = nxt("w", NW)
            ncol = 1024 if name[0] in ("pw", "wo") else 2048
            A("sp", lambda e, u=u, s=s, ncol=ncol: e.dma_start(out=Wt[s][:, 0:ncol], in_=wsc[u][:, 0:ncol]), r=[("wsc", u)], w=[("W", s)], dma=f"W{s}")
            return s

        def mm_in(name, N, cx):
            s = yield name
            b, bk = bank()
            wv = Wt[s][:].rearrange("p (kc j) -> p kc j", kc=16)
            hT = cx[1]
            def f(e, hT=hT):
                ins = None
                for kc in range(16):
                    ins = e.matmul(bk[:, 0:N], lhsT=wv[:, kc, :], rhs=hT[:, kc, 0:N], start=(kc == 0), stop=(kc == 15))
                return ins
            cx[0]("pe", f, r=[("W", s)] + cx[2], w=[("B", b)])
            return b, bk

        def mm_8(name, N, src, srckeys, cx):
            s = yield name
            b, bk = bank()
            wv = Wt[s][:, 0:1024].rearrange("p (kc j) -> p kc j", kc=8)
            def f(e):
                ins = None
                for kc in range(8):
                    ins = e.matmul(bk[:, 0:N], lhsT=wv[:, kc, :], rhs=src[:, kc, 0:N], start=(kc == 0), stop=(kc == 7))
                return ins
            cx[0]("pe", f, r=[("W", s)] + srckeys, w=[("B", b)])
            return b, bk

        def _mm_in(setcur, name, N):
            return (yield from mm_in(name, N, setcur()))

        def _mm_8(setcur, name, N, src, srckeys):
            return (yield from mm_8(name, N, src, srckeys, setcur()))

        def phaseA(xsrc, tbs, hs, part="both", which=None):
            for ti, (r0, rows) in enumerate(tbs):
                if which is not None and ti not in which:
                    continue
                s = ti % 2
                xt = xa[s]
                if part == "back":
                    phaseA_back(xt, s, r0, rows, hs)
                    continue
                sv = st4[s]
                A("sp", lambda e, s=s, r0=r0, rows=rows: e.dma_start(out=xa[s][0:rows, :], in_=xsrc[r0:r0 + rows, :]), w=[("xa", s)], dma=f"xa{s}")
                A("act", lambda e, xt=xt, sv=sv, rows=rows: e.activation(out=junk[0:rows, :], in_=xt[0:rows, :], func=AF.Square, accum_out=sv[0:rows, 0:1]),
                  r=[("xa", s)], w=["junk", ("st4", s, 0)])
                if STOP == "A1":
                    continue
                A("act", lambda e, sv=sv, rows=rows: e.activation(out=sv[0:rows, 1:2], in_=sv[0:rows, 0:1], func=AF.Sqrt, scale=1.0 / D_MODEL, bias=EPS),
                  r=[("st4", s, 0)], w=[("st4", s, 1)])
                A("dve", lambda e, sv=sv, rows=rows: e.reciprocal(out=sv[0:rows, 2:3], in_=sv[0:rows, 1:2]), r=[("st4", s, 1)], w=[("st4", s, 2)])
                if STOP == "A2":
                    continue
                A("act", lambda e, xt=xt, sv=sv, rows=rows: e.activation(out=xt[0:rows, :], in_=xt[0:rows, :], func=AF.Copy, scale=sv[0:rows, 2:3]),
                  r=[("xa", s), ("st4", s, 2)], w=[("xa", s)])
                if STOP == "A3":
                    continue
                if part == "both":
                    phaseA_back(xt, s, r0, rows, hs)

        def phaseA_back(xt, s, r0, rows, hs):
            hT = hT_s if hs == "s" else hTs[hs]
            hkey = (lambda kc: ("s_hT", 0, kc)) if hs == "s" else (lambda kc: ("hT", hs, kc))
            if True:
                for kq in range(4):
                    b, bk = bank()
                    def f(e, xt=xt, bk=bk, kq=kq, rows=rows):
                        ins = None
                        for kcl in range(4):
                            kc = kq * 4 + kcl
                            ins = e.transpose(out=bk[:, kcl * 128:kcl * 128 + rows], in_=xt[0:rows, kc * 128:(kc + 1) * 128], identity=cst[0:rows, 0:rows])
                        return ins
                    A("pe", f, r=[("xa", s), "cst"], w=[("B", b)])
                    if STOP == "A4":
                        continue
                    for kcl in range(4):
                        kc = kq * 4 + kcl
                        if kq % 2 == 0:
                            A("dve", lambda e, bk=bk, kc=kc, kcl=kcl, r0=r0, rows=rows, hT=hT: e.tensor_scalar(
                                out=hT[:, kc, r0:r0 + rows], in0=bk[:, kcl * 128:kcl * 128 + rows], scalar1=normg[:, kc:kc + 1], scalar2=None, op0=ALU.mult),
                              r=[("B", b), "normg"], w=[hkey(kc)])
                        else:
                            A("act", lambda e, bk=bk, kc=kc, kcl=kcl, r0=r0, rows=rows, hT=hT: e.activation(
                                out=hT[:, kc, r0:r0 + rows], in_=bk[:, kcl * 128:kcl * 128 + rows], func=AF.Copy, scale=normg[:, kc:kc + 1]),
                              r=[("B", b), "normg"], w=[hkey(kc)])

        A0 = A
        SKEYS = ("yb", "mg", "z", "qT", "sga", "attn_n", "glu32", "k32", "glu", "hT", "t1", "t2")

        def tile_body(N, xsrc, ydst, tbs, first, last, sample, hooks=(None, None, None, None), hs=0):
            W32 = 32
            if sample:
                def pk(k):
                    if isinstance(k, tuple) and k[0] in SKEYS:
                        return ("s_" + k[0],) + tuple(k[1:])
                    if k in ("v32", "mean", "rstdl", "tmpl"):
                        return "s_" + k
                    return k
                def A(eng, fn, r=(), w=(), dma=None):
                    return A0(eng, fn, [pk(k) for k in r], [pk(k) for k in w], dma)
                yb, merged, z, qT, sga, attn_n, glu32, k32, v32 = SB["yb"], SB["merged"], SB["z"], SB["qT"], SB["sga"], SB["attn_n"], SB["glu32"], SB["k32"], SB["v32"]
                t1 = [t1s, t1s]
                t2 = [t2s, t2s]
                mean, rstdl, tmpl = lns[0], lns[1], lns[2]
                myhT = hT_s
                myHK = [("s_hT", 0, kc) for kc in range(16)]
            else:
                A = A0
                yb, merged, z, qT, sga, attn_n, glu32, k32, v32 = PB["yb"], PB["merged"], PB["z"], PB["qT"], PB["sga"], PB["attn_n"], PB["glu32"], PB["k32"], PB["v32"]
                t1 = T1P
                t2 = T2P
                mean, rstdl, tmpl = TP[3], TP[4], TP[5]
                myhT = hTs[hs]
                myHK = [("hT", hs, kc) for kc in range(16)]
            def setcur():
                return (A, myhT, myHK)
            PL = "dve" if first else "pool"

            def hook(i):
                if hooks[i] is not None:
                    hooks[i]()
            GK = [("glu", c) for c in range(8)]
            if STOP in ('A', 'A1', 'A2', 'A3', 'A4', 'A5', 'A6'):
                return
            dgs = {}
            def build_diag(c):
                di = c % 2
                dg = diag[di]
                dgs[c] = (di, dg)
                def fd(e, c=c, dg=dg):
                    ins = None
                    for tap in range(0, 12):
                        ins = e.tensor_scalar(out=dg[:, tap, :], in0=identbf[:], scalar1=cw[:, c, tap:tap + 1], scalar2=0.0, op0=ALU.mult, op1=ALU.add)
                    return ins
                def fd_dve(e, c=c, dg=dg):
                    ins = None
                    for tap in range(0, 12):
                        ins = e.tensor_scalar(out=dg[:, tap, :], in0=identbf[:], scalar1=cw[:, c, tap:tap + 1], scalar2=None, op0=ALU.mult)
                    return ins
                if first:
                    A("dve", fd_dve, r=["identbf", "cw"], w=[("diag", di, 0)])
                else:
                    A("pool", fd, r=["identbf", "cw"], w=[("diag", di, 0)])
                def fd3(e, c=c, dg=dg):
                    ins = None
                    for tap in range(12, 27):
                        ins = e.tensor_scalar(out=dg[:, tap, :], in0=identbf[:], scalar1=cw[:, c, tap:tap + 1], scalar2=None, op0=ALU.mult)
                    return ins
                A("dve", fd3, r=["identbf", "cw"], w=[("diag", di, 2)])
                def fd2(e, c=c, dg=dg):
                    ins = None
                    for tap in range(27, 31):
                        ins = e.activation(out=dg[:, tap, :], in_=identbf[:], func=AF.Copy, scale=cw[:, c, tap:tap + 1])
                    return ins
                A("act", fd2, r=["identbf", "cw"], w=[("diag", di, 1)])
            if not sample:
                build_diag(0)
            if not sample and not first:
                A("pool", lambda e: e.tensor_copy(out=glu[:, :, 0:30], in_=ctx[:]), r=["ctx"], w=GK)
            if not sample and first:
                A("dve", lambda e: e.memset(glu[:, :, 0:30], 0.0), w=GK)
            for c in range(8):
                ba, bka = yield from _mm_in(setcur, ("a", c), N)
                held.add(ba)
                bb, bkb = yield from _mm_in(setcur, ("b", c), N)
                held.discard(ba)
                s = nxt("sbt", 2)
                A("act", lambda e, s=s, bkb=bkb: e.activation(out=sbt[s][:, 0:N], in_=bkb[:, 0:N], func=AF.Sigmoid), r=[("B", bb)], w=[("sbt", s)])
                if sample:
                    A("dve", lambda e, s=s, bka=bka, c=c: e.tensor_tensor(
                        out=gluS[:, c, :, 30:46], in0=bka[:, 0:32].rearrange("p (s t) -> p s t", s=2),
                        in1=sbt[s][:, 0:32].rearrange("p (s t) -> p s t", s=2), op=ALU.mult),
                      r=[("B", ba), ("sbt", s)], w=[("glu", c)])
                else:
                    A("dve", lambda e, s=s, bka=bka, c=c: e.tensor_tensor(out=glu[:, c, 30:30 + N], in0=bka[:, 0:N], in1=sbt[s][:, 0:N], op=ALU.mult),
                      r=[("B", ba), ("sbt", s)], w=[("glu", c)])
                if last or sample:
                    A("dve", lambda e, s=s, bka=bka, c=c: e.tensor_tensor(out=glu32[:, c, :], in0=bka[:, N - W32:N], in1=sbt[s][:, N - W32:N], op=ALU.mult),
                      r=[("B", ba), ("sbt", s)], w=[("glu32", c)])
            if STOP == 'B1':
                return
            if last or sample:
                c0 = 0 if sample else 2
                nr = 32 - c0
                bs = [bank(), bank()]
                for hh in range(2):
                    b, bk = bs[hh]
                    def f(e, bk=bk, hh=hh):
                        ins = None
                        for cc in range(4):
                            c = hh * 4 + cc
                            ins = e.transpose(out=bk[0:nr, cc * 128:(cc + 1) * 128], in_=glu32[:, c, c0:32], identity=ident)
                        return ins
                    A("pe", f, r=[("glu32", hh * 4 + cc) for cc in range(4)] + ["cst"], w=[("B", b)])
                    A("dve", lambda e, bk=bk, hh=hh: e.tensor_copy(out=cout[0:nr, hh * 512:(hh + 1) * 512], in_=bk[0:nr, :]), r=[("B", b)], w=[("cout", hh)])
                if sample:
                    for s_ in range(2):
                        A("act", lambda e, s_=s_: e.dma_start(out=csw[s_, 14:30, :], in_=cout[s_ * 16:(s_ + 1) * 16, :]), r=[("cout", 0), ("cout", 1)], dma=f"o{s_}")
                        A("act", lambda e, s_=s_: e.dma_start(out=csw[s_, 0:14, :], in_=sc[s_, 16:30, :]), dma=f"o{2 + s_}")
                else:
                    A("act", lambda e: e.dma_start(out=cwin, in_=cout[0:30, :]), r=[("cout", 0), ("cout", 1)], dma="o0")
            bsum, bksum = 6, banks[6]
            bsq, bksq = 7, banks[7]
            if sample:
                build_diag(0)
            prev_stat = None
            for c in range(8):
                if c < 7:
                    build_diag(c + 1)
                di, dg = dgs[c]
                b, bk = bank()
                def fc(e, c=c, bk=bk, dg=dg):
                    ins = None
                    for tap in range(31):
                        if sample:
                            rhs = gluS[:, c, :, tap:tap + 16]
                            o = bk[:, 0:32].rearrange("p (s t) -> p s t", s=2)
                        else:
                            rhs = glu[:, c, tap:tap + N]
                            o = bk[:, 0:N]
                        ins = e.matmul(o, lhsT=dg[:, tap, :], rhs=rhs, start=(tap == 0), stop=(tap == 30))
                    return ins
                A("pe", fc, r=[("diag", di, 0), ("diag", di, 1), ("diag", di, 2), ("glu", c)], w=[("B", b)])
                A("act", lambda e, c=c, bk=bk: e.activation(out=yb[:, c, 0:N], in_=bk[:, 0:N], func=AF.Identity, bias=cw[:, c, 31:32]),
                  r=[("B", b), "cw"], w=[("yb", c)])
                ysqb = (ysq if c % 2 == 0 else TP[0])[:].bitcast(BF16)
                ysk = "ysq" if c % 2 == 0 else ("T", 0)
                def fs(e, c=c, bk=bk, ysqb=ysqb):
                    e.activation(out=ysqb[:, 0:N], in_=bk[:, 0:N], func=AF.Square, bias=cw[:, c, 31:32])
                    return e.activation(out=ysqb[:, TT:TT + N], in_=bk[:, 0:N], func=AF.Identity, bias=cw[:, c, 31:32])
                A("act", fs, r=[("B", b), "cw"], w=[ysk])
                def fst(e, c=c, ysqb=ysqb):
                    e.matmul(bksum[:, 0:N], lhsT=onesbf[:], rhs=ysqb[:, TT:TT + N], start=(c == 0), stop=(c == 7))
                    return e.matmul(bksq[:, 0:N], lhsT=onesbf[:], rhs=ysqb[:, 0:N], start=(c == 0), stop=(c == 7))
                if prev_stat is not None:
                    A("pe", prev_stat[0], r=[prev_stat[1], "onesbf"], w=[("B", bsum), ("B", bsq)])
                prev_stat = (fst, ysk)
                if c == 2:
                    hook(0)
            A("pe", prev_stat[0], r=[prev_stat[1], "onesbf"], w=[("B", bsum), ("B", bsq)])
            if not sample and not last:
                A("dve", lambda e: e.tensor_copy(out=ctx[:], in_=glu[:, :, N:N + 30]), r=GK, w=["ctx"])
            A("act", lambda e: e.activation(out=mean[:, 0:N], in_=bksum[:, 0:N], func=AF.Copy), r=[("B", bsum)], w=["mean"])
            A("dve", lambda e: e.tensor_tensor(out=tmpl[:, 0:N], in0=mean[:, 0:N], in1=mean[:, 0:N], op=ALU.mult), r=["mean"], w=["tmpl"])
            A("dve", lambda e: e.tensor_tensor(out=tmpl[:, 0:N], in0=bksq[:, 0:N], in1=tmpl[:, 0:N], op=ALU.subtract), r=[("B", bsq), "tmpl"], w=["tmpl"])
            A("act", lambda e: e.activation(out=tmpl[:, 0:N], in_=tmpl[:, 0:N], func=AF.Sqrt, bias=EPS), r=["tmpl"], w=["tmpl"])
            A("dve", lambda e: e.reciprocal(out=rstdl[:, 0:N], in_=tmpl[:, 0:N]), r=["tmpl"], w=["rstdl"])
            if DBG and first:
                A("pool", lambda e: e.dma_start(out=dbg["ypre"], in_=yraw[:]), r=[("yb", c) for c in range(8)], dma="dbg0")
                A("pool", lambda e: e.dma_start(out=dbg["stat"][:, 0:512], in_=mean[:]), r=["mean"], dma="dbg0")
                A("pool", lambda e: e.dma_start(out=dbg["stat"][:, 512:1024], in_=rstdl[:]), r=["rstdl"], dma="dbg0")
                A("pool", lambda e: e.dma_start(out=dbg["stat"][:, 1024:1536], in_=tmpl[:]), r=["tmpl"], dma="dbg0")
            def ln_apply(c):
                en = PL if (c % 3 == 2 and not sample) else "dve"
                A(en, lambda e, c=c: e.tensor_tensor(out=yb[:, c, 0:N], in0=yb[:, c, 0:N], in1=mean[:, 0:N], op=ALU.subtract), r=[("yb", c), "mean"], w=[("yb", c)])
                A(en, lambda e, c=c: e.tensor_tensor(out=yb[:, c, 0:N], in0=yb[:, c, 0:N], in1=rstdl[:, 0:N], op=ALU.mult), r=[("yb", c), "rstdl"], w=[("yb", c)])
                A("act", lambda e, c=c: e.activation(out=yb[:, c, 0:N], in_=yb[:, c, 0:N], func=AF.Silu, scale=cw[:, c, 32:33], bias=cw[:, c, 33:34]),
                  r=[("yb", c), "cw"], w=[("yb", c)])
            if STOP == 'B2':
                return
            hook(1)
            if STOP == 'B3':
                return
            if DBG and first:
                A("pool", lambda e: e.dma_start(out=dbg["yb"], in_=yraw[:]), r=[("yb", c) for c in range(8)], dma="dbg0")
            for c in range(8):
                b, bk = yield from _mm_in(setcur, ("q", c), N)
                A("act", lambda e, c=c, bk=bk: e.activation(out=qT[:, c, 0:N], in_=bk[:, 0:N], func=AF.Copy), r=[("B", b)], w=[("qT", c)])
                ln_apply(c)
                if c == 2:
                    hook(3)
            for g in range(2):
                b, bk = yield from _mm_in(setcur, ("k", g), N)
                if sample:
                    A("dve", lambda e, g=g, bk=bk: e.tensor_copy(out=kTs[g][:, :, 128:144], in_=bk[:, 0:32].rearrange("p (s t) -> p s t", s=2)),
                      r=[("B", b)], w=[("kTn", g)])
                else:
                    A("dve", lambda e, g=g, bk=bk: e.tensor_copy(out=kT[g][:, 128:128 + N], in_=bk[:, 0:N]), r=[("B", b)], w=[("kT", g)])
                if last or sample:
                    nk = 32 if sample else 128
                    A("dve", lambda e, g=g, bk=bk, nk=nk: e.tensor_copy(out=k32[g][:, 0:nk], in_=bk[:, N - nk:N]), r=[("B", b)], w=[("k32", g)])
            b, bk = yield from _mm_in(setcur, ("v", 0), N)
            A("dve", lambda e, bk=bk: e.tensor_copy(out=vT[:, 0:N], in_=bk[:, 0:N]), r=[("B", b)], w=["vT"])
            if last or sample:
                nk = 32 if sample else 128
                A("dve", lambda e, bk=bk, nk=nk: e.tensor_copy(out=v32[:, 0:nk], in_=bk[:, N - nk:N]), r=[("B", b)], w=["v32"])
            bv, bkv = bank()
            pbf = bkv[:].bitcast(BF16)
            if sample:
                def fv(e):
                    ins = None
                    for s_ in range(2):
                        ins = e.transpose(out=pbf[0:16, s_ * 128:(s_ + 1) * 128], in_=vT[:, s_ * 16:(s_ + 1) * 16], identity=identbf[:])
                    return ins
                A("pe", fv, r=["vT", "identbf"], w=[("B", bv)])
                for g in range(2):
                    for par in range(2):
                        A("dve", lambda e, g=g, par=par: e.tensor_copy(
                            out=vpadSn[:, :, g, par, par * 64:(par + 1) * 64],
                            in_=pbf[0:16, 0:256].rearrange("p (s c) -> p s c", s=2)[:, :, g * 64:(g + 1) * 64]),
                          r=[("B", bv), "vpadSnz"], w=[("vpadSn", g, par)])
            else:
                nb = N // 128
                def fv(e):
                    ins = None
                    for blk in range(nb):
                        ins = e.transpose(out=pbf[:, blk * 128:(blk + 1) * 128], in_=vT[:, blk * 128:(blk + 1) * 128], identity=identbf[:])
                    return ins
                A("pe", fv, r=["vT", "identbf"], w=[("B", bv)])
                for g in range(2):
                    for par in range(2):
                        A("dve", lambda e, g=g, par=par: e.tensor_copy(
                            out=vpad[:, 1:1 + nb, g, par, par * 64:(par + 1) * 64],
                            in_=pbf[:, 0:nb * 128].rearrange("p (s c) -> p s c", s=nb)[:, :, g * 64:(g + 1) * 64]),
                          r=[("B", bv), "vpadz"], w=[("vpad", g, par)])
            if last or sample:
                nk = 32 if sample else 128
                bo, bko = bank()
                def fo(e):
                    e.transpose(out=bko[0:nk, 0:128], in_=k32[0][:, 0:nk], identity=ident)
                    e.transpose(out=bko[0:nk, 128:256], in_=k32[1][:, 0:nk], identity=ident)
                    return e.transpose(out=bko[0:nk, 256:384], in_=v32[:, 0:nk], identity=ident)
                A("pe", fo, r=[("k32", 0), ("k32", 1), "v32", "cst"], w=[("B", bo)])
                A("dve", lambda e: e.tensor_copy(out=kvo[0:nk, 0:64], in_=bko[0:nk, 0:64]), r=[("B", bo)], w=["kvo0"])
                A("dve", lambda e: e.tensor_copy(out=kvo[0:nk, 64:128], in_=bko[0:nk, 192:256]), r=[("B", bo)], w=["kvo1"])
                A("dve", lambda e: e.tensor_copy(out=kvo[0:nk, 128:256], in_=bko[0:nk, 256:384]), r=[("B", bo)], w=["kvo2"])
                KV = ["kvo0", "kvo1", "kvo2"]
                if sample:
                    for s_ in range(2):
                        A("act", lambda e, s_=s_: e.dma_start(out=ksw[s_, 112:128, :], in_=kvo[s_ * 16:(s_ + 1) * 16, 0:128]), r=KV, dma=f"o{4 + s_}")
                        A("act", lambda e, s_=s_: e.dma_start(out=vsw[s_, 112:128, :], in_=kvo[s_ * 16:(s_ + 1) * 16, 128:256]), r=KV, dma=f"o{6 + s_}")
                        A("act", lambda e, s_=s_: e.dma_start(out=ksw[s_, 0:112, :], in_=ck[s_, 16:128, :]), dma=f"o{8 + s_}")
                        A("act", lambda e, s_=s_: e.dma_start(out=vsw[s_, 0:112, :], in_=cv[s_, 16:128, :]), dma=f"o{10 + s_}")
                else:
                    A("act", lambda e: e.dma_start(out=kwin, in_=kvo[:, 0:128]), r=KV, dma="o1")
                    A("act", lambda e: e.dma_start(out=vwin, in_=kvo[:, 128:256]), r=KV, dma="o2")
            for c in range(8):
                b, bk = yield from _mm_in(setcur, ("ga", c), N)
                A("act", lambda e, c=c, bk=bk: e.activation(out=sga[:, c, 0:N], in_=bk[:, 0:N], func=AF.Silu), r=[("B", b)], w=[("sga", c)])

            if STOP == 'C1':
                return
            for c in range(8):
                b, bk = yield from _mm_in(setcur, ("gc", c), N)
                sgt, sgk = (TP[6], ("T", 6)) if c % 2 == 0 else (TP[7], ("T", 7))
                A("act", lambda e, bk=bk, sgt=sgt: e.activation(out=sgt[:, 0:N], in_=bk[:, 0:N], func=AF.Silu), r=[("B", b)], w=[sgk])
                A("dve", lambda e, c=c, sgt=sgt: e.tensor_tensor(out=z[:, c, 0:N], in0=yb[:, c, 0:N], in1=sgt[:, 0:N], op=ALU.mult), r=[("yb", c), sgk], w=[("z", c)])
            QK = [("qT", c) for c in range(8)]
            hook(2)
            st["nbank"] = 8

            pend = []

            def attn(nq, qsl, ktiles, outsl, extra_r, pe_bias=False):
                for g in range(2):
                    pts = attn_scores(nq, qsl, ktiles, extra_r, pe_bias, g)
                    if pend:
                        attn_pv(*pend.pop())
                    pend.append((nq, outsl, extra_r, g, pts))

            def attn_flush():
                if pend:
                    attn_pv(*pend.pop())

            def attn_scores(nq, qsl, ktiles, extra_r, pe_bias, g):
                if True:
                    pts = []
                    for (nk, kfn, vfn, nd, rk) in ktiles:
                        for par in range(2):
                            b, bk = bank()
                            rows = slice(par * 64, (par + 1) * 64)
                            if pe_bias:
                                def fsc(e, bk=bk, nk=nk, kfn=kfn, g=g, rows=rows, nd=nd, par=par):
                                    e.matmul(bk[0:nk, 0:4 * nq].rearrange("p (j q) -> p j q", j=4),
                                             lhsT=kfn(g)[rows, :], rhs=qT[rows, 4 * g:4 * g + 4, qsl], start=True, stop=False)
                                    rb = 256 + ((g * 2 + par) * 2) * 512
                                    e.matmul(bk[0:nk, 0:512], lhsT=nd, rhs=cbf[:, rb:rb + 512], start=False, stop=False)
                                    return e.matmul(bk[0:nk, 0:512], lhsT=nd, rhs=cbf[:, rb + 512:rb + 1024], start=False, stop=True)
                                A("pe", fsc, r=QK + rk + extra_r + ["cbf"], w=[("B", b)])
                                pi = nxt("PT", 8)
                                A("act", lambda e, bk=bk, pi=pi, nk=nk: e.activation(out=PT[pi][0:nk, 0:4 * nq], in_=bk[0:nk, 0:4 * nq], func=AF.Exp, scale=SCALE),
                                  r=[("B", b)], w=[("PT", pi)])
                                pts.append((pi, nk, vfn, par, rk))
                                continue
                            A("pe", lambda e, bk=bk, nk=nk, kfn=kfn, g=g, rows=rows: e.matmul(
                                bk[0:nk, 0:4 * nq].rearrange("p (j q) -> p j q", j=4),
                                lhsT=kfn(g)[rows, :], rhs=qT[rows, 4 * g:4 * g + 4, qsl], start=True, stop=True),
                              r=QK + rk + extra_r, w=[("B", b)])
                            si = nxt("Sb", 2)
                            def fb(e, bk=bk, nk=nk, nd=nd, si=si, g=g, par=par):
                                ins = None
                                for j in range(4):
                                    h = 8 * g + 2 * j + par
                                    ins = e.scalar_tensor_tensor(out=Sb[si][0:nk, j * nq:(j + 1) * nq], in0=nd[0:nk, 0:nq], scalar=_slope(h) / SCALE,
                                                                 op0=ALU.mult, in1=bk[0:nk, j * nq:(j + 1) * nq], op1=ALU.add)
                                return ins
                            A("dve", fb, r=[("B", b), "cst"], w=[("Sb", si)])
                            pi = nxt("PT", 8)
                            A("act", lambda e, si=si, pi=pi, nk=nk: e.activation(out=PT[pi][0:nk, 0:4 * nq], in_=Sb[si][0:nk, 0:4 * nq], func=AF.Exp, scale=SCALE),
                              r=[("Sb", si)], w=[("PT", pi)])
                            pts.append((pi, nk, vfn, par, rk))
                return pts

            def attn_pv(nq, outsl, extra_r, g, pts):
                if True:
                    bacc, bkacc = bank()
                    bden, bkden = bank()
                    def fpv(e, pts=pts, g=g, bkacc=bkacc):
                        ins = None
                        for i, (pi, nk, vfn, par, rk) in enumerate(pts):
                            ins = e.matmul(bkacc[:, 0:4 * nq], lhsT=vfn(g, par), rhs=PT[pi][0:nk, 0:4 * nq], start=(i == 0), stop=(i == len(pts) - 1))
                        return ins
                    def fden(e, pts=pts, bkden=bkden):
                        ins = None
                        for i, (pi, nk, vfn, par, rk) in enumerate(pts):
                            ins = e.matmul(bkden[:, 0:4 * nq], lhsT=onespad[0:nk, par, :], rhs=PT[pi][0:nk, 0:4 * nq], start=(i == 0), stop=(i == len(pts) - 1))
                        return ins
                    prk = [("PT", p[0]) for p in pts] + sum([p[4] for p in pts], []) + extra_r
                    A("pe", fpv, r=prk, w=[("B", bacc)])
                    A("pe", fden, r=prk + ["onespad"], w=[("B", bden)])
                    def fdn(e, g=g, bkden=bkden):
                        ins = None
                        for j in range(4):
                            ins = e.tensor_scalar(out=dn[:, j * nq:(j + 1) * nq], in0=bkden[:, j * nq:(j + 1) * nq], scalar1=esink[:, 4 * g + j:4 * g + j + 1], scalar2=None, op0=ALU.add)
                        return ins
                    A("dve", fdn, r=[("B", bden), "esinka", "esinkb"], w=["dn"])
                    A("dve", lambda e: e.reciprocal(out=dn[:, 0:4 * nq], in_=dn[:, 0:4 * nq]), r=["dn"], w=["dn"])
                    A("dve", lambda e, bkacc=bkacc: e.tensor_tensor(out=o1[:, 0:4 * nq], in0=bkacc[:, 0:4 * nq], in1=dn[:, 0:4 * nq], op=ALU.mult), r=[("B", bacc), "dn"], w=["o1"])
                    A(PL, lambda e, g=g: e.tensor_tensor(out=attn_n[:, 4 * g:4 * g + 4, outsl], in0=o1[:, 0:4 * nq].rearrange("p (j q) -> p j q", j=4),
                                                           in1=sga[:, 4 * g:4 * g + 4, outsl], op=ALU.mult),
                      r=["o1"] + [("sga", c) for c in range(4 * g, 4 * g + 4)], w=[("attn_n", g, outsl.start)])

            VK = [("vpad", g, par) for g in range(2) for par in range(2)]
            if sample:
                for s_ in range(2):
                    kt = [
                        (128, lambda g, s_=s_: kTs[g][:, s_, 0:128], lambda g, par, s_=s_: vpadSc[:, s_, g, par, :], ndSc,
                         [("kTc", 0, s_), ("kTc", 1, s_)] + [("vpadSc", s_)]),
                        (16, lambda g, s_=s_: kTs[g][:, s_, 128:144], lambda g, par, s_=s_: vpadSn[:, s_, g, par, :], ndSn,
                         [("kTn", 0), ("kTn", 1)] + [("vpadSn", g, par) for g in range(2) for par in range(2)]),
                    ]
                    attn(16, slice(s_ * 16, (s_ + 1) * 16), kt, slice(s_ * 16, (s_ + 1) * 16), [])
                attn_flush()
            else:
                for i in range(N // 128):
                    kt = []
                    if not (first and i == 0):
                        kt.append((128, lambda g, i=i: kT[g][:, i * 128:(i + 1) * 128], lambda g, par, i=i: vpad[:, i, g, par, :], cbf[:, 0:128],
                                   [("kT", 0), ("kT", 1)] + VK))
                    kt.append((128, lambda g, i=i: kT[g][:, 128 + i * 128:128 + (i + 1) * 128], lambda g, par, i=i: vpad[:, i + 1, g, par, :], cbf[:, 128:256],
                               [("kT", 0), ("kT", 1)] + VK))
                    attn(128, slice(i * 128, (i + 1) * 128), kt, slice(i * 128, (i + 1) * 128), [], pe_bias=True)
                attn_flush()
                if not last:
                    for g in range(2):
                        A(PL, lambda e, g=g: e.tensor_copy(out=kT[g][:, 0:128], in_=kT[g][:, N:N + 128]), r=[("kT", g)], w=[("kT", g)])
                    A(PL, lambda e: e.tensor_copy(out=vpad[:, 0], in_=vpad[:, N // 128]), r=VK, w=VK)
            st["nbank"] = 6
            st["bank"] = st["bank"] % 6
            AK = [("attn_n", g, i * (16 if sample else 128)) for g in range(2) for i in range(2 if sample else N // 128)]

            if STOP == 'C2':
                return
            ZK = [("z", c) for c in range(8)]
            xr4 = [xr[0], xr[1], xa[0], xa[1]]
            xk4 = [("xr", 0), ("xr", 1), ("xa", 0), ("xa", 1)]
            if not sample:
                for ti, (r0, rows) in enumerate(tbs):
                    A("sp", lambda e, s=ti, r0=r0, rows=rows: e.dma_start(out=xr4[s][0:rows, :], in_=xsrc[r0:r0 + rows, :]), w=[xk4[ti]], dma=f"xr{ti}")
            for j in range(16):
                bA, bkA = yield from _mm_8(setcur, ("pw", j), N, z, ZK)
                held.add(bA)
                bB, bkB = yield from _mm_in(setcur, ("mc", j), N)
                held.discard(bA)
                s = nxt("smc", 2)
                A("act", lambda e, s=s, bkB=bkB: e.activation(out=smc[s][:, 0:N], in_=bkB[:, 0:N], func=AF.Sigmoid), r=[("B", bB)], w=[("smc", s)])
                A("dve", lambda e, s=s, bkA=bkA: e.tensor_tensor(out=t1[s][:, 0:N], in0=bkA[:, 0:N], in1=smc[s][:, 0:N], op=ALU.mult), r=[("B", bA), ("smc", s)], w=[("t1", s)])
                bC, bkC = yield from _mm_8(setcur, ("wo", j), N, attn_n, AK)
                held.add(bC)
                bD, bkD = yield from _mm_in(setcur, ("ma", j), N)
                held.discard(bC)
                A("act", lambda e, s=s, bkD=bkD: e.activation(out=smc[s][:, 0:N], in_=bkD[:, 0:N], func=AF.Sigmoid), r=[("B", bD), ("t1", s)], w=[("smc", s)])
                A("dve", lambda e, s=s, bkC=bkC: e.tensor_tensor(out=t2[s][:, 0:N], in0=bkC[:, 0:N], in1=smc[s][:, 0:N], op=ALU.mult), r=[("B", bC), ("smc", s)], w=[("t2", s)])
                A(PL, lambda e, s=s, j=j: e.tensor_tensor(out=merged[:, j, 0:N], in0=t1[s][:, 0:N], in1=t2[s][:, 0:N], op=ALU.add), r=[("t1", s), ("t2", s)], w=[("mg", j)])
            MK = [("mg", j) for j in range(16)]

            if STOP == 'M':
                return
            if DBG and first:
                fl = lambda t: t[:].rearrange("p a b -> p (a b)")
                A("pool", lambda e: e.dma_start(out=dbg["z"], in_=fl(z)), r=ZK, dma="dbg1")
                A("pool", lambda e: e.dma_start(out=dbg["attn_n"], in_=fl(attn_n)), r=AK, dma="dbg2")
                pass
                A("pool", lambda e: e.dma_start(out=dbg["qT"], in_=fl(qT)), r=QK, dma="dbg4")
                A("pool", lambda e: e.dma_start(out=dbg["mg0"], in_=merged[:, 0:8, :].rearrange("p a b -> p (a b)")), r=MK, dma="dbg5")
                A("pool", lambda e: e.dma_start(out=dbg["mg1"], in_=merged[:, 8:16, :].rearrange("p a b -> p (a b)")), r=MK, dma="dbg6")
                A("pool", lambda e: e.dma_start(out=dbg["kT0"][:, 0:640], in_=kT[0][:]), r=[("kT", 0)], dma="dbg7")
                A("pool", lambda e: e.dma_start(out=dbg["vpad"][:, 0:2560], in_=vpad[:].rearrange("p a b c d -> p (a b c d)")), r=VK, dma="dbg8")
            if sample:
                yield ("barrier",)
            halves = [tbs]
            st["nbank"] = 8
            for half in halves:
                slots = []
                for ti, (r0, rows) in enumerate(half):
                    s = ti
                    slots.append(s)
                    if sample:
                        A("sp", lambda e, s=s, r0=r0, rows=rows: e.dma_start(out=xr4[s][0:rows, :], in_=xsrc[r0:r0 + rows, :]), w=[xk4[s]], dma=f"xr{s}")
                for cb in range(4):
                    pbs = [bank() for _ in half]
                    for kq in range(4):
                        ws = yield (("outS", cb, kq) if sample else ("out", cb, kq))
                        wv = Wt[ws][:].rearrange("p (kc j) -> p kc j", kc=4)
                        def f(e, wv=wv, pbs=pbs, kq=kq, half=half):
                            ins = None
                            for (r0, rows), (b, bk) in zip(half, pbs):
                                for kcl in range(4):
                                    ins = e.matmul(bk[0:rows, :], lhsT=merged[:, kq * 4 + kcl, r0:r0 + rows], rhs=wv[:, kcl, :],
                                                   start=(kq == 0 and kcl == 0), stop=(kq == 3 and kcl == 3))
                            return ins
                        A("pe", f, r=[("W", ws)] + MK, w=[("B", b) for (b, bk) in pbs])
                    for (r0, rows), (b, bk), s in zip(half, pbs, slots):
                        A("dve", lambda e, bk=bk, s=s, rows=rows, cb=cb: e.tensor_tensor(
                            out=xr4[s][0:rows, cb * 512:(cb + 1) * 512], in0=bk[0:rows, :], in1=xr4[s][0:rows, cb * 512:(cb + 1) * 512], op=ALU.add),
                          r=[("B", b), xk4[s]], w=[xk4[s]])
                for (r0, rows), s in zip(half, slots):
                    sv = st5[s]
                    A("act", lambda e, s=s, sv=sv, rows=rows: e.activation(out=junk[0:rows, :], in_=xr4[s][0:rows, :], func=AF.Square, accum_out=sv[0:rows, 3:4]),
                      r=[xk4[s]], w=["junk", ("st4", s, 3)])
                    A("act", lambda e, sv=sv, rows=rows: e.activation(out=sv[0:rows, 3:4], in_=sv[0:rows, 3:4], func=AF.Sqrt, scale=1.0 / D_MODEL, bias=EPS),
                      r=[("st4", s, 3)], w=[("st4", s, 3)])
                    A("dve", lambda e, sv=sv, rows=rows: e.reciprocal(out=sv[0:rows, 3:4], in_=sv[0:rows, 3:4]), r=[("st4", s, 3)], w=[("st4", s, 3)])
                    if last or sample or first or s >= 2:
                        A("dve", lambda e, s=s, sv=sv, rows=rows: e.scalar_tensor_tensor(out=xr4[s][0:rows, :], in0=xr4[s][0:rows, :], scalar=sv[0:rows, 3:4], op0=ALU.mult,
                                                                                        in1=gfin[0:rows, :], op1=ALU.mult),
                          r=[xk4[s], ("st4", s, 3), "gfin"], w=[xk4[s]])
                    else:
                        A("act", lambda e, s=s, sv=sv, rows=rows: e.activation(out=xr4[s][0:rows, :], in_=xr4[s][0:rows, :], func=AF.Copy, scale=sv[0:rows, 3:4]),
                          r=[xk4[s], ("st4", s, 3)], w=[xk4[s]])
                        A("pool", lambda e, s=s, rows=rows: e.tensor_tensor(out=xr4[s][0:rows, :], in0=xr4[s][0:rows, :], in1=gfin[0:rows, :], op=ALU.mult),
                          r=[xk4[s], "gfin"], w=[xk4[s]])
                    yq = "act" if (last or sample or first) else "pool"
                    A(yq, lambda e, s=s, r0=r0, rows=rows: e.dma_start(out=ydst[r0:r0 + rows, :], in_=xr4[s][0:rows, :]), r=[xk4[s]], dma=(f"yp{s}" if yq == "pool" else f"y{s}"))
            st["nbank"] = 6
            st["bank"] = st["bank"] % 6

        def sample_prep():
            for s_ in range(2):
                A("sp", lambda e, s_=s_: e.dma_start(out=ckl, in_=ck[s_]), w=["ckl"], dma="ckl")
                A("sp", lambda e, s_=s_: e.dma_start(out=cvl, in_=cv[s_]), w=["cvl"], dma="cvl")
                A("sp", lambda e, s_=s_: e.dma_start(out=scl, in_=sc[s_]), w=["scl"], dma="scl")
                for d in range(2):
                    A("dve", lambda e, d=d: e.tensor_copy(out=ctk[:, :, d, :], in_=ckl.rearrange("p (g c) -> p g c", g=2)), r=["ckl"], w=[("ctk", d)])
                for g in range(2):
                    b, bk = bank()
                    A("pe", lambda e, g=g, bk=bk: e.transpose(out=bk[:, 0:128], in_=ctk[:, g].rearrange("p a b -> p (a b)"), identity=ident),
                      r=[("ctk", 0), ("ctk", 1), "cst"], w=[("B", b)])
                    A("dve", lambda e, g=g, bk=bk, s_=s_: e.tensor_copy(out=kTs[g][:, s_, 0:128], in_=bk[:, 0:128]), r=[("B", b)], w=[("kTc", g, s_)])
                for g in range(2):
                    for par in range(2):
                        A("dve", lambda e, g=g, par=par, s_=s_: e.tensor_copy(out=vpadSc[:, s_, g, par, par * 64:(par + 1) * 64], in_=cvl[:, g * 64:(g + 1) * 64]),
                          r=["cvl", "vpadScz"], w=[("vpadSc", s_)])
                for hh in range(2):
                    b, bk = bank()
                    def f(e, bk=bk, hh=hh):
                        ins = None
                        for cc in range(4):
                            c = hh * 4 + cc
                            ins = e.transpose(out=bk[:, cc * 32:cc * 32 + 30], in_=scl[0:30, c * 128:(c + 1) * 128], identity=cst[0:30, 0:30])
                        return ins
                    A("pe", f, r=["scl", "cst"], w=[("B", b)])
                    A("dve", lambda e, bk=bk, hh=hh, s_=s_: e.tensor_copy(out=gluS[:, hh * 4:hh * 4 + 4, s_, 0:30],
                                                                        in_=bk[:, 0:128].rearrange("p (c t) -> p c t", c=4)[:, :, 0:30]),
                      r=[("B", b)], w=[("s_glu", hh * 4 + cc) for cc in range(4)])

        TB4 = [(i * 128, 128) for i in range(4)]

        def drive(gens):
            reqs = {}
            active = []
            for i, g in enumerate(gens):
                try:
                    reqs[i] = next(g)
                    active.append(i)
                except StopIteration:
                    pass
            while active:
                cand = [j for j in active if reqs[j] != ("barrier",)]
                if not cand:
                    j = active[0]
                    try:
                        reqs[j] = gens[j].send(None)
                    except StopIteration:
                        active.remove(j)
                    continue
                name = reqs[cand[0]]
                slot = load_w(name)
                for j in [j for j in cand if reqs[j] == name]:
                    try:
                        reqs[j] = gens[j].send(slot)
                    except StopIteration:
                        active.remove(j)

        if SETUP_LVL >= 9:
            def hooks_for(t, hs):
                xsrc = x[t * TT:(t + 1) * TT, :]
                return (lambda: phaseA(xsrc, TB4, hs, "front", (0, 1)),
                        lambda: phaseA(xsrc, TB4, hs, "back", (0, 1)),
                        lambda: phaseA(xsrc, TB4, hs, "back", (2, 3)),
                        lambda: phaseA(xsrc, TB4, hs, "front", (2, 3)))
            if SAMPLE:
                sample_prep()
                phaseA(xs, [(0, 32)], "s")
            if NT > 0:
                phaseA(x[0:TT, :], TB4, 0)
            for t in range(NT):
                hs = t % 2
                hk = hooks_for(t + 1, 1 - hs) if t + 1 < NT else (None, None, None, None)
                gens = [tile_body(TT, x[t * TT:(t + 1) * TT, :], y[t * TT:(t + 1) * TT, :], TB4,
                                  first=(t == 0), last=(t == NT - 1), sample=False, hooks=hk, hs=hs)]
                if t == NT - 1 and SAMPLE:
                    gens.append(tile_body(32, xs, ys, [(0, 32)], first=False, last=False, sample=True))
                drive(gens)
            if SAMPLE and NT == 0:
                drive([tile_body(32, xs, ys, [(0, 32)], first=False, last=False, sample=True)])

        S.plan()
        sems = {n: es.enter_context(nc.semaphore(n)) for n in S.sem_names()}
        with nc.Block() as block:
            @block.tensor
            def _(e):
                S.run_engine("pe", e, sems)

            @block.scalar
            def _(e):
                S.run_engine("act", e, sems)

            @block.vector
            def _(e):
                S.run_engine("dve", e, sems)

            @block.gpsimd
            def _(e):
                S.run_engine("pool", e, sems)

            @block.sync
            def _(e):
                S.run_engine("sp", e, sems, final_wait=True)
    return nc


_NC_CACHE = {}


def kernel(x_prompt, x_sample, cache_k, cache_v, state_conv, norm_g, w_in, conv_w, conv_b, ln_g, ln_b,
           w_conv_pw, attn_sink, w_o_attn, w_out, final_g):
    f = lambda a: np.ascontiguousarray(np.asarray(a, dtype=np.float32))
    x_prompt, x_sample, cache_k, cache_v, state_conv = map(f, (x_prompt, x_sample, cache_k, cache_v, state_conv))
    shared = {
        "norm_g": f(norm_g).reshape(2048), "w_in": f(w_in).reshape(2048, IN_COLS), "conv_w": f(conv_w).reshape(31, 1024),
        "conv_b": f(conv_b).reshape(1024), "ln_g": f(ln_g).reshape(1024), "ln_b": f(ln_b).reshape(1024),
        "w_pw": f(w_conv_pw).reshape(1024, 2048), "sink": f(attn_sink).reshape(16), "w_o": f(w_o_attn).reshape(1024, 2048),
        "w_out": f(w_out).reshape(2048, 2048), "final_g": f(final_g).reshape(2048), "consts": make_consts(), "cbf": make_cbf(),
    }
    n = 8
    in_maps = []
    for c in range(n):
        m = dict(shared)
        m["x"] = x_prompt[c]
        m["xs"] = x_sample[2 * c:2 * c + 2].reshape(32, 2048)
        m["ck"] = cache_k[0, 2 * c:2 * c + 2].reshape(2, 128, 128)
        m["cv"] = cache_v[0, 2 * c:2 * c + 2].reshape(2, 128, 128)
        m["sc"] = state_conv[0, 2 * c:2 * c + 2]
        in_maps.append(m)
    if "nc" not in _NC_CACHE:
        _NC_CACHE["nc"] = build_nc(8, True)
    res = run_bass_kernel_spmd(_NC_CACHE["nc"], in_maps, core_ids=list(range(n)))
    R = res.results
    g = lambda k: np.stack([np.asarray(R[c][k], dtype=np.float32) for c in range(n)])
    y_prompt = g("y")
    y_sample = g("ys").reshape(16, 16, 2048)
    k_win_p = g("kwin").reshape(1, 8, 128, 2, 64)
    v_win_p = g("vwin").reshape(1, 8, 128, 2, 64)
    conv_p = g("cwin").reshape(1, 8, 30, 1024)
    k_win_s = g("ksw").reshape(1, 16, 128, 2, 64)
    v_win_s = g("vsw").reshape(1, 16, 128, 2, 64)
    conv_s = g("csw").reshape(1, 16, 30, 1024)
    return (y_prompt, y_sample, k_win_p, v_win_p, conv_p, k_win_s, v_win_s, conv_s)
```

```python
from contextlib import ExitStack

import numpy as np
import concourse.bass as bass
import concourse.mybir as mybir
from concourse.bass_utils import run_bass_kernel_spmd

F32 = mybir.dt.float32
BF16 = mybir.dt.bfloat16
AF = mybir.ActivationFunctionType
ALU = mybir.AluOpType

D_MODEL = 2048
SEQ = 4096
TT = 512
EPS = 1e-6
SCALE = 0.125
O_A, O_B, O_GC, O_Q, O_K, O_V, O_GA, O_MC, O_MA = 0, 1024, 2048, 3072, 4096, 4224, 4352, 5376, 7424
IN_COLS = 9472
NW = 4
CW = 544


class Sched:
    ENGS = ("pe", "act", "dve", "pool", "sp")

    def __init__(self):
        self.ops = []
        self.lastw = {}
        self.readers = {}
        self.dma_count = {}
        self.last_dma = {}

    def add(self, eng, fn, r=(), w=(), dma=None):
        i = len(self.ops)
        raw = set()
        deps = set()
        for k in r:
            j = self.lastw.get(k)
            if j is not None:
                raw.add(j)
                deps.add(j)
        for k in w:
            j = self.lastw.get(k)
            if j is not None:
                deps.add(j)
            rd = self.readers.get(k)
            if rd:
                deps.update(rd.values())
        for k in w:
            self.lastw[k] = i
            self.readers[k] = {}
        for k in r:
            rd = self.readers.setdefault(k, {})
            rd[(eng if dma is None else ("dma", i))] = i
        dmaval = None
        if dma is not None:
            j = self.last_dma.get(dma)
            if j is not None:
                deps.add(j)
            self.last_dma[dma] = i
            self.dma_count[dma] = self.dma_count.get(dma, 0) + 1
            dmaval = 16 * self.dma_count[dma]
        self.ops.append([eng, fn, deps, raw, dma, dmaval, False, None])
        return i

    def plan(self):
        ops = self.ops
        for i, op in enumerate(ops):
            eng, fn, deps, raw, dma = op[0], op[1], op[2], op[3], op[4]
            keep = []
            for d in deps:
                o = ops[d]
                if o[4] is None and dma is None and o[0] == eng:
                    if eng == "pe":
                        continue
                keep.append(d)
                o[6] = True
            op[2] = sorted(keep)
        cnt = {e: 0 for e in self.ENGS}
        for op in ops:
            if op[4] is not None:
                op[7] = ("dma_" + op[4], op[5])
            elif op[6]:
                cnt[op[0]] += 1
                op[7] = ("eng_" + op[0], cnt[op[0]])
        self.final = {}
        for op in ops:
            if op[7] is not None:
                s, v = op[7]
                self.final[s] = max(self.final.get(s, 0), v)

    def run_engine(self, eng, e, sems, final_wait=False):
        known = {}
        for op in self.ops:
            if op[0] != eng:
                continue
            for d in op[2]:
                s, v = self.ops[d][7]
                if known.get(s, 0) >= v:
                    continue
                known[s] = v
                e.wait_ge(sems[s], v)
            ins = op[1](e)
            if op[7] is not None:
                s, v = op[7]
                ins.then_inc(sems[s], 16 if op[4] is not None else 1)
        if final_wait:
            for s, v in self.final.items():
                if known.get(s, 0) < v:
                    e.wait_ge(sems[s], v)

    def sem_names(self):
        return ["eng_" + e for e in self.ENGS] + ["dma_" + k for k in self.dma_count]


def _slope(h):
    return float(2.0 ** (-(h + 1) / 2.0))


def make_consts():
    c = np.zeros((128, CW), np.float32)
    c[:, 0:128] = np.eye(128, dtype=np.float32)
    c[:, 128:256] = 1.0 / 1024.0
    c[:, 256:320] = 1.0
    c[:, 384 + 64:512] = 1.0
    s = np.arange(128)[:, None].astype(np.float32)
    q = np.arange(128)[None, :].astype(np.float32)
    c[:, 512:528] = -(128.0 + q[:, :16] - s)
    c[:16, 528:544] = -np.abs(q[:, :16] - s[:16])
    return c


def _nd_tables():
    s = np.arange(128)[:, None].astype(np.float32)
    q = np.arange(128)[None, :].astype(np.float32)
    ndA = -(128.0 + q - s)
    ndA[:64, 64:] = -1e5
    ndB = -np.abs(q - s)
    ndB[64:, :64] = -1e5
    return ndA, ndB


def make_cbf():
    import ml_dtypes
    ndA_, ndB_ = _nd_tables()
    out = np.zeros((128, 256 + 4096), np.float32)
    out[:, 0:128] = ndA_.T
    out[:, 128:256] = ndB_.T
    eye = np.eye(128, dtype=np.float32)
    for g in range(2):
        for par in range(2):
            for j in range(4):
                v = np.float32(_slope(8 * g + 2 * j + par) / SCALE)
                hi = np.float32(v.astype(ml_dtypes.bfloat16))
                lo = np.float32(np.float32(v - hi).astype(ml_dtypes.bfloat16))
                base = 256 + ((g * 2 + par) * 2) * 512 + j * 128
                out[:, base:base + 128] = hi * eye
                out[:, base + 512:base + 640] = lo * eye
    return out


def build_nc(NT=8, SAMPLE=True, STOP=None, DBG=False):
    nc = bass.Bass("TRN2", target_bir_lowering=False)
    din = lambda n, sh: nc.dram_tensor(n, sh, F32, kind="ExternalInput").ap()
    dout = lambda n, sh: nc.dram_tensor(n, sh, F32, kind="ExternalOutput").ap()
    x = din("x", [SEQ, D_MODEL])
    xs = din("xs", [32, D_MODEL])
    ck = din("ck", [2, 128, 128])
    cv = din("cv", [2, 128, 128])
    sc = din("sc", [2, 30, 1024])
    norm_g = din("norm_g", [D_MODEL])
    w_in = din("w_in", [D_MODEL, IN_COLS])
    conv_w = din("conv_w", [31, 1024])
    conv_b = din("conv_b", [1024])
    ln_g = din("ln_g", [1024])
    ln_b = din("ln_b", [1024])
    w_pw = din("w_pw", [1024, D_MODEL])
    sink = din("sink", [16])
    w_o = din("w_o", [1024, D_MODEL])
    w_out = din("w_out", [D_MODEL, D_MODEL])
    final_g = din("final_g", [D_MODEL])
    consts = din("consts", [128, CW])
    cbfd = din("cbf", [128, 4352])
    y = dout("y", [SEQ, D_MODEL])
    ys = dout("ys", [32, D_MODEL])
    kwin = dout("kwin", [128, 128])
    vwin = dout("vwin", [128, 128])
    cwin = dout("cwin", [30, 1024])
    ksw = dout("ksw", [2, 128, 128])
    vsw = dout("vsw", [2, 128, 128])
    csw = dout("csw", [2, 30, 1024])
    if DBG:
        dbg = {n: dout("dbg_" + n, [128, 4096]) for n in ("ypre", "stat", "yb", "z", "attn_n", "sga", "qT", "mg0", "mg1", "kT0", "vpad")}

    units = []
    U = {}
    def addu(name, spec):
        U[name] = len(units)
        units.append(spec)
    for c in range(8):
        addu(("a", c), ("in", O_A + c * 128))
        addu(("b", c), ("in", O_B + c * 128))
    for c in range(8):
        addu(("q", c), ("in", O_Q + c * 128))
    for g in range(2):
        addu(("k", g), ("kdup", O_K + g * 64))
    addu(("v", 0), ("in", O_V))
    for c in range(8):
        addu(("ga", c), ("in", O_GA + c * 128))
    for c in range(8):
        addu(("gc", c), ("in", O_GC + c * 128))
    for j in range(16):
        addu(("pw", j), ("pw", j))
        addu(("mc", j), ("in", O_MC + j * 128))
        addu(("wo", j), ("wo", j))
        addu(("ma", j), ("in", O_MA + j * 128))
    for cb in range(4):
        for kq in range(4):
            addu(("out", cb, kq), ("out", cb, kq))
    NU = len(units)
    wsc = nc.dram_tensor("wsc", [NU, 128, 2048], BF16, kind="Internal").ap()

    S = Sched()
    KM = {"ckl": [("xa", 1)], "cvl": [("xa", 1)], ("ctk", 0): [("xa", 1)], ("ctk", 1): [("xa", 1)],
          "v34a": [("xa", 0)], "v34b": [("xa", 0)], "v34c": [("xa", 0)], "v34d": [("xa", 0)],
          "ysq": [("T", 2)], "mean": [("T", 3)], "rstdl": [("T", 4)], "tmpl": [("T", 5)], "sg": [("T", 6)],
          "dn": [("T", 2)], "o1": [("T", 3)], "junk": [("attn_n", g_, s_) for g_ in range(2) for s_ in (0, 16, 128, 256, 384)], "scl": [("xa", 0)]}
    for i_ in range(2):
        KM[("sbt", i_)] = [("T", i_)]
        KM[("Sb", i_)] = [("T", i_)]
        KM[("smc", i_)] = [("T", 4 + i_)]
        KM[("t1", i_)] = [("T", 6)]
        KM[("t2", i_)] = [("T", 7)]
        KM[("cout", i_)] = [("z", c_) for c_ in range(8)]
    for i_ in range(8):
        KM[("sga", i_)] = [("glu", i_), ("gluL", i_)]
        KM[("PT", i_)] = [("T", 4 + i_ // 2)]
        KM[("yb", i_)] = [("Y", 2 * i_), ("Y", 2 * i_ + 1)]
    for i_ in range(8):
        KM[("s_yb", i_)] = [("sY", 2 * i_), ("sY", 2 * i_ + 1)]
    for i_ in range(16):
        KM[("s_mg", i_)] = [("sY", i_)]
        KM[("mg", i_)] = [("Y", i_)]

    def km(keys):
        out = []
        for k in keys:
            for kk in KM.get(k, [k]):
                if kk not in out:
                    out.append(kk)
        return out

    def A(eng, fn, r=(), w=(), dma=None):
        return S.add(eng, fn, km(r), km(w), dma)
    es = ExitStack()
    with es:
        sb = lambda name, shape, dt: es.enter_context(nc.sbuf_tensor(name, shape, dt))
        cst = sb("cst", [128, CW], F32)
        ident = cst[:, 0:128]
        onesm = cst[:, 128:256]
        ndSc = cst[:, 512:528]
        ndSn = cst[:, 528:544]
        identbf = sb("identbf", [128, 128], BF16)
        cbf = sb("cbf_sb", [128, 4352], BF16)
        onespad = sb("onespad", [128, 2, 128], BF16)
        onesbf = sb("onesbf", [128, 128], BF16)
        ng16 = sb("ng16", [16, 128], F32)
        cw = sb("cw", [128, 8, 34], F32)
        normg = sb("normg", [128, 16], F32)
        gfin = sb("gfin", [128, D_MODEL], F32)
        sk = sb("sk", [128, 16], F32)
        esink = sb("esink", [128, 8], F32)
        Wt = [sb(f"W{i}", [128, 2048], BF16) for i in range(NW)]
        hTs = [sb(f"hT{i}", [128, 16, TT], BF16) for i in range(2)]
        cur = {"hs": 0}
        xa = [sb(f"xa{i}", [128, D_MODEL], F32) for i in range(2)]
        v34 = xa[0][0:34, 0:1024]
        xr = [sb(f"xr{i}", [128, D_MODEL], F32) for i in range(2)]
        st4 = [sb(f"st4_{i}", [128, 4], F32) for i in range(2)]
        st5 = [sb(f"st5_{i}", [128, 4], F32) for i in range(4)]
        glu = sb("glu", [128, 8, 30 + TT], BF16)
        gluS = sb("gluS", [128, 8, 2, 46], BF16)
        glu32 = sb("glu32", [128, 8, 32], F32)
        TP = [sb(f"tp{i}", [128, TT], F32) for i in range(8)]
        sbt = [TP[0], TP[1]]
        diag = [sb(f"diag{i}", [128, 31, 128], BF16) for i in range(2)]
        yraw = sb("yraw", [128, 8 * TT], F32)
        yb = yraw[:].rearrange("p (c n) -> p c n", c=8)
        merged = yraw[:].bitcast(BF16).rearrange("p (j n) -> p j n", j=16)
        ysq, mean, rstdl, tmpl, sg = TP[2], TP[3], TP[4], TP[5], TP[6]
        z = sb("z", [128, 8, TT], BF16)
        qT = sb("qT", [128, 8, TT], BF16)
        kT = [sb(f"kT{g}", [128, 128 + TT], BF16) for g in range(2)]
        kTs = [sb(f"kTs{g}", [128, 2, 144], BF16) for g in range(2)]
        vT = sb("vT", [128, TT], BF16)
        k32 = [sb(f"k32_{g}", [128, 128], F32) for g in range(2)]
        v32 = sb("v32", [128, 128], F32)
        vpad = sb("vpad", [128, 5, 2, 2, 128], BF16)
        vpadSc = sb("vpadSc", [128, 2, 2, 2, 128], BF16)
        vpadSn = sb("vpadSn", [16, 2, 2, 2, 128], BF16)
        sga = glu[:, :, 0:TT]
        ctx = sb("ctx", [128, 8, 30], BF16)
        Sb = [TP[0], TP[1]]
        PT = [TP[4 + i // 2][:].bitcast(BF16)[:, (i % 2) * TT:(i % 2 + 1) * TT] for i in range(8)]
        dn, o1 = TP[2], TP[3]
        attn_n = sb("attn_n", [128, 8, TT], BF16)
        junk = attn_n[:].rearrange("p a b -> p (a b)")[:, 0:D_MODEL]
        smc = [TP[4], TP[5]]
        T1P = [TP[6], TP[6]]
        T2P = [TP[7], TP[7]]
        kvo = sb("kvo", [128, 384], F32)
        cout = z[:].rearrange("p a b -> p (a b)").bitcast(F32)[0:32, 0:1024]
        ctk = xa[1][:, 0:256].rearrange("p (a b c) -> p a b c", a=2, b=2)
        ckl = xa[1][:, 256:384]
        cvl = xa[1][:, 384:512]
        scl = xa[0][0:30, 0:1024]
        hT_s = sb("hT_s", [128, 16, 32], BF16)
        t1s = sb("t1s", [128, 32], F32)
        lns = [sb(f"lns{i}", [128, 32], F32) for i in range(3)]
        t2s = sb("t2s", [128, 32], F32)
        yraw_s = sb("yraw_s", [128, 8 * 32], F32)
        SB = {"yb": yraw_s[:].rearrange("p (c n) -> p c n", c=8),
              "merged": yraw_s[:].bitcast(BF16).rearrange("p (j n) -> p j n", j=16),
              "z": sb("z_s", [128, 8, 32], BF16), "qT": sb("qT_s", [128, 8, 32], BF16), "sga": sb("sga_s", [128, 8, 32], BF16),
              "attn_n": sb("attn_n_s", [128, 8, 32], BF16), "glu32": sb("glu32_s", [128, 8, 32], F32),
              "k32": [sb(f"k32s_{g}", [128, 32], F32) for g in range(2)], "v32": sb("v32_s", [128, 32], F32)}
        PB = {"yb": yb, "merged": merged, "z": z, "qT": qT, "sga": sga, "attn_n": attn_n, "glu32": glu32, "k32": k32, "v32": v32}
        banks = [es.enter_context(nc.psum_tensor(f"pb{i}", [128, 512], F32)) for i in range(8)]

        st = {"nbank": 6, "diag": 0, "bank": 0, "w": 0, "xa": 0, "xr": 0, "sbt": 0, "smc": 0, "Sb": 0, "PT": 0}

        def nxt(name, n):
            v = st[name]
            st[name] = (v + 1) % n
            return v

        held = set()

        def bank():
            nb = st["nbank"]
            b = nxt("bank", nb)
            while b in held:
                b = nxt("bank", nb)
            return b, banks[b]

        A("sp", lambda e: e.dma_start(out=cst[:], in_=consts), w=["cst"], dma="c0")
        A("sp", lambda e: e.dma_start(out=v34[0:31, :], in_=conv_w), w=["v34a"], dma="c1")
        A("sp", lambda e: e.dma_start(out=v34[31:32, :], in_=conv_b.rearrange("(o n) -> o n", o=1)), w=["v34b"], dma="c2")
        A("sp", lambda e: e.dma_start(out=v34[32:33, :], in_=ln_g.rearrange("(o n) -> o n", o=1)), w=["v34c"], dma="c3")
        A("sp", lambda e: e.dma_start(out=v34[33:34, :], in_=ln_b.rearrange("(o n) -> o n", o=1)), w=["v34d"], dma="c4")
        A("sp", lambda e: e.dma_start(out=ng16[:], in_=norm_g.rearrange("(k p) -> k p", p=128)), w=["ng16"], dma="c5")
        A("sp", lambda e: e.dma_start(out=gfin[:], in_=final_g.partition_broadcast(128)), w=["gfin"], dma="c6")
        A("sp", lambda e: e.dma_start(out=sk[:], in_=sink.partition_broadcast(128)), w=["sk"], dma="c7")
        A("dve", lambda e: e.tensor_copy(out=identbf[:], in_=ident), r=["cst"], w=["identbf"])
        A("dve", lambda e: e.tensor_copy(out=onesbf[:], in_=onesm), r=["cst"], w=["onesbf"])
        A("dve", lambda e: e.tensor_copy(out=onespad[:].rearrange("p a b -> p (a b)"), in_=cst[:, 256:512]), r=["cst"], w=["onespad"])
        SETUP_LVL = {"S1": 1, "S2": 2, "S3": 3, "S4": 4}.get(STOP, 9)
        if SETUP_LVL >= 2:
          A("pool", lambda e: e.memset(vpad[:].rearrange("p a b c d -> p (a b c d)"), 0.0), w=["vpadz"])
          A("pool", lambda e: e.memset(vpadSc[:].rearrange("p a b c d -> p (a b c d)"), 0.0), w=["vpadScz"])
          A("pool", lambda e: e.memset(vpadSn[:].rearrange("p a b c d -> p (a b c d)"), 0.0), w=["vpadSnz"])
          A("pool", lambda e: e.memset(glu[:, :, 0:30], 0.0), w=[("glu", c) for c in range(8)])
          A("act", lambda e: e.activation(out=sk[:], in_=sk[:], func=AF.Exp), r=["sk"], w=["sk"])
          A("dve", lambda e: e.tensor_copy(out=esink[0:64, :], in_=sk[0:64, 0:16:2]), r=["sk"], w=["esinka"])
          A("dve", lambda e: e.tensor_copy(out=esink[64:128, :], in_=sk[64:128, 1:16:2]), r=["sk"], w=["esinkb"])
        b0, bk0 = bank()
        if SETUP_LVL >= 3:
          pass
        def _tp(e):
            ins = None
            for c in range(8):
                ins = e.transpose(out=bk0[:, c * 34:(c + 1) * 34], in_=v34[0:34, c * 128:(c + 1) * 128], identity=cst[0:34, 0:34])
            return ins
        A("pe", _tp, r=["cst", "v34a", "v34b", "v34c", "v34d"], w=[("B", b0)])
        A("dve", lambda e: e.tensor_copy(out=cw[:].rearrange("p a b -> p (a b)"), in_=bk0[:, 0:272]), r=[("B", b0)], w=["cw"])
        b1, bk1 = bank()
        A("pe", lambda e: e.transpose(out=bk1[:, 0:16], in_=ng16[:], identity=cst[0:16, 0:16]), r=["cst", "ng16"], w=[("B", b1)])
        A("dve", lambda e: e.tensor_copy(out=normg[:], in_=bk1[:, 0:16]), r=[("B", b1)], w=["normg"])

        if SETUP_LVL < 4:
            A("pool", lambda e: e.dma_start(out=cbf[:], in_=cbfd), w=["cbf"], dma="c8")
        for u, spec in enumerate(units if SETUP_LVL >= 4 else []):
            if u == 48:
                A("pool", lambda e: e.dma_start(out=cbf[:], in_=cbfd), w=["cbf"], dma="c8")
            dk = f"cv{u % 8}"
            if spec[0] == "in":
                c0 = spec[1]
                A("pool", lambda e, u=u, c0=c0: e.dma_start(
                    out=wsc[u].rearrange("p (kc j) -> p kc j", kc=16),
                    in_=w_in[:, c0:c0 + 128].rearrange("(kc p) j -> p kc j", p=128)), w=[("wsc", u)], dma=dk)
            elif spec[0] == "kdup":
                c0 = spec[1]
                A("pool", lambda e, u=u, c0=c0: e.dma_start(
                    out=wsc[u].rearrange("p (kc j) -> p kc j", kc=16)[:, :, 0:64],
                    in_=w_in[:, c0:c0 + 64].rearrange("(kc p) j -> p kc j", p=128)), w=[("wsc", u, 0)], dma=dk)
                A("pool", lambda e, u=u, c0=c0: e.dma_start(
                    out=wsc[u].rearrange("p (kc j) -> p kc j", kc=16)[:, :, 64:128],
                    in_=w_in[:, c0:c0 + 64].rearrange("(kc p) j -> p kc j", p=128)), w=[("wsc", u)], r=[("wsc", u, 0)], dma=dk)
            elif spec[0] in ("pw", "wo"):
                src = w_pw if spec[0] == "pw" else w_o
                j = spec[1]
                A("pool", lambda e, u=u, j=j, src=src: e.dma_start(
                    out=wsc[u][:, 0:1024].rearrange("p (kc j) -> p kc j", kc=8),
                    in_=src[:, j * 128:(j + 1) * 128].rearrange("(kc p) j -> p kc j", p=128)), w=[("wsc", u)], dma=dk)
            else:
                cb, kq = spec[1], spec[2]
                A("pool", lambda e, u=u, cb=cb, kq=kq: e.dma_start(
                    out=wsc[u].rearrange("p (kc j) -> p kc j", kc=4),
                    in_=w_out[kq * 512:(kq + 1) * 512, cb * 512:(cb + 1) * 512].rearrange("(kc p) j -> p kc j", p=128)),
                    w=[("wsc", u)], dma=dk)


        def load_w(name):
            if name[0] == "outS":
                name = ("out",) + tuple(name[1:])
            u = U[name]
            s = nxt("w", NW)
            ncol = 1024 if name[0] in ("pw", "wo") else 2048
            A("sp", lambda e, u=u, s=s, ncol=ncol: e.dma_start(out=Wt[s][:, 0:ncol], in_=wsc[u][:, 0:ncol]), r=[("wsc", u)], w=[("W", s)], dma=f"W{s}")
            return s

        def mm_in(name, N, cx):
            s = yield name
            b, bk = bank()
            wv = Wt[s][:].rearrange("p (kc j) -> p kc j", kc=16)
            hT = cx[1]
            def f(e, hT=hT):
                ins = None
                for kc in range(16):
                    ins = e.matmul(bk[:, 0:N], lhsT=wv[:, kc, :], rhs=hT[:, kc, 0:N], start=(kc == 0), stop=(kc == 15))
                return ins
            cx[0]("pe", f, r=[("W", s)] + cx[2], w=[("B", b)])
            return b, bk

        def mm_8(name, N, src, srckeys, cx):
            s = yield name
            b, bk = bank()
            wv = Wt[s][:, 0:1024].rearrange("p (kc j) -> p kc j", kc=8)
            def f(e):
                ins = None
                for kc in range(8):
                    ins = e.matmul(bk[:, 0:N], lhsT=wv[:, kc, :], rhs=src[:, kc, 0:N], start=(kc == 0), stop=(kc == 7))
                return ins
            cx[0]("pe", f, r=[("W", s)] + srckeys, w=[("B", b)])
            return b, bk

        def _mm_in(setcur, name, N):
            return (yield from mm_in(name, N, setcur()))

        def _mm_8(setcur, name, N, src, srckeys):
            return (yield from mm_8(name, N, src, srckeys, setcur()))

        def phaseA(xsrc, tbs, hs, part="both", which=None):
            for ti, (r0, rows) in enumerate(tbs):
                if which is not None and ti not in which:
                    continue
                s = ti % 2
                xt = xa[s]
                if part == "back":
                    phaseA_back(xt, s, r0, rows, hs)
                    continue
                sv = st4[s]
                A("sp", lambda e, s=s, r0=r0, rows=rows: e.dma_start(out=xa[s][0:rows, :], in_=xsrc[r0:r0 + rows, :]), w=[("xa", s)], dma=f"xa{s}")
                A("act", lambda e, xt=xt, sv=sv, rows=rows: e.activation(out=junk[0:rows, :], in_=xt[0:rows, :], func=AF.Square, accum_out=sv[0:rows, 0:1]),
                  r=[("xa", s)], w=["junk", ("st4", s, 0)])
                if STOP == "A1":
                    continue
                A("act", lambda e, sv=sv, rows=rows: e.activation(out=sv[0:rows, 1:2], in_=sv[0:rows, 0:1], func=AF.Sqrt, scale=1.0 / D_MODEL, bias=EPS),
                  r=[("st4", s, 0)], w=[("st4", s, 1)])
                A("dve", lambda e, sv=sv, rows=rows: e.reciprocal(out=sv[0:rows, 2:3], in_=sv[0:rows, 1:2]), r=[("st4", s, 1)], w=[("st4", s, 2)])
                if STOP == "A2":
                    continue
                A("act", lambda e, xt=xt, sv=sv, rows=rows: e.activation(out=xt[0:rows, :], in_=xt[0:rows, :], func=AF.Copy, scale=sv[0:rows, 2:3]),
                  r=[("xa", s), ("st4", s, 2)], w=[("xa", s)])
                if STOP == "A3":
                    continue
                if part == "both":
                    phaseA_back(xt, s, r0, rows, hs)

        def phaseA_back(xt, s, r0, rows, hs):
            hT = hT_s if hs == "s" else hTs[hs]
            hkey = (lambda kc: ("s_hT", 0, kc)) if hs == "s" else (lambda kc: ("hT", hs, kc))
            if True:
                for kq in range(4):
                    b, bk = bank()
                    def f(e, xt=xt, bk=bk, kq=kq, rows=rows):
                        ins = None
                        for kcl in range(4):
                            kc = kq * 4 + kcl
                            ins = e.transpose(out=bk[:, kcl * 128:kcl * 128 + rows], in_=xt[0:rows, kc * 128:(kc + 1) * 128], identity=cst[0:rows, 0:rows])
                        return ins
                    A("pe", f, r=[("xa", s), "cst"], w=[("B", b)])
                    if STOP == "A4":
                        continue
                    for kcl in range(4):
                        kc = kq * 4 + kcl
                        if kq % 2 == 0:
                            A("dve", lambda e, bk=bk, kc=kc, kcl=kcl, r0=r0, rows=rows, hT=hT: e.tensor_scalar(
                                out=hT[:, kc, r0:r0 + rows], in0=bk[:, kcl * 128:kcl * 128 + rows], scalar1=normg[:, kc:kc + 1], scalar2=None, op0=ALU.mult),
                              r=[("B", b), "normg"], w=[hkey(kc)])
                        else:
                            A("act", lambda e, bk=bk, kc=kc, kcl=kcl, r0=r0, rows=rows, hT=hT: e.activation(
                                out=hT[:, kc, r0:r0 + rows], in_=bk[:, kcl * 128:kcl * 128 + rows], func=AF.Copy, scale=normg[:, kc:kc + 1]),
                              r=[("B", b), "normg"], w=[hkey(kc)])

        A0 = A
        SKEYS = ("yb", "mg", "z", "qT", "sga", "attn_n", "glu32", "k32", "glu", "gluL", "hT", "t1", "t2")

        def tile_body(N, xsrc, ydst, tbs, first, last, sample, hooks=(None, None, None, None), hs=0):
            W32 = 32
            if sample:
                def pk(k):
                    if isinstance(k, tuple) and k[0] in SKEYS:
                        return ("s_" + k[0],) + tuple(k[1:])
                    if k in ("v32", "mean", "rstdl", "tmpl"):
                        return "s_" + k
                    return k
                def A(eng, fn, r=(), w=(), dma=None):
                    return A0(eng, fn, [pk(k) for k in r], [pk(k) for k in w], dma)
                yb, merged, z, qT, sga, attn_n, glu32, k32, v32 = SB["yb"], SB["merged"], SB["z"], SB["qT"], SB["sga"], SB["attn_n"], SB["glu32"], SB["k32"], SB["v32"]
                t1 = [t1s, t1s]
                t2 = [t2s, t2s]
                mean, rstdl, tmpl = lns[0], lns[1], lns[2]
                myhT = hT_s
                myHK = [("s_hT", 0, kc) for kc in range(16)]
            else:
                A = A0
                yb, merged, z, qT, sga, attn_n, glu32, k32, v32 = PB["yb"], PB["merged"], PB["z"], PB["qT"], PB["sga"], PB["attn_n"], PB["glu32"], PB["k32"], PB["v32"]
                t1 = T1P
                t2 = T2P
                mean, rstdl, tmpl = TP[3], TP[4], TP[5]
                myhT = hTs[hs]
                myHK = [("hT", hs, kc) for kc in range(16)]
            def setcur():
                return (A, myhT, myHK)
            PL = "dve" if first else "pool"

            def hook(i):
                if hooks[i] is not None:
                    hooks[i]()
            GK = [("glu", c) for c in range(8)]
            GL = [("gluL", c) for c in range(8)]
            if STOP in ('A', 'A1', 'A2', 'A3', 'A4', 'A5', 'A6'):
                return
            dgs = {}
            def build_diag(c):
                di = c % 2
                dg = diag[di]
                dgs[c] = (di, dg)
                def fd(e, c=c, dg=dg):
                    ins = None
                    for tap in range(0, 12):
                        ins = e.tensor_scalar(out=dg[:, tap, :], in0=identbf[:], scalar1=cw[:, c, tap:tap + 1], scalar2=0.0, op0=ALU.mult, op1=ALU.add)
                    return ins
                def fd_dve(e, c=c, dg=dg):
                    ins = None
                    for tap in range(0, 12):
                        ins = e.tensor_scalar(out=dg[:, tap, :], in0=identbf[:], scalar1=cw[:, c, tap:tap + 1], scalar2=None, op0=ALU.mult)
                    return ins
                if first:
                    A("dve", fd_dve, r=["identbf", "cw"], w=[("diag", di, 0)])
                else:
                    A("pool", fd, r=["identbf", "cw"], w=[("diag", di, 0)])
                def fd3(e, c=c, dg=dg):
                    ins = None
                    for tap in range(12, 27):
                        ins = e.tensor_scalar(out=dg[:, tap, :], in0=identbf[:], scalar1=cw[:, c, tap:tap + 1], scalar2=None, op0=ALU.mult)
                    return ins
                A("dve", fd3, r=["identbf", "cw"], w=[("diag", di, 2)])
                def fd2(e, c=c, dg=dg):
                    ins = None
                    for tap in range(27, 31):
                        ins = e.activation(out=dg[:, tap, :], in_=identbf[:], func=AF.Copy, scale=cw[:, c, tap:tap + 1])
                    return ins
                A("act", fd2, r=["identbf", "cw"], w=[("diag", di, 1)])
            if not sample:
                build_diag(0)
            if not sample and not first:
                A("pool", lambda e: e.tensor_copy(out=glu[:, :, 0:30], in_=ctx[:]), r=["ctx"], w=GL)
            if not sample and first:
                A("dve", lambda e: e.memset(glu[:, :, 0:30], 0.0), w=GL)
            for c in range(8):
                ba, bka = yield from _mm_in(setcur, ("a", c), N)
                held.add(ba)
                bb, bkb = yield from _mm_in(setcur, ("b", c), N)
                held.discard(ba)
                s = nxt("sbt", 2)
                A("act", lambda e, s=s, bkb=bkb: e.activation(out=sbt[s][:, 0:N], in_=bkb[:, 0:N], func=AF.Sigmoid), r=[("B", bb)], w=[("sbt", s)])
                if sample:
                    A("dve", lambda e, s=s, bka=bka, c=c: e.tensor_tensor(
                        out=gluS[:, c, :, 30:46], in0=bka[:, 0:32].rearrange("p (s t) -> p s t", s=2),
                        in1=sbt[s][:, 0:32].rearrange("p (s t) -> p s t", s=2), op=ALU.mult),
                      r=[("B", ba), ("sbt", s)], w=[("glu", c)])
                else:
                    A("dve", lambda e, s=s, bka=bka, c=c: e.tensor_tensor(out=glu[:, c, 30:30 + N], in0=bka[:, 0:N], in1=sbt[s][:, 0:N], op=ALU.mult),
                      r=[("B", ba), ("sbt", s)], w=[("glu", c)])
                if last or sample:
                    A("dve", lambda e, s=s, bka=bka, c=c: e.tensor_tensor(out=glu32[:, c, :], in0=bka[:, N - W32:N], in1=sbt[s][:, N - W32:N], op=ALU.mult),
                      r=[("B", ba), ("sbt", s)], w=[("glu32", c)])
            if STOP == 'B1':
                return
            if last or sample:
                c0 = 0 if sample else 2
                nr = 32 - c0
                bs = [bank(), bank()]
                for hh in range(2):
                    b, bk = bs[hh]
                    def f(e, bk=bk, hh=hh):
                        ins = None
                        for cc in range(4):
                            c = hh * 4 + cc
                            ins = e.transpose(out=bk[0:nr, cc * 128:(cc + 1) * 128], in_=glu32[:, c, c0:32], identity=ident)
                        return ins
                    A("pe", f, r=[("glu32", hh * 4 + cc) for cc in range(4)] + ["cst"], w=[("B", b)])
                    A("dve", lambda e, bk=bk, hh=hh: e.tensor_copy(out=cout[0:nr, hh * 512:(hh + 1) * 512], in_=bk[0:nr, :]), r=[("B", b)], w=[("cout", hh)])
                if sample:
                    for s_ in range(2):
                        A("act", lambda e, s_=s_: e.dma_start(out=csw[s_, 14:30, :], in_=cout[s_ * 16:(s_ + 1) * 16, :]), r=[("cout", 0), ("cout", 1)], dma=f"o{s_}")
                        A("act", lambda e, s_=s_: e.dma_start(out=csw[s_, 0:14, :], in_=sc[s_, 16:30, :]), dma=f"o{2 + s_}")
                else:
                    A("act", lambda e: e.dma_start(out=cwin, in_=cout[0:30, :]), r=[("cout", 0), ("cout", 1)], dma="o0")
            bsum, bksum = 6, banks[6]
            bsq, bksq = 7, banks[7]
            if sample:
                build_diag(0)
            prev_stat = None
            for c in range(8):
                if c < 7:
                    build_diag(c + 1)
                di, dg = dgs[c]
                b, bk = bank()
                def fc(e, c=c, bk=bk, dg=dg):
                    ins = None
                    for tap in range(31):
                        if sample:
                            rhs = gluS[:, c, :, tap:tap + 16]
                            o = bk[:, 0:32].rearrange("p (s t) -> p s t", s=2)
                        else:
                            rhs = glu[:, c, tap:tap + N]
                            o = bk[:, 0:N]
                        ins = e.matmul(o, lhsT=dg[:, tap, :], rhs=rhs, start=(tap == 0), stop=(tap == 30))
                    return ins
                A("pe", fc, r=[("diag", di, 0), ("diag", di, 1), ("diag", di, 2), ("glu", c), ("gluL", c)], w=[("B", b)])
                A("act", lambda e, c=c, bk=bk: e.activation(out=yb[:, c, 0:N], in_=bk[:, 0:N], func=AF.Identity, bias=cw[:, c, 31:32]),
                  r=[("B", b), "cw"], w=[("yb", c)])
                ysqb = (ysq if c % 2 == 0 else TP[0])[:].bitcast(BF16)
                ysk = "ysq" if c % 2 == 0 else ("T", 0)
                def fs(e, c=c, bk=bk, ysqb=ysqb):
                    e.activation(out=ysqb[:, 0:N], in_=bk[:, 0:N], func=AF.Square, bias=cw[:, c, 31:32])
                    return e.activation(out=ysqb[:, TT:TT + N], in_=bk[:, 0:N], func=AF.Identity, bias=cw[:, c, 31:32])
                A("act", fs, r=[("B", b), "cw"], w=[ysk])
                def fst(e, c=c, ysqb=ysqb):
                    e.matmul(bksum[:, 0:N], lhsT=onesbf[:], rhs=ysqb[:, TT:TT + N], start=(c == 0), stop=(c == 7))
                    return e.matmul(bksq[:, 0:N], lhsT=onesbf[:], rhs=ysqb[:, 0:N], start=(c == 0), stop=(c == 7))
                if prev_stat is not None:
                    A("pe", prev_stat[0], r=[prev_stat[1], "onesbf"], w=[("B", bsum), ("B", bsq)])
                prev_stat = (fst, ysk)
                if c == 2:
                    hook(0)
            A("pe", prev_stat[0], r=[prev_stat[1], "onesbf"], w=[("B", bsum), ("B", bsq)])
            if not sample and not last:
                A("dve", lambda e: e.tensor_copy(out=ctx[:], in_=glu[:, :, N:N + 30]), r=GK, w=["ctx"])
            A("act", lambda e: e.activation(out=mean[:, 0:N], in_=bksum[:, 0:N], func=AF.Copy), r=[("B", bsum)], w=["mean"])
            A("dve", lambda e: e.tensor_tensor(out=tmpl[:, 0:N], in0=mean[:, 0:N], in1=mean[:, 0:N], op=ALU.mult), r=["mean"], w=["tmpl"])
            A("dve", lambda e: e.tensor_tensor(out=tmpl[:, 0:N], in0=bksq[:, 0:N], in1=tmpl[:, 0:N], op=ALU.subtract), r=[("B", bsq), "tmpl"], w=["tmpl"])
            A("act", lambda e: e.activation(out=tmpl[:, 0:N], in_=tmpl[:, 0:N], func=AF.Sqrt, bias=EPS), r=["tmpl"], w=["tmpl"])
            A("dve", lambda e: e.reciprocal(out=rstdl[:, 0:N], in_=tmpl[:, 0:N]), r=["tmpl"], w=["rstdl"])
            if DBG and first:
                A("pool", lambda e: e.dma_start(out=dbg["ypre"], in_=yraw[:]), r=[("yb", c) for c in range(8)], dma="dbg0")
                A("pool", lambda e: e.dma_start(out=dbg["stat"][:, 0:512], in_=mean[:]), r=["mean"], dma="dbg0")
                A("pool", lambda e: e.dma_start(out=dbg["stat"][:, 512:1024], in_=rstdl[:]), r=["rstdl"], dma="dbg0")
                A("pool", lambda e: e.dma_start(out=dbg["stat"][:, 1024:1536], in_=tmpl[:]), r=["tmpl"], dma="dbg0")
            def ln_apply(c):
                en = PL if (c % 3 == 2 and not sample) else "dve"
                A(en, lambda e, c=c: e.tensor_tensor(out=yb[:, c, 0:N], in0=yb[:, c, 0:N], in1=mean[:, 0:N], op=ALU.subtract), r=[("yb", c), "mean"], w=[("yb", c)])
                A(en, lambda e, c=c: e.tensor_tensor(out=yb[:, c, 0:N], in0=yb[:, c, 0:N], in1=rstdl[:, 0:N], op=ALU.mult), r=[("yb", c), "rstdl"], w=[("yb", c)])
                A("act", lambda e, c=c: e.activation(out=yb[:, c, 0:N], in_=yb[:, c, 0:N], func=AF.Silu, scale=cw[:, c, 32:33], bias=cw[:, c, 33:34]),
                  r=[("yb", c), "cw"], w=[("yb", c)])
            if STOP == 'B2':
                return
            hook(1)
            if STOP == 'B3':
                return
            if DBG and first:
                A("pool", lambda e: e.dma_start(out=dbg["yb"], in_=yraw[:]), r=[("yb", c) for c in range(8)], dma="dbg0")
            for c in range(8):
                b, bk = yield from _mm_in(setcur, ("q", c), N)
                A("act", lambda e, c=c, bk=bk: e.activation(out=qT[:, c, 0:N], in_=bk[:, 0:N], func=AF.Copy), r=[("B", b)], w=[("qT", c)])
                ln_apply(c)
                if c == 2:
                    hook(3)
            for g in range(2):
                b, bk = yield from _mm_in(setcur, ("k", g), N)
                if sample:
                    A("dve", lambda e, g=g, bk=bk: e.tensor_copy(out=kTs[g][:, :, 128:144], in_=bk[:, 0:32].rearrange("p (s t) -> p s t", s=2)),
                      r=[("B", b)], w=[("kTn", g)])
                else:
                    A("dve", lambda e, g=g, bk=bk: e.tensor_copy(out=kT[g][:, 128:128 + N], in_=bk[:, 0:N]), r=[("B", b)], w=[("kT", g)])
                if last or sample:
                    nk = 32 if sample else 128
                    A("dve", lambda e, g=g, bk=bk, nk=nk: e.tensor_copy(out=k32[g][:, 0:nk], in_=bk[:, N - nk:N]), r=[("B", b)], w=[("k32", g)])
            b, bk = yield from _mm_in(setcur, ("v", 0), N)
            A("dve", lambda e, bk=bk: e.tensor_copy(out=vT[:, 0:N], in_=bk[:, 0:N]), r=[("B", b)], w=["vT"])
            if last or sample:
                nk = 32 if sample else 128
                A("dve", lambda e, bk=bk, nk=nk: e.tensor_copy(out=v32[:, 0:nk], in_=bk[:, N - nk:N]), r=[("B", b)], w=["v32"])
            bv, bkv = bank()
            pbf = bkv[:].bitcast(BF16)
            if sample:
                def fv(e):
                    ins = None
                    for s_ in range(2):
                        ins = e.transpose(out=pbf[0:16, s_ * 128:(s_ + 1) * 128], in_=vT[:, s_ * 16:(s_ + 1) * 16], identity=identbf[:])
                    return ins
                A("pe", fv, r=["vT", "identbf"], w=[("B", bv)])
                for g in range(2):
                    for par in range(2):
                        A("dve", lambda e, g=g, par=par: e.tensor_copy(
                            out=vpadSn[:, :, g, par, par * 64:(par + 1) * 64],
                            in_=pbf[0:16, 0:256].rearrange("p (s c) -> p s c", s=2)[:, :, g * 64:(g + 1) * 64]),
                          r=[("B", bv), "vpadSnz"], w=[("vpadSn", g, par)])
            else:
                nb = N // 128
                def fv(e):
                    ins = None
                    for blk in range(nb):
                        ins = e.transpose(out=pbf[:, blk * 128:(blk + 1) * 128], in_=vT[:, blk * 128:(blk + 1) * 128], identity=identbf[:])
                    return ins
                A("pe", fv, r=["vT", "identbf"], w=[("B", bv)])
                for g in range(2):
                    for par in range(2):
                        A("dve", lambda e, g=g, par=par: e.tensor_copy(
                            out=vpad[:, 1:1 + nb, g, par, par * 64:(par + 1) * 64],
                            in_=pbf[:, 0:nb * 128].rearrange("p (s c) -> p s c", s=nb)[:, :, g * 64:(g + 1) * 64]),
                          r=[("B", bv), "vpadz"], w=[("vpad", g, par)])
            if last or sample:
                nk = 32 if sample else 128
                bo, bko = bank()
                def fo(e):
                    e.transpose(out=bko[0:nk, 0:128], in_=k32[0][:, 0:nk], identity=ident)
                    e.transpose(out=bko[0:nk, 128:256], in_=k32[1][:, 0:nk], identity=ident)
                    return e.transpose(out=bko[0:nk, 256:384], in_=v32[:, 0:nk], identity=ident)
                A("pe", fo, r=[("k32", 0), ("k32", 1), "v32", "cst"], w=[("B", bo)])
                A("dve", lambda e: e.tensor_copy(out=kvo[0:nk, 0:64], in_=bko[0:nk, 0:64]), r=[("B", bo)], w=["kvo0"])
                A("dve", lambda e: e.tensor_copy(out=kvo[0:nk, 64:128], in_=bko[0:nk, 192:256]), r=[("B", bo)], w=["kvo1"])
                A("dve", lambda e: e.tensor_copy(out=kvo[0:nk, 128:256], in_=bko[0:nk, 256:384]), r=[("B", bo)], w=["kvo2"])
                KV = ["kvo0", "kvo1", "kvo2"]
                if sample:
                    for s_ in range(2):
                        A("act", lambda e, s_=s_: e.dma_start(out=ksw[s_, 112:128, :], in_=kvo[s_ * 16:(s_ + 1) * 16, 0:128]), r=KV, dma=f"o{4 + s_}")
                        A("act", lambda e, s_=s_: e.dma_start(out=vsw[s_, 112:128, :], in_=kvo[s_ * 16:(s_ + 1) * 16, 128:256]), r=KV, dma=f"o{6 + s_}")
                        A("act", lambda e, s_=s_: e.dma_start(out=ksw[s_, 0:112, :], in_=ck[s_, 16:128, :]), dma=f"o{8 + s_}")
                        A("act", lambda e, s_=s_: e.dma_start(out=vsw[s_, 0:112, :], in_=cv[s_, 16:128, :]), dma=f"o{10 + s_}")
                else:
                    A("act", lambda e: e.dma_start(out=kwin, in_=kvo[:, 0:128]), r=KV, dma="o1")
                    A("act", lambda e: e.dma_start(out=vwin, in_=kvo[:, 128:256]), r=KV, dma="o2")
            for c in range(8):
                b, bk = yield from _mm_in(setcur, ("ga", c), N)
                A("act", lambda e, c=c, bk=bk: e.activation(out=sga[:, c, 0:N], in_=bk[:, 0:N], func=AF.Silu), r=[("B", b)], w=[("sga", c)])

            if STOP == 'C1':
                return
            for c in range(8):
                b, bk = yield from _mm_in(setcur, ("gc", c), N)
                sgt, sgk = (TP[6], ("T", 6)) if c % 2 == 0 else (TP[7], ("T", 7))
                A("act", lambda e, bk=bk, sgt=sgt: e.activation(out=sgt[:, 0:N], in_=bk[:, 0:N], func=AF.Silu), r=[("B", b)], w=[sgk])
                A("dve", lambda e, c=c, sgt=sgt: e.tensor_tensor(out=z[:, c, 0:N], in0=yb[:, c, 0:N], in1=sgt[:, 0:N], op=ALU.mult), r=[("yb", c), sgk], w=[("z", c)])
            QK = [("qT", c) for c in range(8)]
            hook(2)
            st["nbank"] = 8

            pend = []

            def attn(nq, qsl, ktiles, outsl, extra_r, pe_bias=False):
                for g in range(2):
                    pts = attn_scores(nq, qsl, ktiles, extra_r, pe_bias, g)
                    if pend:
                        attn_pv(*pend.pop())
                    pend.append((nq, outsl, extra_r, g, pts))

            def attn_flush():
                if pend:
                    attn_pv(*pend.pop())

            def attn_scores(nq, qsl, ktiles, extra_r, pe_bias, g):
                if True:
                    pts = []
                    for (nk, kfn, vfn, nd, rk) in ktiles:
                        for par in range(2):
                            b, bk = bank()
                            rows = slice(par * 64, (par + 1) * 64)
                            if pe_bias:
                                def fsc(e, bk=bk, nk=nk, kfn=kfn, g=g, rows=rows, nd=nd, par=par):
                                    e.matmul(bk[0:nk, 0:4 * nq].rearrange("p (j q) -> p j q", j=4),
                                             lhsT=kfn(g)[rows, :], rhs=qT[rows, 4 * g:4 * g + 4, qsl], start=True, stop=False)
                                    rb = 256 + ((g * 2 + par) * 2) * 512
                                    e.matmul(bk[0:nk, 0:512], lhsT=nd, rhs=cbf[:, rb:rb + 512], start=False, stop=False)
                                    return e.matmul(bk[0:nk, 0:512], lhsT=nd, rhs=cbf[:, rb + 512:rb + 1024], start=False, stop=True)
                                A("pe", fsc, r=QK + rk + extra_r + ["cbf"], w=[("B", b)])
                                pi = nxt("PT", 8)
                                A("act", lambda e, bk=bk, pi=pi, nk=nk: e.activation(out=PT[pi][0:nk, 0:4 * nq], in_=bk[0:nk, 0:4 * nq], func=AF.Exp, scale=SCALE),
                                  r=[("B", b)], w=[("PT", pi)])
                                pts.append((pi, nk, vfn, par, rk))
                                continue
                            A("pe", lambda e, bk=bk, nk=nk, kfn=kfn, g=g, rows=rows: e.matmul(
                                bk[0:nk, 0:4 * nq].rearrange("p (j q) -> p j q", j=4),
                                lhsT=kfn(g)[rows, :], rhs=qT[rows, 4 * g:4 * g + 4, qsl], start=True, stop=True),
                              r=QK + rk + extra_r, w=[("B", b)])
                            si = nxt("Sb", 2)
                            def fb(e, bk=bk, nk=nk, nd=nd, si=si, g=g, par=par):
                                ins = None
                                for j in range(4):
                                    h = 8 * g + 2 * j + par
                                    ins = e.scalar_tensor_tensor(out=Sb[si][0:nk, j * nq:(j + 1) * nq], in0=nd[0:nk, 0:nq], scalar=_slope(h) / SCALE,
                                                                 op0=ALU.mult, in1=bk[0:nk, j * nq:(j + 1) * nq], op1=ALU.add)
                                return ins
                            A("dve", fb, r=[("B", b), "cst"], w=[("Sb", si)])
                            pi = nxt("PT", 8)
                            A("act", lambda e, si=si, pi=pi, nk=nk: e.activation(out=PT[pi][0:nk, 0:4 * nq], in_=Sb[si][0:nk, 0:4 * nq], func=AF.Exp, scale=SCALE),
                              r=[("Sb", si)], w=[("PT", pi)])
                            pts.append((pi, nk, vfn, par, rk))
                return pts

            def attn_pv(nq, outsl, extra_r, g, pts):
                if True:
                    bacc, bkacc = bank()
                    bden, bkden = bank()
                    def fpv(e, pts=pts, g=g, bkacc=bkacc):
                        ins = None
                        for i, (pi, nk, vfn, par, rk) in enumerate(pts):
                            ins = e.matmul(bkacc[:, 0:4 * nq], lhsT=vfn(g, par), rhs=PT[pi][0:nk, 0:4 * nq], start=(i == 0), stop=(i == len(pts) - 1))
                        return ins
                    def fden(e, pts=pts, bkden=bkden):
                        ins = None
                        for i, (pi, nk, vfn, par, rk) in enumerate(pts):
                            ins = e.matmul(bkden[:, 0:4 * nq], lhsT=onespad[0:nk, par, :], rhs=PT[pi][0:nk, 0:4 * nq], start=(i == 0), stop=(i == len(pts) - 1))
                        return ins
                    prk = [("PT", p[0]) for p in pts] + sum([p[4] for p in pts], []) + extra_r
                    A("pe", fpv, r=prk, w=[("B", bacc)])
                    A("pe", fden, r=prk + ["onespad"], w=[("B", bden)])
                    def fdn(e, g=g, bkden=bkden):
                        ins = None
                        for j in range(4):
                            ins = e.tensor_scalar(out=dn[:, j * nq:(j + 1) * nq], in0=bkden[:, j * nq:(j + 1) * nq], scalar1=esink[:, 4 * g + j:4 * g + j + 1], scalar2=None, op0=ALU.add)
                        return ins
                    A("dve", fdn, r=[("B", bden), "esinka", "esinkb"], w=["dn"])
                    A("dve", lambda e: e.reciprocal(out=dn[:, 0:4 * nq], in_=dn[:, 0:4 * nq]), r=["dn"], w=["dn"])
                    A("dve", lambda e, bkacc=bkacc: e.tensor_tensor(out=o1[:, 0:4 * nq], in0=bkacc[:, 0:4 * nq], in1=dn[:, 0:4 * nq], op=ALU.mult), r=[("B", bacc), "dn"], w=["o1"])
                    A(PL, lambda e, g=g: e.tensor_tensor(out=attn_n[:, 4 * g:4 * g + 4, outsl], in0=o1[:, 0:4 * nq].rearrange("p (j q) -> p j q", j=4),
                                                           in1=sga[:, 4 * g:4 * g + 4, outsl], op=ALU.mult),
                      r=["o1"] + [("sga", c) for c in range(4 * g, 4 * g + 4)], w=[("attn_n", g, outsl.start)])

            VK = [("vpad", g, par) for g in range(2) for par in range(2)]
            if sample:
                for s_ in range(2):
                    kt = [
                        (128, lambda g, s_=s_: kTs[g][:, s_, 0:128], lambda g, par, s_=s_: vpadSc[:, s_, g, par, :], ndSc,
                         [("kTc", 0, s_), ("kTc", 1, s_)] + [("vpadSc", s_)]),
                        (16, lambda g, s_=s_: kTs[g][:, s_, 128:144], lambda g, par, s_=s_: vpadSn[:, s_, g, par, :], ndSn,
                         [("kTn", 0), ("kTn", 1)] + [("vpadSn", g, par) for g in range(2) for par in range(2)]),
                    ]
                    attn(16, slice(s_ * 16, (s_ + 1) * 16), kt, slice(s_ * 16, (s_ + 1) * 16), [])
                attn_flush()
            else:
                for i in range(N // 128):
                    kt = []
                    if not (first and i == 0):
                        kt.append((128, lambda g, i=i: kT[g][:, i * 128:(i + 1) * 128], lambda g, par, i=i: vpad[:, i, g, par, :], cbf[:, 0:128],
                                   [("kT", 0), ("kT", 1)] + VK))
                    kt.append((128, lambda g, i=i: kT[g][:, 128 + i * 128:128 + (i + 1) * 128], lambda g, par, i=i: vpad[:, i + 1, g, par, :], cbf[:, 128:256],
                               [("kT", 0), ("kT", 1)] + VK))
                    attn(128, slice(i * 128, (i + 1) * 128), kt, slice(i * 128, (i + 1) * 128), [], pe_bias=True)
                attn_flush()
                if not last:
                    for g in range(2):
                        A(PL, lambda e, g=g: e.tensor_copy(out=kT[g][:, 0:128], in_=kT[g][:, N:N + 128]), r=[("kT", g)], w=[("kT", g)])
                    A(PL, lambda e: e.tensor_copy(out=vpad[:, 0], in_=vpad[:, N // 128]), r=VK, w=VK)
            st["nbank"] = 6
            st["bank"] = st["bank"] % 6
            AK = [("attn_n", g, i * (16 if sample else 128)) for g in range(2) for i in range(2 if sample else N // 128)]

            if STOP == 'C2':
                return
            ZK = [("z", c) for c in range(8)]
            xr4 = [xr[0], xr[1], xa[0], xa[1]]
            xk4 = [("xr", 0), ("xr", 1), ("xa", 0), ("xa", 1)]
            if not sample:
                for ti, (r0, rows) in enumerate(tbs):
                    A("sp", lambda e, s=ti, r0=r0, rows=rows: e.dma_start(out=xr4[s][0:rows, :], in_=xsrc[r0:r0 + rows, :]), w=[xk4[ti]], dma=f"xr{ti}")
            for j in range(16):
                bA, bkA = yield from _mm_8(setcur, ("pw", j), N, z, ZK)
                held.add(bA)
                bB, bkB = yield from _mm_in(setcur, ("mc", j), N)
                held.discard(bA)
                s = nxt("smc", 2)
                A("act", lambda e, s=s, bkB=bkB: e.activation(out=smc[s][:, 0:N], in_=bkB[:, 0:N], func=AF.Sigmoid), r=[("B", bB)], w=[("smc", s)])
                A("dve", lambda e, s=s, bkA=bkA: e.tensor_tensor(out=t1[s][:, 0:N], in0=bkA[:, 0:N], in1=smc[s][:, 0:N], op=ALU.mult), r=[("B", bA), ("smc", s)], w=[("t1", s)])
                bC, bkC = yield from _mm_8(setcur, ("wo", j), N, attn_n, AK)
                held.add(bC)
                bD, bkD = yield from _mm_in(setcur, ("ma", j), N)
                held.discard(bC)
                A("act", lambda e, s=s, bkD=bkD: e.activation(out=smc[s][:, 0:N], in_=bkD[:, 0:N], func=AF.Sigmoid), r=[("B", bD), ("t1", s)], w=[("smc", s)])
                A("dve", lambda e, s=s, bkC=bkC: e.tensor_tensor(out=t2[s][:, 0:N], in0=bkC[:, 0:N], in1=smc[s][:, 0:N], op=ALU.mult), r=[("B", bC), ("smc", s)], w=[("t2", s)])
                A(PL, lambda e, s=s, j=j: e.tensor_tensor(out=merged[:, j, 0:N], in0=t1[s][:, 0:N], in1=t2[s][:, 0:N], op=ALU.add), r=[("t1", s), ("t2", s)], w=[("mg", j)])
            MK = [("mg", j) for j in range(16)]

            if STOP == 'M':
                return
            if DBG and first:
                fl = lambda t: t[:].rearrange("p a b -> p (a b)")
                A("pool", lambda e: e.dma_start(out=dbg["z"], in_=fl(z)), r=ZK, dma="dbg1")
                A("pool", lambda e: e.dma_start(out=dbg["attn_n"], in_=fl(attn_n)), r=AK, dma="dbg2")
                pass
                A("pool", lambda e: e.dma_start(out=dbg["qT"], in_=fl(qT)), r=QK, dma="dbg4")
                A("pool", lambda e: e.dma_start(out=dbg["mg0"], in_=merged[:, 0:8, :].rearrange("p a b -> p (a b)")), r=MK, dma="dbg5")
                A("pool", lambda e: e.dma_start(out=dbg["mg1"], in_=merged[:, 8:16, :].rearrange("p a b -> p (a b)")), r=MK, dma="dbg6")
                A("pool", lambda e: e.dma_start(out=dbg["kT0"][:, 0:640], in_=kT[0][:]), r=[("kT", 0)], dma="dbg7")
                A("pool", lambda e: e.dma_start(out=dbg["vpad"][:, 0:2560], in_=vpad[:].rearrange("p a b c d -> p (a b c d)")), r=VK, dma="dbg8")
            if sample:
                yield ("barrier",)
            halves = [tbs]
            st["nbank"] = 8
            for half in halves:
                slots = []
                for ti, (r0, rows) in enumerate(half):
                    s = ti
                    slots.append(s)
                    if sample:
                        A("sp", lambda e, s=s, r0=r0, rows=rows: e.dma_start(out=xr4[s][0:rows, :], in_=xsrc[r0:r0 + rows, :]), w=[xk4[s]], dma=f"xr{s}")
                for cb in range(4):
                    pbs = [bank() for _ in half]
                    for kq in range(4):
                        ws = yield (("outS", cb, kq) if sample else ("out", cb, kq))
                        wv = Wt[ws][:].rearrange("p (kc j) -> p kc j", kc=4)
                        def f(e, wv=wv, pbs=pbs, kq=kq, half=half):
                            ins = None
                            for (r0, rows), (b, bk) in zip(half, pbs):
                                for kcl in range(4):
                                    ins = e.matmul(bk[0:rows, :], lhsT=merged[:, kq * 4 + kcl, r0:r0 + rows], rhs=wv[:, kcl, :],
                                                   start=(kq == 0 and kcl == 0), stop=(kq == 3 and kcl == 3))
                            return ins
                        A("pe", f, r=[("W", ws)] + MK, w=[("B", b) for (b, bk) in pbs])
                    for (r0, rows), (b, bk), s in zip(half, pbs, slots):
                        A("dve", lambda e, bk=bk, s=s, rows=rows, cb=cb: e.tensor_tensor(
                            out=xr4[s][0:rows, cb * 512:(cb + 1) * 512], in0=bk[0:rows, :], in1=xr4[s][0:rows, cb * 512:(cb + 1) * 512], op=ALU.add),
                          r=[("B", b), xk4[s]], w=[xk4[s]])
                for (r0, rows), s in zip(half, slots):
                    sv = st5[s]
                    A("act", lambda e, s=s, sv=sv, rows=rows: e.activation(out=junk[0:rows, :], in_=xr4[s][0:rows, :], func=AF.Square, accum_out=sv[0:rows, 3:4]),
                      r=[xk4[s]], w=["junk", ("st4", s, 3)])
                    A("act", lambda e, sv=sv, rows=rows: e.activation(out=sv[0:rows, 3:4], in_=sv[0:rows, 3:4], func=AF.Sqrt, scale=1.0 / D_MODEL, bias=EPS),
                      r=[("st4", s, 3)], w=[("st4", s, 3)])
                    A("dve", lambda e, sv=sv, rows=rows: e.reciprocal(out=sv[0:rows, 3:4], in_=sv[0:rows, 3:4]), r=[("st4", s, 3)], w=[("st4", s, 3)])
                    if last or sample or first or s >= 2:
                        A("dve", lambda e, s=s, sv=sv, rows=rows: e.scalar_tensor_tensor(out=xr4[s][0:rows, :], in0=xr4[s][0:rows, :], scalar=sv[0:rows, 3:4], op0=ALU.mult,
                                                                                        in1=gfin[0:rows, :], op1=ALU.mult),
                          r=[xk4[s], ("st4", s, 3), "gfin"], w=[xk4[s]])
                    else:
                        A("act", lambda e, s=s, sv=sv, rows=rows: e.activation(out=xr4[s][0:rows, :], in_=xr4[s][0:rows, :], func=AF.Copy, scale=sv[0:rows, 3:4]),
                          r=[xk4[s], ("st4", s, 3)], w=[xk4[s]])
                        A("pool", lambda e, s=s, rows=rows: e.tensor_tensor(out=xr4[s][0:rows, :], in0=xr4[s][0:rows, :], in1=gfin[0:rows, :], op=ALU.mult),
                          r=[xk4[s], "gfin"], w=[xk4[s]])
                    yq = "act" if (last or sample or first) else "pool"
                    A(yq, lambda e, s=s, r0=r0, rows=rows: e.dma_start(out=ydst[r0:r0 + rows, :], in_=xr4[s][0:rows, :]), r=[xk4[s]], dma=(f"yp{s}" if yq == "pool" else f"y{s}"))
            st["nbank"] = 6
            st["bank"] = st["bank"] % 6

        def sample_prep():
            for s_ in range(2):
                A("sp", lambda e, s_=s_: e.dma_start(out=ckl, in_=ck[s_]), w=["ckl"], dma="ckl")
                A("sp", lambda e, s_=s_: e.dma_start(out=cvl, in_=cv[s_]), w=["cvl"], dma="cvl")
                A("sp", lambda e, s_=s_: e.dma_start(out=scl, in_=sc[s_]), w=["scl"], dma="scl")
                for d in range(2):
                    A("dve", lambda e, d=d: e.tensor_copy(out=ctk[:, :, d, :], in_=ckl.rearrange("p (g c) -> p g c", g=2)), r=["ckl"], w=[("ctk", d)])
                for g in range(2):
                    b, bk = bank()
                    A("pe", lambda e, g=g, bk=bk: e.transpose(out=bk[:, 0:128], in_=ctk[:, g].rearrange("p a b -> p (a b)"), identity=ident),
                      r=[("ctk", 0), ("ctk", 1), "cst"], w=[("B", b)])
                    A("dve", lambda e, g=g, bk=bk, s_=s_: e.tensor_copy(out=kTs[g][:, s_, 0:128], in_=bk[:, 0:128]), r=[("B", b)], w=[("kTc", g, s_)])
                for g in range(2):
                    for par in range(2):
                        A("dve", lambda e, g=g, par=par, s_=s_: e.tensor_copy(out=vpadSc[:, s_, g, par, par * 64:(par + 1) * 64], in_=cvl[:, g * 64:(g + 1) * 64]),
                          r=["cvl", "vpadScz"], w=[("vpadSc", s_)])
                for hh in range(2):
                    b, bk = bank()
                    def f(e, bk=bk, hh=hh):
                        ins = None
                        for cc in range(4):
                            c = hh * 4 + cc
                            ins = e.transpose(out=bk[:, cc * 32:cc * 32 + 30], in_=scl[0:30, c * 128:(c + 1) * 128], identity=cst[0:30, 0:30])
                        return ins
                    A("pe", f, r=["scl", "cst"], w=[("B", b)])
                    A("dve", lambda e, bk=bk, hh=hh, s_=s_: e.tensor_copy(out=gluS[:, hh * 4:hh * 4 + 4, s_, 0:30],
                                                                        in_=bk[:, 0:128].rearrange("p (c t) -> p c t", c=4)[:, :, 0:30]),
                      r=[("B", b)], w=[("s_glu", hh * 4 + cc) for cc in range(4)])

        TB4 = [(i * 128, 128) for i in range(4)]

        def drive(gens):
            reqs = {}
            active = []
            for i, g in enumerate(gens):
                try:
                    reqs[i] = next(g)
                    active.append(i)
                except StopIteration:
                    pass
            while active:
                cand = [j for j in active if reqs[j] != ("barrier",)]
                if not cand:
                    j = active[0]
                    try:
                        reqs[j] = gens[j].send(None)
                    except StopIteration:
                        active.remove(j)
                    continue
                name = reqs[cand[0]]
                slot = load_w(name)
                for j in [j for j in cand if reqs[j] == name]:
                    try:
                        reqs[j] = gens[j].send(slot)
                    except StopIteration:
                        active.remove(j)

        if SETUP_LVL >= 9:
            def hooks_for(t, hs):
                xsrc = x[t * TT:(t + 1) * TT, :]
                return (lambda: phaseA(xsrc, TB4, hs, "front", (0, 1)),
                        lambda: phaseA(xsrc, TB4, hs, "back", (0, 1)),
                        lambda: phaseA(xsrc, TB4, hs, "back", (2, 3)),
                        lambda: phaseA(xsrc, TB4, hs, "front", (2, 3)))
            if SAMPLE:
                sample_prep()
                phaseA(xs, [(0, 32)], "s")
            if NT > 0:
                phaseA(x[0:TT, :], TB4, 0)
            for t in range(NT):
                hs = t % 2
                hk = hooks_for(t + 1, 1 - hs) if t + 1 < NT else (None, None, None, None)
                gens = [tile_body(TT, x[t * TT:(t + 1) * TT, :], y[t * TT:(t + 1) * TT, :], TB4,
                                  first=(t == 0), last=(t == NT - 1), sample=False, hooks=hk, hs=hs)]
                if t == NT - 1 and SAMPLE:
                    gens.append(tile_body(32, xs, ys, [(0, 32)], first=False, last=False, sample=True))
                drive(gens)
            if SAMPLE and NT == 0:
                drive([tile_body(32, xs, ys, [(0, 32)], first=False, last=False, sample=True)])

        S.plan()
        sems = {n: es.enter_context(nc.semaphore(n)) for n in S.sem_names()}
        with nc.Block() as block:
            @block.tensor
            def _(e):
                S.run_engine("pe", e, sems)

            @block.scalar
            def _(e):
                S.run_engine("act", e, sems)

            @block.vector
            def _(e):
                S.run_engine("dve", e, sems)

            @block.gpsimd
            def _(e):
                S.run_engine("pool", e, sems)

            @block.sync
            def _(e):
                S.run_engine("sp", e, sems, final_wait=True)
    return nc


_NC_CACHE = {}


def kernel(x_prompt, x_sample, cache_k, cache_v, state_conv, norm_g, w_in, conv_w, conv_b, ln_g, ln_b,
           w_conv_pw, attn_sink, w_o_attn, w_out, final_g):
    f = lambda a: np.ascontiguousarray(np.asarray(a, dtype=np.float32))
    x_prompt, x_sample, cache_k, cache_v, state_conv = map(f, (x_prompt, x_sample, cache_k, cache_v, state_conv))
    shared = {
        "norm_g": f(norm_g).reshape(2048), "w_in": f(w_in).reshape(2048, IN_COLS), "conv_w": f(conv_w).reshape(31, 1024),
        "conv_b": f(conv_b).reshape(1024), "ln_g": f(ln_g).reshape(1024), "ln_b": f(ln_b).reshape(1024),
        "w_pw": f(w_conv_pw).reshape(1024, 2048), "sink": f(attn_sink).reshape(16), "w_o": f(w_o_attn).reshape(1024, 2048),
        "w_out": f(w_out).reshape(2048, 2048), "final_g": f(final_g).reshape(2048), "consts": make_consts(), "cbf": make_cbf(),
    }
    n = 8
    in_maps = []
    for c in range(n):
        m = dict(shared)
        m["x"] = x_prompt[c]
        m["xs"] = x_sample[2 * c:2 * c + 2].reshape(32, 2048)
        m["ck"] = cache_k[0, 2 * c:2 * c + 2].reshape(2, 128, 128)
        m["cv"] = cache_v[0, 2 * c:2 * c + 2].reshape(2, 128, 128)
        m["sc"] = state_conv[0, 2 * c:2 * c + 2]
        in_maps.append(m)
    if "nc" not in _NC_CACHE:
        _NC_CACHE["nc"] = build_nc(8, True)
    res = run_bass_kernel_spmd(_NC_CACHE["nc"], in_maps, core_ids=list(range(n)))
    R = res.results
    g = lambda k: np.stack([np.asarray(R[c][k], dtype=np.float32) for c in range(n)])
    y_prompt = g("y")
    y_sample = g("ys").reshape(16, 16, 2048)
    k_win_p = g("kwin").reshape(1, 8, 128, 2, 64)
    v_win_p = g("vwin").reshape(1, 8, 128, 2, 64)
    conv_p = g("cwin").reshape(1, 8, 30, 1024)
    k_win_s = g("ksw").reshape(1, 16, 128, 2, 64)
    v_win_s = g("vsw").reshape(1, 16, 128, 2, 64)
    conv_s = g("csw").reshape(1, 16, 30, 1024)
    return (y_prompt, y_sample, k_win_p, v_win_p, conv_p, k_win_s, v_win_s, conv_s)
```

```python
from contextlib import ExitStack

import numpy as np
import concourse.bass as bass
import concourse.mybir as mybir
from concourse.bass_utils import run_bass_kernel_spmd

F32 = mybir.dt.float32
BF16 = mybir.dt.bfloat16
AF = mybir.ActivationFunctionType
ALU = mybir.AluOpType

D_MODEL = 2048
SEQ = 4096
TT = 512
EPS = 1e-6
SCALE = 0.125
O_A, O_B, O_GC, O_Q, O_K, O_V, O_GA, O_MC, O_MA = 0, 1024, 2048, 3072, 4096, 4224, 4352, 5376, 7424
IN_COLS = 9472
NW = 4
CW = 544


class Sched:
    ENGS = ("pe", "act", "dve", "pool", "sp")

    def __init__(self):
        self.ops = []
        self.lastw = {}
        self.readers = {}
        self.dma_count = {}
        self.last_dma = {}

    def add(self, eng, fn, r=(), w=(), dma=None):
        i = len(self.ops)
        raw = set()
        deps = set()
        for k in r:
            j = self.lastw.get(k)
            if j is not None:
                raw.add(j)
                deps.add(j)
        for k in w:
            j = self.lastw.get(k)
            if j is not None:
                deps.add(j)
            rd = self.readers.get(k)
            if rd:
                deps.update(rd.values())
        for k in w:
            self.lastw[k] = i
            self.readers[k] = {}
        for k in r:
            rd = self.readers.setdefault(k, {})
            rd[(eng if dma is None else ("dma", i))] = i
        dmaval = None
        if dma is not None:
            j = self.last_dma.get(dma)
            if j is not None:
                deps.add(j)
            self.last_dma[dma] = i
            self.dma_count[dma] = self.dma_count.get(dma, 0) + 1
            dmaval = 16 * self.dma_count[dma]
        self.ops.append([eng, fn, deps, raw, dma, dmaval, False, None])
        return i

    def plan(self):
        ops = self.ops
        for i, op in enumerate(ops):
            eng, fn, deps, raw, dma = op[0], op[1], op[2], op[3], op[4]
            keep = []
            for d in deps:
                o = ops[d]
                if o[4] is None and dma is None and o[0] == eng:
                    if eng == "pe":
                        continue
                keep.append(d)
                o[6] = True
            op[2] = sorted(keep)
        cnt = {e: 0 for e in self.ENGS}
        for op in ops:
            if op[4] is not None:
                op[7] = ("dma_" + op[4], op[5])
            elif op[6]:
                cnt[op[0]] += 1
                op[7] = ("eng_" + op[0], cnt[op[0]])
        self.final = {}
        for op in ops:
            if op[7] is not None:
                s, v = op[7]
                self.final[s] = max(self.final.get(s, 0), v)

    def run_engine(self, eng, e, sems, final_wait=False):
        known = {}
        for op in self.ops:
            if op[0] != eng:
                continue
            for d in op[2]:
                s, v = self.ops[d][7]
                if known.get(s, 0) >= v:
                    continue
                known[s] = v
                e.wait_ge(sems[s], v)
            ins = op[1](e)
            if op[7] is not None:
                s, v = op[7]
                ins.then_inc(sems[s], 16 if op[4] is not None else 1)
        if final_wait:
            for s, v in self.final.items():
                if known.get(s, 0) < v:
                    e.wait_ge(sems[s], v)

    def sem_names(self):
        return ["eng_" + e for e in self.ENGS] + ["dma_" + k for k in self.dma_count]


def _slope(h):
    return float(2.0 ** (-(h + 1) / 2.0))


def make_consts():
    c = np.zeros((128, CW), np.float32)
    c[:, 0:128] = np.eye(128, dtype=np.float32)
    c[:, 128:256] = 1.0 / 1024.0
    c[:, 256:320] = 1.0
    c[:, 384 + 64:512] = 1.0
    s = np.arange(128)[:, None].astype(np.float32)
    q = np.arange(128)[None, :].astype(np.float32)
    c[:, 512:528] = -(128.0 + q[:, :16] - s)
    c[:16, 528:544] = -np.abs(q[:, :16] - s[:16])
    return c


def _nd_tables():
    s = np.arange(128)[:, None].astype(np.float32)
    q = np.arange(128)[None, :].astype(np.float32)
    ndA = -(128.0 + q - s)
    ndA[:64, 64:] = -1e5
    ndB = -np.abs(q - s)
    ndB[64:, :64] = -1e5
    return ndA, ndB


def make_cbf():
    import ml_dtypes
    ndA_, ndB_ = _nd_tables()
    out = np.zeros((128, 256 + 4096), np.float32)
    out[:, 0:128] = ndA_.T
    out[:, 128:256] = ndB_.T
    eye = np.eye(128, dtype=np.float32)
    for g in range(2):
        for par in range(2):
            for j in range(4):
                v = np.float32(_slope(8 * g + 2 * j + par) / SCALE)
                hi = np.float32(v.astype(ml_dtypes.bfloat16))
                lo = np.float32(np.float32(v - hi).astype(ml_dtypes.bfloat16))
                base = 256 + ((g * 2 + par) * 2) * 512 + j * 128
                out[:, base:base + 128] = hi * eye
                out[:, base + 512:base + 640] = lo * eye
    return out


def build_nc(NT=8, SAMPLE=True, STOP=None, DBG=False):
    nc = bass.Bass("TRN2", target_bir_lowering=False)
    din = lambda n, sh: nc.dram_tensor(n, sh, F32, kind="ExternalInput").ap()
    dout = lambda n, sh: nc.dram_tensor(n, sh, F32, kind="ExternalOutput").ap()
    x = din("x", [SEQ, D_MODEL])
    xs = din("xs", [32, D_MODEL])
    ck = din("ck", [2, 128, 128])
    cv = din("cv", [2, 128, 128])
    sc = din("sc", [2, 30, 1024])
    norm_g = din("norm_g", [D_MODEL])
    w_in = din("w_in", [D_MODEL, IN_COLS])
    conv_w = din("conv_w", [31, 1024])
    conv_b = din("conv_b", [1024])
    ln_g = din("ln_g", [1024])
    ln_b = din("ln_b", [1024])
    w_pw = din("w_pw", [1024, D_MODEL])
    sink = din("sink", [16])
    w_o = din("w_o", [1024, D_MODEL])
    w_out = din("w_out", [D_MODEL, D_MODEL])
    final_g = din("final_g", [D_MODEL])
    consts = din("consts", [128, CW])
    cbfd = din("cbf", [128, 4352])
    y = dout("y", [SEQ, D_MODEL])
    ys = dout("ys", [32, D_MODEL])
    kwin = dout("kwin", [128, 128])
    vwin = dout("vwin", [128, 128])
    cwin = dout("cwin", [30, 1024])
    ksw = dout("ksw", [2, 128, 128])
    vsw = dout("vsw", [2, 128, 128])
    csw = dout("csw", [2, 30, 1024])
    if DBG:
        dbg = {n: dout("dbg_" + n, [128, 4096]) for n in ("ypre", "stat", "yb", "z", "attn_n", "sga", "qT", "mg0", "mg1", "kT0", "vpad")}

    units = []
    U = {}
    def addu(name, spec):
        U[name] = len(units)
        units.append(spec)
    for c in range(8):
        addu(("a", c), ("in", O_A + c * 128))
        addu(("b", c), ("in", O_B + c * 128))
    for c in range(8):
        addu(("q", c), ("in", O_Q + c * 128))
    for g in range(2):
        addu(("k", g), ("kdup", O_K + g * 64))
    addu(("v", 0), ("in", O_V))
    for c in range(8):
        addu(("ga", c), ("in", O_GA + c * 128))
    for c in range(8):
        addu(("gc", c), ("in", O_GC + c * 128))
    for j in range(16):
        addu(("pw", j), ("pw", j))
        addu(("mc", j), ("in", O_MC + j * 128))
        addu(("wo", j), ("wo", j))
        addu(("ma", j), ("in", O_MA + j * 128))
    for cb in range(4):
        for kq in range(4):
            addu(("out", cb, kq), ("out", cb, kq))
    NU = len(units)
    wsc = nc.dram_tensor("wsc", [NU, 128, 2048], BF16, kind="Internal").ap()

    S = Sched()
    KM = {"ckl": [("xa", 1)], "cvl": [("xa", 1)], ("ctk", 0): [("xa", 1)], ("ctk", 1): [("xa", 1)],
          "v34a": [("xa", 0)], "v34b": [("xa", 0)], "v34c": [("xa", 0)], "v34d": [("xa", 0)],
          "ysq": [("T", 2)], "mean": [("T", 3)], "rstdl": [("T", 4)], "tmpl": [("T", 5)], "sg": [("T", 6)],
          "dn": [("T", 2)], "o1": [("T", 3)], "junk": [("attn_n", g_, s_) for g_ in range(2) for s_ in (0, 16, 128, 256, 384)], "scl": [("xa", 0)]}
    for i_ in range(2):
        KM[("sbt", i_)] = [("T", i_)]
        KM[("Sb", i_)] = [("T", i_)]
        KM[("smc", i_)] = [("T", 4 + i_)]
        KM[("t1", i_)] = [("T", 6)]
        KM[("t2", i_)] = [("T", 7)]
        KM[("cout", i_)] = [("z", c_) for c_ in range(8)]
    for i_ in range(8):
        KM[("sga", i_)] = [("glu", i_), ("gluL", i_)]
        KM[("PT", i_)] = [("T", 4 + i_ // 2)]
        KM[("yb", i_)] = [("Y", 2 * i_), ("Y", 2 * i_ + 1)]
    for i_ in range(8):
        KM[("s_yb", i_)] = [("sY", 2 * i_), ("sY", 2 * i_ + 1)]
    for i_ in range(16):
        KM[("s_mg", i_)] = [("sY", i_)]
        KM[("mg", i_)] = [("Y", i_)]

    def km(keys):
        out = []
        for k in keys:
            for kk in KM.get(k, [k]):
                if kk not in out:
                    out.append(kk)
        return out

    def A(eng, fn, r=(), w=(), dma=None):
        return S.add(eng, fn, km(r), km(w), dma)
    es = ExitStack()
    with es:
        sb = lambda name, shape, dt: es.enter_context(nc.sbuf_tensor(name, shape, dt))
        cst = sb("cst", [128, CW], F32)
        ident = cst[:, 0:128]
        onesm = cst[:, 128:256]
        ndSc = cst[:, 512:528]
        ndSn = cst[:, 528:544]
        identbf = sb("identbf", [128, 128], BF16)
        cbf = sb("cbf_sb", [128, 4352], BF16)
        onespad = sb("onespad", [128, 2, 128], BF16)
        onesbf = sb("onesbf", [128, 128], BF16)
        ng16 = sb("ng16", [16, 128], F32)
        cw = sb("cw", [128, 8, 34], F32)
        normg = sb("normg", [128, 16], F32)
        gfin = sb("gfin", [128, D_MODEL], F32)
        sk = sb("sk", [128, 16], F32)
        esink = sb("esink", [128, 8], F32)
        Wt = [sb(f"W{i}", [128, 2048], BF16) for i in range(NW)]
        hTs = [sb(f"hT{i}", [128, 16, TT], BF16) for i in range(2)]
        cur = {"hs": 0}
        xa = [sb(f"xa{i}", [128, D_MODEL], F32) for i in range(2)]
        v34 = xa[0][0:34, 0:1024]
        xr = [sb(f"xr{i}", [128, D_MODEL], F32) for i in range(2)]
        st4 = [sb(f"st4_{i}", [128, 4], F32) for i in range(2)]
        st5 = [sb(f"st5_{i}", [128, 4], F32) for i in range(4)]
        glu = sb("glu", [128, 8, 30 + TT], BF16)
        gluS = sb("gluS", [128, 8, 2, 46], BF16)
        glu32 = sb("glu32", [128, 8, 32], F32)
        TP = [sb(f"tp{i}", [128, TT], F32) for i in range(8)]
        sbt = [TP[0], TP[1]]
        diag = [sb(f"diag{i}", [128, 31, 128], BF16) for i in range(2)]
        yraw = sb("yraw", [128, 8 * TT], F32)
        yb = yraw[:].rearrange("p (c n) -> p c n", c=8)
        merged = yraw[:].bitcast(BF16).rearrange("p (j n) -> p j n", j=16)
        ysq, mean, rstdl, tmpl, sg = TP[2], TP[3], TP[4], TP[5], TP[6]
        z = sb("z", [128, 8, TT], BF16)
        qT = sb("qT", [128, 8, TT], BF16)
        kT = [sb(f"kT{g}", [128, 128 + TT], BF16) for g in range(2)]
        kTs = [sb(f"kTs{g}", [128, 2, 144], BF16) for g in range(2)]
        vT = sb("vT", [128, TT], BF16)
        k32 = [sb(f"k32_{g}", [128, 128], F32) for g in range(2)]
        v32 = sb("v32", [128, 128], F32)
        vpad = sb("vpad", [128, 5, 2, 2, 128], BF16)
        vpadSc = sb("vpadSc", [128, 2, 2, 2, 128], BF16)
        vpadSn = sb("vpadSn", [16, 2, 2, 2, 128], BF16)
        sga = glu[:, :, 0:TT]
        ctx = sb("ctx", [128, 8, 30], BF16)
        Sb = [TP[0], TP[1]]
        PT = [TP[4 + i // 2][:].bitcast(BF16)[:, (i % 2) * TT:(i % 2 + 1) * TT] for i in range(8)]
        dn, o1 = TP[2], TP[3]
        attn_n = sb("attn_n", [128, 8, TT], BF16)
        junk = attn_n[:].rearrange("p a b -> p (a b)")[:, 0:D_MODEL]
        smc = [TP[4], TP[5]]
        T1P = [TP[6], TP[6]]
        T2P = [TP[7], TP[7]]
        kvo = sb("kvo", [128, 384], F32)
        cout = z[:].rearrange("p a b -> p (a b)").bitcast(F32)[0:32, 0:1024]
        ctk = xa[1][:, 0:256].rearrange("p (a b c) -> p a b c", a=2, b=2)
        ckl = xa[1][:, 256:384]
        cvl = xa[1][:, 384:512]
        scl = xa[0][0:30, 0:1024]
        hT_s = sb("hT_s", [128, 16, 32], BF16)
        t1s = sb("t1s", [128, 32], F32)
        lns = [sb(f"lns{i}", [128, 32], F32) for i in range(3)]
        t2s = sb("t2s", [128, 32], F32)
        yraw_s = sb("yraw_s", [128, 8 * 32], F32)
        SB = {"yb": yraw_s[:].rearrange("p (c n) -> p c n", c=8),
              "merged": yraw_s[:].bitcast(BF16).rearrange("p (j n) -> p j n", j=16),
              "z": sb("z_s", [128, 8, 32], BF16), "qT": sb("qT_s", [128, 8, 32], BF16), "sga": sb("sga_s", [128, 8, 32], BF16),
              "attn_n": sb("attn_n_s", [128, 8, 32], BF16), "glu32": sb("glu32_s", [128, 8, 32], F32),
              "k32": [sb(f"k32s_{g}", [128, 32], F32) for g in range(2)], "v32": sb("v32_s", [128, 32], F32)}
        PB = {"yb": yb, "merged": merged, "z": z, "qT": qT, "sga": sga, "attn_n": attn_n, "glu32": glu32, "k32": k32, "v32": v32}
        banks = [es.enter_context(nc.psum_tensor(f"pb{i}", [128, 512], F32)) for i in range(8)]

        st = {"nbank": 6, "diag": 0, "bank": 0, "w": 0, "xa": 0, "xr": 0, "sbt": 0, "smc": 0, "Sb": 0, "PT": 0}

        def nxt(name, n):
            v = st[name]
            st[name] = (v + 1) % n
            return v

        held = set()

        def bank():
            nb = st["nbank"]
            b = nxt("bank", nb)
            while b in held:
                b = nxt("bank", nb)
            return b, banks[b]

        A("sp", lambda e: e.dma_start(out=cst[:], in_=consts), w=["cst"], dma="c0")
        A("sp", lambda e: e.dma_start(out=v34[0:31, :], in_=conv_w), w=["v34a"], dma="c1")
        A("sp", lambda e: e.dma_start(out=v34[31:32, :], in_=conv_b.rearrange("(o n) -> o n", o=1)), w=["v34b"], dma="c2")
        A("sp", lambda e: e.dma_start(out=v34[32:33, :], in_=ln_g.rearrange("(o n) -> o n", o=1)), w=["v34c"], dma="c3")
        A("sp", lambda e: e.dma_start(out=v34[33:34, :], in_=ln_b.rearrange("(o n) -> o n", o=1)), w=["v34d"], dma="c4")
        A("sp", lambda e: e.dma_start(out=ng16[:], in_=norm_g.rearrange("(k p) -> k p", p=128)), w=["ng16"], dma="c5")
        A("sp", lambda e: e.dma_start(out=gfin[:], in_=final_g.partition_broadcast(128)), w=["gfin"], dma="c6")
        A("sp", lambda e: e.dma_start(out=sk[:], in_=sink.partition_broadcast(128)), w=["sk"], dma="c7")
        A("dve", lambda e: e.tensor_copy(out=identbf[:], in_=ident), r=["cst"], w=["identbf"])
        A("dve", lambda e: e.tensor_copy(out=onesbf[:], in_=onesm), r=["cst"], w=["onesbf"])
        A("dve", lambda e: e.tensor_copy(out=onespad[:].rearrange("p a b -> p (a b)"), in_=cst[:, 256:512]), r=["cst"], w=["onespad"])
        SETUP_LVL = {"S1": 1, "S2": 2, "S3": 3, "S4": 4}.get(STOP, 9)
        if SETUP_LVL >= 2:
          A("pool", lambda e: e.memset(vpad[:].rearrange("p a b c d -> p (a b c d)"), 0.0), w=["vpadz"])
          A("pool", lambda e: e.memset(vpadSc[:].rearrange("p a b c d -> p (a b c d)"), 0.0), w=["vpadScz"])
          A("pool", lambda e: e.memset(vpadSn[:].rearrange("p a b c d -> p (a b c d)"), 0.0), w=["vpadSnz"])
          A("pool", lambda e: e.memset(glu[:, :, 0:30], 0.0), w=[("glu", c) for c in range(8)])
          A("act", lambda e: e.activation(out=sk[:], in_=sk[:], func=AF.Exp), r=["sk"], w=["sk"])
          A("dve", lambda e: e.tensor_copy(out=esink[0:64, :], in_=sk[0:64, 0:16:2]), r=["sk"], w=["esinka"])
          A("dve", lambda e: e.tensor_copy(out=esink[64:128, :], in_=sk[64:128, 1:16:2]), r=["sk"], w=["esinkb"])
        b0, bk0 = bank()
        if SETUP_LVL >= 3:
          pass
        def _tp(e):
            ins = None
            for c in range(8):
                ins = e.transpose(out=bk0[:, c * 34:(c + 1) * 34], in_=v34[0:34, c * 128:(c + 1) * 128], identity=cst[0:34, 0:34])
            return ins
        A("pe", _tp, r=["cst", "v34a", "v34b", "v34c", "v34d"], w=[("B", b0)])
        A("dve", lambda e: e.tensor_copy(out=cw[:].rearrange("p a b -> p (a b)"), in_=bk0[:, 0:272]), r=[("B", b0)], w=["cw"])
        b1, bk1 = bank()
        A("pe", lambda e: e.transpose(out=bk1[:, 0:16], in_=ng16[:], identity=cst[0:16, 0:16]), r=["cst", "ng16"], w=[("B", b1)])
        A("dve", lambda e: e.tensor_copy(out=normg[:], in_=bk1[:, 0:16]), r=[("B", b1)], w=["normg"])

        if SETUP_LVL < 4:
            A("pool", lambda e: e.dma_start(out=cbf[:], in_=cbfd), w=["cbf"], dma="c8")
        for u, spec in enumerate(units if SETUP_LVL >= 4 else []):
            if u == 48:
                A("pool", lambda e: e.dma_start(out=cbf[:], in_=cbfd), w=["cbf"], dma="c8")
            dk = f"cv{u % 8}"
            if spec[0] == "in":
                c0 = spec[1]
                A("pool", lambda e, u=u, c0=c0: e.dma_start(
                    out=wsc[u].rearrange("p (kc j) -> p kc j", kc=16),
                    in_=w_in[:, c0:c0 + 128].rearrange("(kc p) j -> p kc j", p=128)), w=[("wsc", u)], dma=dk)
            elif spec[0] == "kdup":
                c0 = spec[1]
                A("pool", lambda e, u=u, c0=c0: e.dma_start(
                    out=wsc[u].rearrange("p (kc j) -> p kc j", kc=16)[:, :, 0:64],
                    in_=w_in[:, c0:c0 + 64].rearrange("(kc p) j -> p kc j", p=128)), w=[("wsc", u, 0)], dma=dk)
                A("pool", lambda e, u=u, c0=c0: e.dma_start(
                    out=wsc[u].rearrange("p (kc j) -> p kc j", kc=16)[:, :, 64:128],
                    in_=w_in[:, c0:c0 + 64].rearrange("(kc p) j -> p kc j", p=128)), w=[("wsc", u)], r=[("wsc", u, 0)], dma=dk)
            elif spec[0] in ("pw", "wo"):
                src = w_pw if spec[0] == "pw" else w_o
                j = spec[1]
                A("pool", lambda e, u=u, j=j, src=src: e.dma_start(
                    out=wsc[u][:, 0:1024].rearrange("p (kc j) -> p kc j", kc=8),
                    in_=src[:, j * 128:(j + 1) * 128].rearrange("(kc p) j -> p kc j", p=128)), w=[("wsc", u)], dma=dk)
            else:
                cb, kq = spec[1], spec[2]
                A("pool", lambda e, u=u, cb=cb, kq=kq: e.dma_start(
                    out=wsc[u].rearrange("p (kc j) -> p kc j", kc=4),
                    in_=w_out[kq * 512:(kq + 1) * 512, cb * 512:(cb + 1) * 512].rearrange("(kc p) j -> p kc j", p=128)),
                    w=[("wsc", u)], dma=dk)


        def load_w(name):
            if name[0] == "outS":
                name = ("out",) + tuple(name[1:])
            u = U[name]
            s = nxt("w", NW)
            ncol = 1024 if name[0] in ("pw", "wo") else 2048
            A("sp", lambda e, u=u, s=s, ncol=ncol: e.dma_start(out=Wt[s][:, 0:ncol], in_=wsc[u][:, 0:ncol]), r=[("wsc", u)], w=[("W", s)], dma=f"W{s}")
            return s

        def mm_in(name, N, cx):
            s = yield name
            b, bk = bank()
            wv = Wt[s][:].rearrange("p (kc j) -> p kc j", kc=16)
            hT = cx[1]
            def f(e, hT=hT):
                ins = None
                for kc in range(16):
                    ins = e.matmul(bk[:, 0:N], lhsT=wv[:, kc, :], rhs=hT[:, kc, 0:N], start=(kc == 0), stop=(kc == 15))
                return ins
            cx[0]("pe", f, r=[("W", s)] + cx[2], w=[("B", b)])
            return b, bk

        def mm_8(name, N, src, srckeys, cx):
            s = yield name
            b, bk = bank()
            wv = Wt[s][:, 0:1024].rearrange("p (kc j) -> p kc j", kc=8)
            def f(e):
                ins = None
                for kc in range(8):
                    ins = e.matmul(bk[:, 0:N], lhsT=wv[:, kc, :], rhs=src[:, kc, 0:N], start=(kc == 0), stop=(kc == 7))
                return ins
            cx[0]("pe", f, r=[("W", s)] + srckeys, w=[("B", b)])
            return b, bk

        def _mm_in(setcur, name, N):
            return (yield from mm_in(name, N, setcur()))

        def _mm_8(setcur, name, N, src, srckeys):
            return (yield from mm_8(name, N, src, srckeys, setcur()))

        def phaseA(xsrc, tbs, hs, part="both", which=None):
            for ti, (r0, rows) in enumerate(tbs):
                if which is not None and ti not in which:
                    continue
                s = ti % 2
                xt = xa[s]
                if part == "back":
                    phaseA_back(xt, s, r0, rows, hs)
                    continue
                sv = st4[s]
                A("sp", lambda e, s=s, r0=r0, rows=rows: e.dma_start(out=xa[s][0:rows, :], in_=xsrc[r0:r0 + rows, :]), w=[("xa", s)], dma=f"xa{s}")
                A("act", lambda e, xt=xt, sv=sv, rows=rows: e.activation(out=junk[0:rows, :], in_=xt[0:rows, :], func=AF.Square, accum_out=sv[0:rows, 0:1]),
                  r=[("xa", s)], w=["junk", ("st4", s, 0)])
                if STOP == "A1":
                    continue
                A("act", lambda e, sv=sv, rows=rows: e.activation(out=sv[0:rows, 1:2], in_=sv[0:rows, 0:1], func=AF.Sqrt, scale=1.0 / D_MODEL, bias=EPS),
                  r=[("st4", s, 0)], w=[("st4", s, 1)])
                A("dve", lambda e, sv=sv, rows=rows: e.reciprocal(out=sv[0:rows, 2:3], in_=sv[0:rows, 1:2]), r=[("st4", s, 1)], w=[("st4", s, 2)])
                if STOP == "A2":
                    continue
                A("dve", lambda e, xt=xt, sv=sv, rows=rows: e.tensor_scalar(out=xt[0:rows, :], in0=xt[0:rows, :], scalar1=sv[0:rows, 2:3], scalar2=None, op0=ALU.mult),
                  r=[("xa", s), ("st4", s, 2)], w=[("xa", s)])
                if STOP == "A3":
                    continue
                if part == "both":
                    phaseA_back(xt, s, r0, rows, hs)

        def phaseA_back(xt, s, r0, rows, hs):
            hT = hT_s if hs == "s" else hTs[hs]
            hkey = (lambda kc: ("s_hT", 0, kc)) if hs == "s" else (lambda kc: ("hT", hs, kc))
            if True:
                for kq in range(4):
                    b, bk = bank()
                    def f(e, xt=xt, bk=bk, kq=kq, rows=rows):
                        ins = None
                        for kcl in range(4):
                            kc = kq * 4 + kcl
                            ins = e.transpose(out=bk[:, kcl * 128:kcl * 128 + rows], in_=xt[0:rows, kc * 128:(kc + 1) * 128], identity=cst[0:rows, 0:rows])
                        return ins
                    A("pe", f, r=[("xa", s), "cst"], w=[("B", b)])
                    if STOP == "A4":
                        continue
                    for kcl in range(4):
                        kc = kq * 4 + kcl
                        if kq % 2 == 0:
                            A("dve", lambda e, bk=bk, kc=kc, kcl=kcl, r0=r0, rows=rows, hT=hT: e.tensor_scalar(
                                out=hT[:, kc, r0:r0 + rows], in0=bk[:, kcl * 128:kcl * 128 + rows], scalar1=normg[:, kc:kc + 1], scalar2=None, op0=ALU.mult),
                              r=[("B", b), "normg"], w=[hkey(kc)])
                        else:
                            A("act", lambda e, bk=bk, kc=kc, kcl=kcl, r0=r0, rows=rows, hT=hT: e.activation(
                                out=hT[:, kc, r0:r0 + rows], in_=bk[:, kcl * 128:kcl * 128 + rows], func=AF.Copy, scale=normg[:, kc:kc + 1]),
                              r=[("B", b), "normg"], w=[hkey(kc)])

        A0 = A
        SKEYS = ("yb", "mg", "z", "qT", "sga", "attn_n", "glu32", "k32", "glu", "gluL", "hT", "t1", "t2")

        def tile_body(N, xsrc, ydst, tbs, first, last, sample, hooks=(None, None, None, None), hs=0):
            W32 = 32
            if sample:
                def pk(k):
                    if isinstance(k, tuple) and k[0] in SKEYS:
                        return ("s_" + k[0],) + tuple(k[1:])
                    if k in ("v32", "mean", "rstdl", "tmpl"):
                        return "s_" + k
                    return k
                def A(eng, fn, r=(), w=(), dma=None):
                    return A0(eng, fn, [pk(k) for k in r], [pk(k) for k in w], dma)
                yb, merged, z, qT, sga, attn_n, glu32, k32, v32 = SB["yb"], SB["merged"], SB["z"], SB["qT"], SB["sga"], SB["attn_n"], SB["glu32"], SB["k32"], SB["v32"]
                t1 = [t1s, t1s]
                t2 = [t2s, t2s]
                mean, rstdl, tmpl = lns[0], lns[1], lns[2]
                myhT = hT_s
                myHK = [("s_hT", 0, kc) for kc in range(16)]
            else:
                A = A0
                yb, merged, z, qT, sga, attn_n, glu32, k32, v32 = PB["yb"], PB["merged"], PB["z"], PB["qT"], PB["sga"], PB["attn_n"], PB["glu32"], PB["k32"], PB["v32"]
                t1 = T1P
                t2 = T2P
                mean, rstdl, tmpl = TP[3], TP[4], TP[5]
                myhT = hTs[hs]
                myHK = [("hT", hs, kc) for kc in range(16)]
            def setcur():
                return (A, myhT, myHK)
            PL = "dve" if first else "pool"

            def hook(i):
                if hooks[i] is not None:
                    hooks[i]()
            GK = [("glu", c) for c in range(8)]
            GL = [("gluL", c) for c in range(8)]
            if STOP in ('A', 'A1', 'A2', 'A3', 'A4', 'A5', 'A6'):
                return
            dgs = {}
            def build_diag(c):
                di = c % 2
                dg = diag[di]
                dgs[c] = (di, dg)
                def fd(e, c=c, dg=dg):
                    ins = None
                    for tap in range(0, 12):
                        ins = e.tensor_scalar(out=dg[:, tap, :], in0=identbf[:], scalar1=cw[:, c, tap:tap + 1], scalar2=0.0, op0=ALU.mult, op1=ALU.add)
                    return ins
                def fd_dve(e, c=c, dg=dg):
                    ins = None
                    for tap in range(0, 12):
                        ins = e.tensor_scalar(out=dg[:, tap, :], in0=identbf[:], scalar1=cw[:, c, tap:tap + 1], scalar2=None, op0=ALU.mult)
                    return ins
                if first:
                    A("dve", fd_dve, r=["identbf", "cw"], w=[("diag", di, 0)])
                else:
                    A("pool", fd, r=["identbf", "cw"], w=[("diag", di, 0)])
                def fd3(e, c=c, dg=dg):
                    ins = None
                    for tap in range(12, 27):
                        ins = e.tensor_scalar(out=dg[:, tap, :], in0=identbf[:], scalar1=cw[:, c, tap:tap + 1], scalar2=None, op0=ALU.mult)
                    return ins
                A("dve", fd3, r=["identbf", "cw"], w=[("diag", di, 2)])
                def fd2(e, c=c, dg=dg):
                    ins = None
                    for tap in range(27, 31):
                        ins = e.activation(out=dg[:, tap, :], in_=identbf[:], func=AF.Copy, scale=cw[:, c, tap:tap + 1])
                    return ins
                A("act", fd2, r=["identbf", "cw"], w=[("diag", di, 1)])
            if not sample:
                build_diag(0)
            if not sample and not first:
                A("pool", lambda e: e.tensor_copy(out=glu[:, :, 0:30], in_=ctx[:]), r=["ctx"], w=GL)
            if not sample and first:
                A("dve", lambda e: e.memset(glu[:, :, 0:30], 0.0), w=GL)
            for c in range(8):
                ba, bka = yield from _mm_in(setcur, ("a", c), N)
                held.add(ba)
                bb, bkb = yield from _mm_in(setcur, ("b", c), N)
                held.discard(ba)
                s = nxt("sbt", 2)
                A("act", lambda e, s=s, bkb=bkb: e.activation(out=sbt[s][:, 0:N], in_=bkb[:, 0:N], func=AF.Sigmoid), r=[("B", bb)], w=[("sbt", s)])
                if sample:
                    A("dve", lambda e, s=s, bka=bka, c=c: e.tensor_tensor(
                        out=gluS[:, c, :, 30:46], in0=bka[:, 0:32].rearrange("p (s t) -> p s t", s=2),
                        in1=sbt[s][:, 0:32].rearrange("p (s t) -> p s t", s=2), op=ALU.mult),
                      r=[("B", ba), ("sbt", s)], w=[("glu", c)])
                else:
                    A("dve", lambda e, s=s, bka=bka, c=c: e.tensor_tensor(out=glu[:, c, 30:30 + N], in0=bka[:, 0:N], in1=sbt[s][:, 0:N], op=ALU.mult),
                      r=[("B", ba), ("sbt", s)], w=[("glu", c)])
                if last or sample:
                    A("dve", lambda e, s=s, bka=bka, c=c: e.tensor_tensor(out=glu32[:, c, :], in0=bka[:, N - W32:N], in1=sbt[s][:, N - W32:N], op=ALU.mult),
                      r=[("B", ba), ("sbt", s)], w=[("glu32", c)])
            if STOP == 'B1':
                return
            if last or sample:
                c0 = 0 if sample else 2
                nr = 32 - c0
                bs = [bank(), bank()]
                for hh in range(2):
                    b, bk = bs[hh]
                    def f(e, bk=bk, hh=hh):
                        ins = None
                        for cc in range(4):
                            c = hh * 4 + cc
                            ins = e.transpose(out=bk[0:nr, cc * 128:(cc + 1) * 128], in_=glu32[:, c, c0:32], identity=ident)
                        return ins
                    A("pe", f, r=[("glu32", hh * 4 + cc) for cc in range(4)] + ["cst"], w=[("B", b)])
                    A("dve", lambda e, bk=bk, hh=hh: e.tensor_copy(out=cout[0:nr, hh * 512:(hh + 1) * 512], in_=bk[0:nr, :]), r=[("B", b)], w=[("cout", hh)])
                if sample:
                    for s_ in range(2):
                        A("act", lambda e, s_=s_: e.dma_start(out=csw[s_, 14:30, :], in_=cout[s_ * 16:(s_ + 1) * 16, :]), r=[("cout", 0), ("cout", 1)], dma=f"o{s_}")
                        A("act", lambda e, s_=s_: e.dma_start(out=csw[s_, 0:14, :], in_=sc[s_, 16:30, :]), dma=f"o{2 + s_}")
                else:
                    A("act", lambda e: e.dma_start(out=cwin, in_=cout[0:30, :]), r=[("cout", 0), ("cout", 1)], dma="o0")
            bsum, bksum = 6, banks[6]
            bsq, bksq = 7, banks[7]
            if sample:
                build_diag(0)
            prev_stat = None
            for c in range(8):
                if c < 7:
                    build_diag(c + 1)
                di, dg = dgs[c]
                b, bk = bank()
                def fc(e, c=c, bk=bk, dg=dg):
                    ins = None
                    for tap in range(31):
                        if sample:
                            rhs = gluS[:, c, :, tap:tap + 16]
                            o = bk[:, 0:32].rearrange("p (s t) -> p s t", s=2)
                        else:
                            rhs = glu[:, c, tap:tap + N]
                            o = bk[:, 0:N]
                        ins = e.matmul(o, lhsT=dg[:, tap, :], rhs=rhs, start=(tap == 0), stop=(tap == 30))
                    return ins
                A("pe", fc, r=[("diag", di, 0), ("diag", di, 1), ("diag", di, 2), ("glu", c), ("gluL", c)], w=[("B", b)])
                A("act", lambda e, c=c, bk=bk: e.activation(out=yb[:, c, 0:N], in_=bk[:, 0:N], func=AF.Identity, bias=cw[:, c, 31:32]),
                  r=[("B", b), "cw"], w=[("yb", c)])
                ysqb = (ysq if c % 2 == 0 else TP[0])[:].bitcast(BF16)
                ysk = "ysq" if c % 2 == 0 else ("T", 0)
                def fs(e, c=c, bk=bk, ysqb=ysqb):
                    e.activation(out=ysqb[:, 0:N], in_=bk[:, 0:N], func=AF.Square, bias=cw[:, c, 31:32])
                    return e.activation(out=ysqb[:, TT:TT + N], in_=bk[:, 0:N], func=AF.Identity, bias=cw[:, c, 31:32])
                A("act", fs, r=[("B", b), "cw"], w=[ysk])
                def fst(e, c=c, ysqb=ysqb):
                    e.matmul(bksum[:, 0:N], lhsT=onesbf[:], rhs=ysqb[:, TT:TT + N], start=(c == 0), stop=(c == 7))
                    return e.matmul(bksq[:, 0:N], lhsT=onesbf[:], rhs=ysqb[:, 0:N], start=(c == 0), stop=(c == 7))
                if prev_stat is not None:
                    A("pe", prev_stat[0], r=[prev_stat[1], "onesbf"], w=[("B", bsum), ("B", bsq)])
                prev_stat = (fst, ysk)
                if c == 2:
                    hook(0)
            A("pe", prev_stat[0], r=[prev_stat[1], "onesbf"], w=[("B", bsum), ("B", bsq)])
            if not sample and not last:
                A("dve", lambda e: e.tensor_copy(out=ctx[:], in_=glu[:, :, N:N + 30]), r=GK, w=["ctx"])
            A("act", lambda e: e.activation(out=mean[:, 0:N], in_=bksum[:, 0:N], func=AF.Copy), r=[("B", bsum)], w=["mean"])
            A("dve", lambda e: e.tensor_tensor(out=tmpl[:, 0:N], in0=mean[:, 0:N], in1=mean[:, 0:N], op=ALU.mult), r=["mean"], w=["tmpl"])
            A("dve", lambda e: e.tensor_tensor(out=tmpl[:, 0:N], in0=bksq[:, 0:N], in1=tmpl[:, 0:N], op=ALU.subtract), r=[("B", bsq), "tmpl"], w=["tmpl"])
            A("act", lambda e: e.activation(out=tmpl[:, 0:N], in_=tmpl[:, 0:N], func=AF.Sqrt, bias=EPS), r=["tmpl"], w=["tmpl"])
            A("dve", lambda e: e.reciprocal(out=rstdl[:, 0:N], in_=tmpl[:, 0:N]), r=["tmpl"], w=["rstdl"])
            if DBG and first:
                A("pool", lambda e: e.dma_start(out=dbg["ypre"], in_=yraw[:]), r=[("yb", c) for c in range(8)], dma="dbg0")
                A("pool", lambda e: e.dma_start(out=dbg["stat"][:, 0:512], in_=mean[:]), r=["mean"], dma="dbg0")
                A("pool", lambda e: e.dma_start(out=dbg["stat"][:, 512:1024], in_=rstdl[:]), r=["rstdl"], dma="dbg0")
                A("pool", lambda e: e.dma_start(out=dbg["stat"][:, 1024:1536], in_=tmpl[:]), r=["tmpl"], dma="dbg0")
            def ln_apply(c):
                en = PL if (c % 3 == 2 and not sample) else "dve"
                A(en, lambda e, c=c: e.tensor_tensor(out=yb[:, c, 0:N], in0=yb[:, c, 0:N], in1=mean[:, 0:N], op=ALU.subtract), r=[("yb", c), "mean"], w=[("yb", c)])
                A(en, lambda e, c=c: e.tensor_tensor(out=yb[:, c, 0:N], in0=yb[:, c, 0:N], in1=rstdl[:, 0:N], op=ALU.mult), r=[("yb", c), "rstdl"], w=[("yb", c)])
                A("act", lambda e, c=c: e.activation(out=yb[:, c, 0:N], in_=yb[:, c, 0:N], func=AF.Silu, scale=cw[:, c, 32:33], bias=cw[:, c, 33:34]),
                  r=[("yb", c), "cw"], w=[("yb", c)])
            if STOP == 'B2':
                return
            hook(1)
            if STOP == 'B3':
                return
            if DBG and first:
                A("pool", lambda e: e.dma_start(out=dbg["yb"], in_=yraw[:]), r=[("yb", c) for c in range(8)], dma="dbg0")
            for c in range(8):
                b, bk = yield from _mm_in(setcur, ("q", c), N)
                A("act", lambda e, c=c, bk=bk: e.activation(out=qT[:, c, 0:N], in_=bk[:, 0:N], func=AF.Copy), r=[("B", b)], w=[("qT", c)])
                ln_apply(c)
                if c == 2:
                    hook(3)
            for g in range(2):
                b, bk = yield from _mm_in(setcur, ("k", g), N)
                if sample:
                    A("dve", lambda e, g=g, bk=bk: e.tensor_copy(out=kTs[g][:, :, 128:144], in_=bk[:, 0:32].rearrange("p (s t) -> p s t", s=2)),
                      r=[("B", b)], w=[("kTn", g)])
                else:
                    A("dve", lambda e, g=g, bk=bk: e.tensor_copy(out=kT[g][:, 128:128 + N], in_=bk[:, 0:N]), r=[("B", b)], w=[("kT", g)])
                if last or sample:
                    nk = 32 if sample else 128
                    A("dve", lambda e, g=g, bk=bk, nk=nk: e.tensor_copy(out=k32[g][:, 0:nk], in_=bk[:, N - nk:N]), r=[("B", b)], w=[("k32", g)])
            b, bk = yield from _mm_in(setcur, ("v", 0), N)
            A("dve", lambda e, bk=bk: e.tensor_copy(out=vT[:, 0:N], in_=bk[:, 0:N]), r=[("B", b)], w=["vT"])
            if last or sample:
                nk = 32 if sample else 128
                A("dve", lambda e, bk=bk, nk=nk: e.tensor_copy(out=v32[:, 0:nk], in_=bk[:, N - nk:N]), r=[("B", b)], w=["v32"])
            bv, bkv = bank()
            pbf = bkv[:].bitcast(BF16)
            if sample:
                def fv(e):
                    ins = None
                    for s_ in range(2):
                        ins = e.transpose(out=pbf[0:16, s_ * 128:(s_ + 1) * 128], in_=vT[:, s_ * 16:(s_ + 1) * 16], identity=identbf[:])
                    return ins
                A("pe", fv, r=["vT", "identbf"], w=[("B", bv)])
                for g in range(2):
                    for par in range(2):
                        A("dve", lambda e, g=g, par=par: e.tensor_copy(
                            out=vpadSn[:, :, g, par, par * 64:(par + 1) * 64],
                            in_=pbf[0:16, 0:256].rearrange("p (s c) -> p s c", s=2)[:, :, g * 64:(g + 1) * 64]),
                          r=[("B", bv), "vpadSnz"], w=[("vpadSn", g, par)])
            else:
                nb = N // 128
                def fv(e):
                    ins = None
                    for blk in range(nb):
                        ins = e.transpose(out=pbf[:, blk * 128:(blk + 1) * 128], in_=vT[:, blk * 128:(blk + 1) * 128], identity=identbf[:])
                    return ins
                A("pe", fv, r=["vT", "identbf"], w=[("B", bv)])
                for g in range(2):
                    for par in range(2):
                        A("dve", lambda e, g=g, par=par: e.tensor_copy(
                            out=vpad[:, 1:1 + nb, g, par, par * 64:(par + 1) * 64],
                            in_=pbf[:, 0:nb * 128].rearrange("p (s c) -> p s c", s=nb)[:, :, g * 64:(g + 1) * 64]),
                          r=[("B", bv), "vpadz"], w=[("vpad", g, par)])
            if last or sample:
                nk = 32 if sample else 128
                bo, bko = bank()
                def fo(e):
                    e.transpose(out=bko[0:nk, 0:128], in_=k32[0][:, 0:nk], identity=ident)
                    e.transpose(out=bko[0:nk, 128:256], in_=k32[1][:, 0:nk], identity=ident)
                    return e.transpose(out=bko[0:nk, 256:384], in_=v32[:, 0:nk], identity=ident)
                A("pe", fo, r=[("k32", 0), ("k32", 1), "v32", "cst"], w=[("B", bo)])
                A("dve", lambda e: e.tensor_copy(out=kvo[0:nk, 0:64], in_=bko[0:nk, 0:64]), r=[("B", bo)], w=["kvo0"])
                A("dve", lambda e: e.tensor_copy(out=kvo[0:nk, 64:128], in_=bko[0:nk, 192:256]), r=[("B", bo)], w=["kvo1"])
                A("dve", lambda e: e.tensor_copy(out=kvo[0:nk, 128:256], in_=bko[0:nk, 256:384]), r=[("B", bo)], w=["kvo2"])
                KV = ["kvo0", "kvo1", "kvo2"]
                if sample:
                    for s_ in range(2):
                        A("act", lambda e, s_=s_: e.dma_start(out=ksw[s_, 112:128, :], in_=kvo[s_ * 16:(s_ + 1) * 16, 0:128]), r=KV, dma=f"o{4 + s_}")
                        A("act", lambda e, s_=s_: e.dma_start(out=vsw[s_, 112:128, :], in_=kvo[s_ * 16:(s_ + 1) * 16, 128:256]), r=KV, dma=f"o{6 + s_}")
                        A("act", lambda e, s_=s_: e.dma_start(out=ksw[s_, 0:112, :], in_=ck[s_, 16:128, :]), dma=f"o{8 + s_}")
                        A("act", lambda e, s_=s_: e.dma_start(out=vsw[s_, 0:112, :], in_=cv[s_, 16:128, :]), dma=f"o{10 + s_}")
                else:
                    A("act", lambda e: e.dma_start(out=kwin, in_=kvo[:, 0:128]), r=KV, dma="o1")
                    A("act", lambda e: e.dma_start(out=vwin, in_=kvo[:, 128:256]), r=KV, dma="o2")
            for c in range(8):
                b, bk = yield from _mm_in(setcur, ("ga", c), N)
                A("act", lambda e, c=c, bk=bk: e.activation(out=sga[:, c, 0:N], in_=bk[:, 0:N], func=AF.Silu), r=[("B", b)], w=[("sga", c)])

            if STOP == 'C1':
                return
            for c in range(8):
                b, bk = yield from _mm_in(setcur, ("gc", c), N)
                sgt, sgk = (TP[6], ("T", 6)) if c % 2 == 0 else (TP[7], ("T", 7))
                A("act", lambda e, bk=bk, sgt=sgt: e.activation(out=sgt[:, 0:N], in_=bk[:, 0:N], func=AF.Silu), r=[("B", b)], w=[sgk])
                A("dve", lambda e, c=c, sgt=sgt: e.tensor_tensor(out=z[:, c, 0:N], in0=yb[:, c, 0:N], in1=sgt[:, 0:N], op=ALU.mult), r=[("yb", c), sgk], w=[("z", c)])
            QK = [("qT", c) for c in range(8)]
            hook(2)
            st["nbank"] = 8

            pend = []

            def attn(nq, qsl, ktiles, outsl, extra_r, pe_bias=False):
                for g in range(2):
                    pts = attn_scores(nq, qsl, ktiles, extra_r, pe_bias, g)
                    if pend:
                        attn_pv(*pend.pop())
                    pend.append((nq, outsl, extra_r, g, pts))

            def attn_flush():
                if pend:
                    attn_pv(*pend.pop())

            def attn_scores(nq, qsl, ktiles, extra_r, pe_bias, g):
                if True:
                    pts = []
                    for (nk, kfn, vfn, nd, rk) in ktiles:
                        for par in range(2):
                            b, bk = bank()
                            rows = slice(par * 64, (par + 1) * 64)
                            if pe_bias:
                                def fsc(e, bk=bk, nk=nk, kfn=kfn, g=g, rows=rows, nd=nd, par=par):
                                    e.matmul(bk[0:nk, 0:4 * nq].rearrange("p (j q) -> p j q", j=4),
                                             lhsT=kfn(g)[rows, :], rhs=qT[rows, 4 * g:4 * g + 4, qsl], start=True, stop=False)
                                    rb = 256 + ((g * 2 + par) * 2) * 512
                                    e.matmul(bk[0:nk, 0:512], lhsT=nd, rhs=cbf[:, rb:rb + 512], start=False, stop=False)
                                    return e.matmul(bk[0:nk, 0:512], lhsT=nd, rhs=cbf[:, rb + 512:rb + 1024], start=False, stop=True)
                                A("pe", fsc, r=QK + rk + extra_r + ["cbf"], w=[("B", b)])
                                pi = nxt("PT", 8)
                                A("act", lambda e, bk=bk, pi=pi, nk=nk: e.activation(out=PT[pi][0:nk, 0:4 * nq], in_=bk[0:nk, 0:4 * nq], func=AF.Exp, scale=SCALE),
                                  r=[("B", b)], w=[("PT", pi)])
                                pts.append((pi, nk, vfn, par, rk))
                                continue
                            A("pe", lambda e, bk=bk, nk=nk, kfn=kfn, g=g, rows=rows: e.matmul(
                                bk[0:nk, 0:4 * nq].rearrange("p (j q) -> p j q", j=4),
                                lhsT=kfn(g)[rows, :], rhs=qT[rows, 4 * g:4 * g + 4, qsl], start=True, stop=True),
                              r=QK + rk + extra_r, w=[("B", b)])
                            si = nxt("Sb", 2)
                            def fb(e, bk=bk, nk=nk, nd=nd, si=si, g=g, par=par):
                                ins = None
                                for j in range(4):
                                    h = 8 * g + 2 * j + par
                                    ins = e.scalar_tensor_tensor(out=Sb[si][0:nk, j * nq:(j + 1) * nq], in0=nd[0:nk, 0:nq], scalar=_slope(h) / SCALE,
                                                                 op0=ALU.mult, in1=bk[0:nk, j * nq:(j + 1) * nq], op1=ALU.add)
                                return ins
                            A("dve", fb, r=[("B", b), "cst"], w=[("Sb", si)])
                            pi = nxt("PT", 8)
                            A("act", lambda e, si=si, pi=pi, nk=nk: e.activation(out=PT[pi][0:nk, 0:4 * nq], in_=Sb[si][0:nk, 0:4 * nq], func=AF.Exp, scale=SCALE),
                              r=[("Sb", si)], w=[("PT", pi)])
                            pts.append((pi, nk, vfn, par, rk))
                return pts

            def attn_pv(nq, outsl, extra_r, g, pts):
                if True:
                    bacc, bkacc = bank()
                    bden, bkden = bank()
                    def fpv(e, pts=pts, g=g, bkacc=bkacc):
                        ins = None
                        for i, (pi, nk, vfn, par, rk) in enumerate(pts):
                            ins = e.matmul(bkacc[:, 0:4 * nq], lhsT=vfn(g, par), rhs=PT[pi][0:nk, 0:4 * nq], start=(i == 0), stop=(i == len(pts) - 1))
                        return ins
                    def fden(e, pts=pts, bkden=bkden):
                        ins = None
                        for i, (pi, nk, vfn, par, rk) in enumerate(pts):
                            ins = e.matmul(bkden[:, 0:4 * nq], lhsT=onespad[0:nk, par, :], rhs=PT[pi][0:nk, 0:4 * nq], start=(i == 0), stop=(i == len(pts) - 1))
                        return ins
                    prk = [("PT", p[0]) for p in pts] + sum([p[4] for p in pts], []) + extra_r
                    A("pe", fpv, r=prk, w=[("B", bacc)])
                    A("pe", fden, r=prk + ["onespad"], w=[("B", bden)])
                    def fdn(e, g=g, bkden=bkden):
                        ins = None
                        for j in range(4):
                            ins = e.tensor_scalar(out=dn[:, j * nq:(j + 1) * nq], in0=bkden[:, j * nq:(j + 1) * nq], scalar1=esink[:, 4 * g + j:4 * g + j + 1], scalar2=None, op0=ALU.add)
                        return ins
                    A("dve", fdn, r=[("B", bden), "esinka", "esinkb"], w=["dn"])
                    A("dve", lambda e: e.reciprocal(out=dn[:, 0:4 * nq], in_=dn[:, 0:4 * nq]), r=["dn"], w=["dn"])
                    A("dve", lambda e, bkacc=bkacc: e.tensor_tensor(out=o1[:, 0:4 * nq], in0=bkacc[:, 0:4 * nq], in1=dn[:, 0:4 * nq], op=ALU.mult), r=[("B", bacc), "dn"], w=["o1"])
                    A(PL, lambda e, g=g: e.tensor_tensor(out=attn_n[:, 4 * g:4 * g + 4, outsl], in0=o1[:, 0:4 * nq].rearrange("p (j q) -> p j q", j=4),
                                                           in1=sga[:, 4 * g:4 * g + 4, outsl], op=ALU.mult),
                      r=["o1"] + [("sga", c) for c in range(4 * g, 4 * g + 4)], w=[("attn_n", g, outsl.start)])

            VK = [("vpad", g, par) for g in range(2) for par in range(2)]
            if sample:
                for s_ in range(2):
                    kt = [
                        (128, lambda g, s_=s_: kTs[g][:, s_, 0:128], lambda g, par, s_=s_: vpadSc[:, s_, g, par, :], ndSc,
                         [("kTc", 0, s_), ("kTc", 1, s_)] + [("vpadSc", s_)]),
                        (16, lambda g, s_=s_: kTs[g][:, s_, 128:144], lambda g, par, s_=s_: vpadSn[:, s_, g, par, :], ndSn,
                         [("kTn", 0), ("kTn", 1)] + [("vpadSn", g, par) for g in range(2) for par in range(2)]),
                    ]
                    attn(16, slice(s_ * 16, (s_ + 1) * 16), kt, slice(s_ * 16, (s_ + 1) * 16), [])
                attn_flush()
            else:
                for i in range(N // 128):
                    kt = []
                    if not (first and i == 0):
                        kt.append((128, lambda g, i=i: kT[g][:, i * 128:(i + 1) * 128], lambda g, par, i=i: vpad[:, i, g, par, :], cbf[:, 0:128],
                                   [("kT", 0), ("kT", 1)] + VK))
                    kt.append((128, lambda g, i=i: kT[g][:, 128 + i * 128:128 + (i + 1) * 128], lambda g, par, i=i: vpad[:, i + 1, g, par, :], cbf[:, 128:256],
                               [("kT", 0), ("kT", 1)] + VK))
                    attn(128, slice(i * 128, (i + 1) * 128), kt, slice(i * 128, (i + 1) * 128), [], pe_bias=True)
                attn_flush()
                if not last:
                    for g in range(2):
                        A(PL, lambda e, g=g: e.tensor_copy(out=kT[g][:, 0:128], in_=kT[g][:, N:N + 128]), r=[("kT", g)], w=[("kT", g)])
                    A(PL, lambda e: e.tensor_copy(out=vpad[:, 0], in_=vpad[:, N // 128]), r=VK, w=VK)
            st["nbank"] = 6
            st["bank"] = st["bank"] % 6
            AK = [("attn_n", g, i * (16 if sample else 128)) for g in range(2) for i in range(2 if sample else N // 128)]

            if STOP == 'C2':
                return
            ZK = [("z", c) for c in range(8)]
            xr4 = [xr[0], xr[1], xa[0], xa[1]]
            xk4 = [("xr", 0), ("xr", 1), ("xa", 0), ("xa", 1)]
            if not sample:
                for ti, (r0, rows) in enumerate(tbs):
                    A("sp", lambda e, s=ti, r0=r0, rows=rows: e.dma_start(out=xr4[s][0:rows, :], in_=xsrc[r0:r0 + rows, :]), w=[xk4[ti]], dma=f"xr{ti}")
            for j in range(16):
                bA, bkA = yield from _mm_8(setcur, ("pw", j), N, z, ZK)
                held.add(bA)
                bB, bkB = yield from _mm_in(setcur, ("mc", j), N)
                held.discard(bA)
                s = nxt("smc", 2)
                A("act", lambda e, s=s, bkB=bkB: e.activation(out=smc[s][:, 0:N], in_=bkB[:, 0:N], func=AF.Sigmoid), r=[("B", bB)], w=[("smc", s)])
                A("dve", lambda e, s=s, bkA=bkA: e.tensor_tensor(out=t1[s][:, 0:N], in0=bkA[:, 0:N], in1=smc[s][:, 0:N], op=ALU.mult), r=[("B", bA), ("smc", s)], w=[("t1", s)])
                bC, bkC = yield from _mm_8(setcur, ("wo", j), N, attn_n, AK)
                held.add(bC)
                bD, bkD = yield from _mm_in(setcur, ("ma", j), N)
                held.discard(bC)
                A("act", lambda e, s=s, bkD=bkD: e.activation(out=smc[s][:, 0:N], in_=bkD[:, 0:N], func=AF.Sigmoid), r=[("B", bD), ("t1", s)], w=[("smc", s)])
                A("dve", lambda e, s=s, bkC=bkC: e.tensor_tensor(out=t2[s][:, 0:N], in0=bkC[:, 0:N], in1=smc[s][:, 0:N], op=ALU.mult), r=[("B", bC), ("smc", s)], w=[("t2", s)])
                A(PL, lambda e, s=s, j=j: e.tensor_tensor(out=merged[:, j, 0:N], in0=t1[s][:, 0:N], in1=t2[s][:, 0:N], op=ALU.add), r=[("t1", s), ("t2", s)], w=[("mg", j)])
            MK = [("mg", j) for j in range(16)]

            if STOP == 'M':
                return
            if DBG and first:
                fl = lambda t: t[:].rearrange("p a b -> p (a b)")
                A("pool", lambda e: e.dma_start(out=dbg["z"], in_=fl(z)), r=ZK, dma="dbg1")
                A("pool", lambda e: e.dma_start(out=dbg["attn_n"], in_=fl(attn_n)), r=AK, dma="dbg2")
                pass
                A("pool", lambda e: e.dma_start(out=dbg["qT"], in_=fl(qT)), r=QK, dma="dbg4")
                A("pool", lambda e: e.dma_start(out=dbg["mg0"], in_=merged[:, 0:8, :].rearrange("p a b -> p (a b)")), r=MK, dma="dbg5")
                A("pool", lambda e: e.dma_start(out=dbg["mg1"], in_=merged[:, 8:16, :].rearrange("p a b -> p (a b)")), r=MK, dma="dbg6")
                A("pool", lambda e: e.dma_start(out=dbg["kT0"][:, 0:640], in_=kT[0][:]), r=[("kT", 0)], dma="dbg7")
                A("pool", lambda e: e.dma_start(out=dbg["vpad"][:, 0:2560], in_=vpad[:].rearrange("p a b c d -> p (a b c d)")), r=VK, dma="dbg8")
            if sample:
                yield ("barrier",)
            halves = [tbs]
            st["nbank"] = 8
            for half in halves:
                slots = []
                for ti, (r0, rows) in enumerate(half):
                    s = ti
                    slots.append(s)
                    if sample:
                        A("sp", lambda e, s=s, r0=r0, rows=rows: e.dma_start(out=xr4[s][0:rows, :], in_=xsrc[r0:r0 + rows, :]), w=[xk4[s]], dma=f"xr{s}")
                for cb in range(4):
                    pbs = [bank() for _ in half]
                    for kq in range(4):
                        ws = yield (("outS", cb, kq) if sample else ("out", cb, kq))
                        wv = Wt[ws][:].rearrange("p (kc j) -> p kc j", kc=4)
                        def f(e, wv=wv, pbs=pbs, kq=kq, half=half):
                            ins = None
                            for (r0, rows), (b, bk) in zip(half, pbs):
                                for kcl in range(4):
                                    ins = e.matmul(bk[0:rows, :], lhsT=merged[:, kq * 4 + kcl, r0:r0 + rows], rhs=wv[:, kcl, :],
                                                   start=(kq == 0 and kcl == 0), stop=(kq == 3 and kcl == 3))
                            return ins
                        A("pe", f, r=[("W", ws)] + MK, w=[("B", b) for (b, bk) in pbs])
                    for (r0, rows), (b, bk), s in zip(half, pbs, slots):
                        A("dve", lambda e, bk=bk, s=s, rows=rows, cb=cb: e.tensor_tensor(
                            out=xr4[s][0:rows, cb * 512:(cb + 1) * 512], in0=bk[0:rows, :], in1=xr4[s][0:rows, cb * 512:(cb + 1) * 512], op=ALU.add),
                          r=[("B", b), xk4[s]], w=[xk4[s]])
                for (r0, rows), s in zip(half, slots):
                    sv = st5[s]
                    A("act", lambda e, s=s, sv=sv, rows=rows: e.activation(out=junk[0:rows, :], in_=xr4[s][0:rows, :], func=AF.Square, accum_out=sv[0:rows, 3:4]),
                      r=[xk4[s]], w=["junk", ("st4", s, 3)])
                    A("act", lambda e, sv=sv, rows=rows: e.activation(out=sv[0:rows, 3:4], in_=sv[0:rows, 3:4], func=AF.Sqrt, scale=1.0 / D_MODEL, bias=EPS),
                      r=[("st4", s, 3)], w=[("st4", s, 3)])
                    A("dve", lambda e, sv=sv, rows=rows: e.reciprocal(out=sv[0:rows, 3:4], in_=sv[0:rows, 3:4]), r=[("st4", s, 3)], w=[("st4", s, 3)])
                    if last or sample or first or s >= 2:
                        A("dve", lambda e, s=s, sv=sv, rows=rows: e.scalar_tensor_tensor(out=xr4[s][0:rows, :], in0=xr4[s][0:rows, :], scalar=sv[0:rows, 3:4], op0=ALU.mult,
                                                                                        in1=gfin[0:rows, :], op1=ALU.mult),
                          r=[xk4[s], ("st4", s, 3), "gfin"], w=[xk4[s]])
                    else:
                        A("act", lambda e, s=s, sv=sv, rows=rows: e.activation(out=xr4[s][0:rows, :], in_=xr4[s][0:rows, :], func=AF.Copy, scale=sv[0:rows, 3:4]),
                          r=[xk4[s], ("st4", s, 3)], w=[xk4[s]])
                        A("pool", lambda e, s=s, rows=rows: e.tensor_tensor(out=xr4[s][0:rows, :], in0=xr4[s][0:rows, :], in1=gfin[0:rows, :], op=ALU.mult),
                          r=[xk4[s], "gfin"], w=[xk4[s]])
                    yq = "act" if (last or sample or first) else "pool"
                    A(yq, lambda e, s=s, r0=r0, rows=rows: e.dma_start(out=ydst[r0:r0 + rows, :], in_=xr4[s][0:rows, :]), r=[xk4[s]], dma=(f"yp{s}" if yq == "pool" else f"y{s}"))
            st["nbank"] = 6
            st["bank"] = st["bank"] % 6

        def sample_prep():
            for s_ in range(2):
                A("sp", lambda e, s_=s_: e.dma_start(out=ckl, in_=ck[s_]), w=["ckl"], dma="ckl")
                A("sp", lambda e, s_=s_: e.dma_start(out=cvl, in_=cv[s_]), w=["cvl"], dma="cvl")
                A("sp", lambda e, s_=s_: e.dma_start(out=scl, in_=sc[s_]), w=["scl"], dma="scl")
                for d in range(2):
                    A("dve", lambda e, d=d: e.tensor_copy(out=ctk[:, :, d, :], in_=ckl.rearrange("p (g c) -> p g c", g=2)), r=["ckl"], w=[("ctk", d)])
                for g in range(2):
                    b, bk = bank()
                    A("pe", lambda e, g=g, bk=bk: e.transpose(out=bk[:, 0:128], in_=ctk[:, g].rearrange("p a b -> p (a b)"), identity=ident),
                      r=[("ctk", 0), ("ctk", 1), "cst"], w=[("B", b)])
                    A("dve", lambda e, g=g, bk=bk, s_=s_: e.tensor_copy(out=kTs[g][:, s_, 0:128], in_=bk[:, 0:128]), r=[("B", b)], w=[("kTc", g, s_)])
                for g in range(2):
                    for par in range(2):
                        A("dve", lambda e, g=g, par=par, s_=s_: e.tensor_copy(out=vpadSc[:, s_, g, par, par * 64:(par + 1) * 64], in_=cvl[:, g * 64:(g + 1) * 64]),
                          r=["cvl", "vpadScz"], w=[("vpadSc", s_)])
                for hh in range(2):
                    b, bk = bank()
                    def f(e, bk=bk, hh=hh):
                        ins = None
                        for cc in range(4):
                            c = hh * 4 + cc
                            ins = e.transpose(out=bk[:, cc * 32:cc * 32 + 30], in_=scl[0:30, c * 128:(c + 1) * 128], identity=cst[0:30, 0:30])
                        return ins
                    A("pe", f, r=["scl", "cst"], w=[("B", b)])
                    A("dve", lambda e, bk=bk, hh=hh, s_=s_: e.tensor_copy(out=gluS[:, hh * 4:hh * 4 + 4, s_, 0:30],
                                                                        in_=bk[:, 0:128].rearrange("p (c t) -> p c t", c=4)[:, :, 0:30]),
                      r=[("B", b)], w=[("s_glu", hh * 4 + cc) for cc in range(4)])

        TB4 = [(i * 128, 128) for i in range(4)]

        def drive(gens):
            reqs = {}
            active = []
            for i, g in enumerate(gens):
                try:
                    reqs[i] = next(g)
                    active.append(i)
                except StopIteration:
                    pass
            while active:
                cand = [j for j in active if reqs[j] != ("barrier",)]
                if not cand:
                    j = active[0]
                    try:
                        reqs[j] = gens[j].send(None)
                    except StopIteration:
                        active.remove(j)
                    continue
                name = reqs[cand[0]]
                slot = load_w(name)
                for j in [j for j in cand if reqs[j] == name]:
                    try:
                        reqs[j] = gens[j].send(slot)
                    except StopIteration:
                        active.remove(j)

        if SETUP_LVL >= 9:
            def hooks_for(t, hs):
                xsrc = x[t * TT:(t + 1) * TT, :]
                return (lambda: phaseA(xsrc, TB4, hs, "front", (0, 1)),
                        lambda: phaseA(xsrc, TB4, hs, "back", (0, 1)),
                        lambda: phaseA(xsrc, TB4, hs, "back", (2, 3)),
                        lambda: phaseA(xsrc, TB4, hs, "front", (2, 3)))
            if SAMPLE:
                sample_prep()
                phaseA(xs, [(0, 32)], "s")
            if NT > 0:
                phaseA(x[0:TT, :], TB4, 0)
            for t in range(NT):
                hs = t % 2
                hk = hooks_for(t + 1, 1 - hs) if t + 1 < NT else (None, None, None, None)
                gens = [tile_body(TT, x[t * TT:(t + 1) * TT, :], y[t * TT:(t + 1) * TT, :], TB4,
                                  first=(t == 0), last=(t == NT - 1), sample=False, hooks=hk, hs=hs)]
                if t == NT - 1 and SAMPLE:
                    gens.append(tile_body(32, xs, ys, [(0, 32)], first=False, last=False, sample=True))
                drive(gens)
            if SAMPLE and NT == 0:
                drive([tile_body(32, xs, ys, [(0, 32)], first=False, last=False, sample=True)])

        S.plan()
        sems = {n: es.enter_context(nc.semaphore(n)) for n in S.sem_names()}
        with nc.Block() as block:
            @block.tensor
            def _(e):
                S.run_engine("pe", e, sems)

            @block.scalar
            def _(e):
                S.run_engine("act", e, sems)

            @block.vector
            def _(e):
                S.run_engine("dve", e, sems)

            @block.gpsimd
            def _(e):
                S.run_engine("pool", e, sems)

            @block.sync
            def _(e):
                S.run_engine("sp", e, sems, final_wait=True)
    return nc


_NC_CACHE = {}


def kernel(x_prompt, x_sample, cache_k, cache_v, state_conv, norm_g, w_in, conv_w, conv_b, ln_g, ln_b,
           w_conv_pw, attn_sink, w_o_attn, w_out, final_g):
    f = lambda a: np.ascontiguousarray(np.asarray(a, dtype=np.float32))
    x_prompt, x_sample, cache_k, cache_v, state_conv = map(f, (x_prompt, x_sample, cache_k, cache_v, state_conv))
    shared = {
        "norm_g": f(norm_g).reshape(2048), "w_in": f(w_in).reshape(2048, IN_COLS), "conv_w": f(conv_w).reshape(31, 1024),
        "conv_b": f(conv_b).reshape(1024), "ln_g": f(ln_g).reshape(1024), "ln_b": f(ln_b).reshape(1024),
        "w_pw": f(w_conv_pw).reshape(1024, 2048), "sink": f(attn_sink).reshape(16), "w_o": f(w_o_attn).reshape(1024, 2048),
        "w_out": f(w_out).reshape(2048, 2048), "final_g": f(final_g).reshape(2048), "consts": make_consts(), "cbf": make_cbf(),
    }
    n = 8
    in_maps = []
    for c in range(n):
        m = dict(shared)
        m["x"] = x_prompt[c]
        m["xs"] = x_sample[2 * c:2 * c + 2].reshape(32, 2048)
        m["ck"] = cache_k[0, 2 * c:2 * c + 2].reshape(2, 128, 128)
        m["cv"] = cache_v[0, 2 * c:2 * c + 2].reshape(2, 128, 128)
        m["sc"] = state_conv[0, 2 * c:2 * c + 2]
        in_maps.append(m)
    if "nc" not in _NC_CACHE:
        _NC_CACHE["nc"] = build_nc(8, True)
    res = run_bass_kernel_spmd(_NC_CACHE["nc"], in_maps, core_ids=list(range(n)))
    R = res.results
    g = lambda k: np.stack([np.asarray(R[c][k], dtype=np.float32) for c in range(n)])
    y_prompt = g("y")
    y_sample = g("ys").reshape(16, 16, 2048)
    k_win_p = g("kwin").reshape(1, 8, 128, 2, 64)
    v_win_p = g("vwin").reshape(1, 8, 128, 2, 64)
    conv_p = g("cwin").reshape(1, 8, 30, 1024)
    k_win_s = g("ksw").reshape(1, 16, 128, 2, 64)
    v_win_s = g("vsw").reshape(1, 16, 128, 2, 64)
    conv_s = g("csw").reshape(1, 16, 30, 1024)
    return (y_prompt, y_sample, k_win_p, v_win_p, conv_p, k_win_s, v_win_s, conv_s)
```
